# Optimizing a Trainium2 kernel written in Bass

```python
import numpy as np
import jax, jax.numpy as jnp
from jax import lax

D_MODEL = 1024
BATCH = 8
SEQ = 2048
DEPTH = 4

HEAD_DIM = 64
ROPE_THETA = 10000.0
RMS_EPS = 1e-6
A_HEADS = 8
A_KV = 1
A_WINDOW = 128
A_QBLOCK = 128
B_HEADS = 8
B_KV = 2
CMP_LEN = 32
CMP_STRIDE = 16
CMP_HIDDEN = 256
SLC_LEN = 64
SLC_TOPN = 8
WIN_LEN = 512
B_QBLOCK = 128
FORCE_BONUS = 1e4
C_HEADS = 16
MOBA_BLOCK = 256
MOBA_TOPK = 3
C_QBLOCK = 32

A_QW = A_HEADS * HEAD_DIM
A_KVW = A_KV * HEAD_DIM
B_QW = B_HEADS * HEAD_DIM
B_KVW = B_KV * HEAD_DIM
B_GATEW = 3 * B_HEADS
C_W = C_HEADS * HEAD_DIM
EVEN_SIZES = (A_QW, A_KVW, A_KVW, A_QW, B_QW, B_KVW, B_KVW, B_KVW, B_KVW, B_KVW, B_KVW, B_GATEW, B_QW)
EVEN_WIDTH = A_QW + 2 * A_KVW + A_QW + B_QW + 6 * B_KVW + B_GATEW + B_QW
EVEN_OUT = A_QW + B_QW
ODD_SIZES = (C_W, C_W, C_W, C_W)
ODD_WIDTH = 4 * C_W
N_EVEN = (DEPTH + 1) // 2
N_ODD = DEPTH // 2

kernel_name = "hybrid_swa_nsa_moba_adaln_trunk"


def _offsets(sizes):
    out, acc = [], 0
    for s in sizes[:-1]:
        acc += s
        out.append(acc)
    return out


def rms_norm(x, g):
    xf = x.astype(jnp.float32)
    y = xf * lax.rsqrt(jnp.mean(xf * xf, axis=-1, keepdims=True) + RMS_EPS)
    return (y * g.astype(jnp.float32)).astype(x.dtype)


def rope_tables(n):
    inv = ROPE_THETA ** (-jnp.arange(0, HEAD_DIM, 2, dtype=jnp.float32) / HEAD_DIM)
    ang = jnp.arange(n, dtype=jnp.float32)[:, None] * inv[None, :]
    return jnp.cos(ang), jnp.sin(ang)


def apply_rope(x, cos, sin):
    half = HEAD_DIM // 2
    xf = x.astype(jnp.float32)
    x1, x2 = xf[..., :half], xf[..., half:]
    return jnp.concatenate([x1 * cos - x2 * sin, x2 * cos + x1 * sin], axis=-1).astype(x.dtype)


def split_heads(t, n):
    b, s, _ = t.shape
    return t.reshape(b, s, n, HEAD_DIM).transpose(0, 2, 1, 3)


def merge_heads(o):
    b, h, s, d = o.shape
    return o.transpose(0, 2, 1, 3).reshape(b, s, h * d)


def masked_softmax(s, mask, sink=None):
    s = jnp.where(mask, s, -jnp.inf)
    m = jnp.max(s, axis=-1, keepdims=True)
    if sink is not None:
        m = jnp.maximum(m, sink)
    m = jnp.where(jnp.isfinite(m), m, 0.0)
    e = jnp.where(mask, jnp.exp(s - m), 0.0)
    den = jnp.sum(e, axis=-1, keepdims=True)
    if sink is not None:
        den = den + jnp.exp(sink - m)
    return e / jnp.where(den > 0, den, 1.0)


def gather_blocks(kb, idx):
    return jax.vmap(jax.vmap(lambda a, i: a[i]))(kb, idx)


def banded_attention(q, k, v, window, block, sink=None):
    b, g, r, s, d = q.shape
    nb = s // block
    n_prev = -(-(window - 1) // block)
    span = (n_prev + 1) * block
    pad = n_prev * block
    kp = jnp.pad(k, ((0, 0), (0, 0), (pad, 0), (0, 0)))
    vp = jnp.pad(v, ((0, 0), (0, 0), (pad, 0), (0, 0)))
    idx = np.arange(nb)[:, None] * block + np.arange(span)[None, :]
    kb = kp[:, :, idx]
    vb = vp[:, :, idx]
    qb = q.reshape(b, g, r, nb, block, d)
    sc = jnp.einsum('bgrnqd,bgnkd->bgrnqk', qb, kb).astype(jnp.float32) * (HEAD_DIM ** -0.5)
    qpos = np.arange(nb)[:, None] * block + np.arange(block)[None, :]
    kpos = idx - pad
    diff = qpos[:, :, None] - kpos[:, None, :]
    mask = jnp.asarray((diff >= 0) & (diff < window) & (kpos[:, None, :] >= 0))
    sk = None if sink is None else sink.astype(jnp.float32).reshape(1, g, r, 1, 1, 1)
    p = masked_softmax(sc, mask, sk)
    o = jnp.einsum('bgrnqk,bgnkd->bgrnqd', p.astype(v.dtype), vb)
    return o.reshape(b, g, r, s, d)


def swa_sink_attention(qa, ka, va, sinks, cos, sin):
    b, s, _ = qa.shape
    q = apply_rope(split_heads(qa, A_HEADS), cos, sin).reshape(b, A_KV, A_HEADS // A_KV, s, HEAD_DIM)
    k = apply_rope(split_heads(ka, A_KV), cos, sin)
    v = split_heads(va, A_KV)
    o = banded_attention(q, k, v, A_WINDOW, A_QBLOCK, sinks.reshape(A_KV, A_HEADS // A_KV))
    return merge_heads(o.reshape(b, A_HEADS, s, HEAD_DIM))


def compress_mlp(blocks, pe, w1, w2):
    b, g, nc, l, d = blocks.shape
    u = (blocks + pe).reshape(b, g, nc, l * d)
    return jax.nn.gelu(u @ w1) @ w2


def nsa_attention(qb, kc, vc, ks, vs, kw, vw, gl, pe_k, w1k, w2k, pe_v, w1v, w2v, cos, sin):
    b, s, _ = qb.shape
    rr = B_HEADS // B_KV
    scale = HEAD_DIM ** -0.5
    q = apply_rope(split_heads(qb, B_HEADS), cos, sin).reshape(b, B_KV, rr, s, HEAD_DIM)
    kc = apply_rope(split_heads(kc, B_KV), cos, sin)
    ks = apply_rope(split_heads(ks, B_KV), cos, sin)
    kw = apply_rope(split_heads(kw, B_KV), cos, sin)
    vc, vs, vw = split_heads(vc, B_KV), split_heads(vs, B_KV), split_heads(vw, B_KV)

    nc = (s - CMP_LEN) // CMP_STRIDE + 1
    cidx = np.arange(nc)[:, None] * CMP_STRIDE + np.arange(CMP_LEN)[None, :]
    k_cmp = compress_mlp(kc[:, :, cidx], pe_k, w1k, w2k)
    v_cmp = compress_mlp(vc[:, :, cidx], pe_v, w1v, w2v)
    sc = jnp.einsum('bgrsd,bgcd->bgrsc', q, k_cmp).astype(jnp.float32) * scale
    mask_c = jnp.asarray(np.arange(s)[:, None] >= cidx[None, :, -1])
    p_cmp = masked_softmax(sc, mask_c)
    o_cmp = jnp.einsum('bgrsc,bgcd->bgrsd', p_cmp.astype(v_cmp.dtype), v_cmp)

    nsb = s // SLC_LEN
    cst = cidx[:, 0]
    jj = np.arange(nsb)
    overlap = ((cst[:, None] < (jj[None, :] + 1) * SLC_LEN) & (cst[:, None] + CMP_LEN > jj[None, :] * SLC_LEN)).astype(np.float32)
    imp = jnp.einsum('bgrsc,cj->bgsj', p_cmp, jnp.asarray(overlap))
    tb = np.arange(s)[:, None] // SLC_LEN
    valid = jnp.asarray(jj[None, :] <= tb)
    forced = jnp.asarray((jj[None, :] == 0) | (jj[None, :] == tb) | (jj[None, :] == tb - 1))
    score = jnp.where(valid, jnp.where(forced, FORCE_BONUS, imp), -jnp.inf)
    top_s, top_i = lax.top_k(score, min(SLC_TOPN, nsb))
    sel_ok = top_s > -jnp.inf
    n_sel = top_i.shape[-1]
    ksb = ks.reshape(b, B_KV, nsb, SLC_LEN, HEAD_DIM)
    vsb = vs.reshape(b, B_KV, nsb, SLC_LEN, HEAD_DIM)

    def sel_chunk(ci):
        start = ci * B_QBLOCK
        qc = lax.dynamic_slice_in_dim(q, start, B_QBLOCK, axis=3)
        ic = lax.dynamic_slice_in_dim(top_i, start, B_QBLOCK, axis=2)
        okc = lax.dynamic_slice_in_dim(sel_ok, start, B_QBLOCK, axis=2)
        kg = gather_blocks(ksb, ic)
        vg = gather_blocks(vsb, ic)
        sc_ = jnp.einsum('bgrqd,bgqnkd->bgrqnk', qc, kg).astype(jnp.float32) * scale
        tpos = start + jnp.arange(B_QBLOCK)
        kpos = ic[..., None] * SLC_LEN + jnp.arange(SLC_LEN)
        msk = okc[..., None] & (kpos <= tpos[None, None, :, None, None])
        m_tot = n_sel * SLC_LEN
        p = masked_softmax(sc_.reshape(b, B_KV, rr, B_QBLOCK, m_tot), msk.reshape(b, B_KV, 1, B_QBLOCK, m_tot))
        return jnp.einsum('bgrqm,bgqmd->bgrqd', p.astype(vg.dtype), vg.reshape(b, B_KV, B_QBLOCK, m_tot, HEAD_DIM))

    o_slc = lax.map(sel_chunk, jnp.arange(s // B_QBLOCK))
    o_slc = o_slc.transpose(1, 2, 3, 0, 4, 5).reshape(b, B_KV, rr, s, HEAD_DIM)

    o_win = banded_attention(q, kw, vw, WIN_LEN, B_QBLOCK)

    g = jax.nn.sigmoid(gl.astype(jnp.float32)).reshape(b, s, B_KV, rr, 3).transpose(0, 2, 3, 1, 4).astype(q.dtype)
    o = g[..., 0:1] * o_cmp + g[..., 1:2] * o_slc + g[..., 2:3] * o_win
    return merge_heads(o.reshape(b, B_HEADS, s, HEAD_DIM))


def moba_attention(qc_, kc_, vc_, cos, sin):
    b, s, _ = qc_.shape
    scale = HEAD_DIM ** -0.5
    q = apply_rope(split_heads(qc_, C_HEADS), cos, sin)
    k = apply_rope(split_heads(kc_, C_HEADS), cos, sin)
    v = split_heads(vc_, C_HEADS)
    sp = -(-s // MOBA_BLOCK) * MOBA_BLOCK
    padw = ((0, 0), (0, 0), (0, sp - s), (0, 0))
    q, k, v = jnp.pad(q, padw), jnp.pad(k, padw), jnp.pad(v, padw)
    nblk = sp // MOBA_BLOCK
    kb = k.reshape(b, C_HEADS, nblk, MOBA_BLOCK, HEAD_DIM)
    vb = v.reshape(b, C_HEADS, nblk, MOBA_BLOCK, HEAD_DIM)
    kmean = jnp.mean(kb.astype(jnp.float32), axis=3)
    gs = jnp.einsum('bhsd,bhjd->bhsj', q.astype(jnp.float32), kmean)
    past = jnp.asarray(np.arange(nblk)[None, :] < (np.arange(sp) // MOBA_BLOCK)[:, None])
    gs = jnp.where(past, gs, -jnp.inf)
    top_s, top_i = lax.top_k(gs, min(MOBA_TOPK, nblk))
    ok = top_s > -jnp.inf
    n_sel = top_i.shape[-1]

    def chunk(ci):
        start = ci * C_QBLOCK
        qc = lax.dynamic_slice_in_dim(q, start, C_QBLOCK, axis=2)
        ic = lax.dynamic_slice_in_dim(top_i, start, C_QBLOCK, axis=2)
        okc = lax.dynamic_slice_in_dim(ok, start, C_QBLOCK, axis=2)
        kg = gather_blocks(kb, ic)
        vg = gather_blocks(vb, ic)
        s_sel = jnp.einsum('bhqd,bhqnkd->bhqnk', qc, kg).astype(jnp.float32).reshape(b, C_HEADS, C_QBLOCK, n_sel * MOBA_BLOCK)
        own = (start // MOBA_BLOCK) * MOBA_BLOCK
        kown = lax.dynamic_slice_in_dim(k, own, MOBA_BLOCK, axis=2)
        vown = lax.dynamic_slice_in_dim(v, own, MOBA_BLOCK, axis=2)
        s_own = jnp.einsum('bhqd,bhkd->bhqk', qc, kown).astype(jnp.float32)
        tpos = start + jnp.arange(C_QBLOCK)
        m_own = (own + jnp.arange(MOBA_BLOCK))[None, :] <= tpos[:, None]
        m_sel = jnp.repeat(okc, MOBA_BLOCK, axis=-1)
        mask = jnp.concatenate([m_sel, jnp.broadcast_to(m_own, (b, C_HEADS, C_QBLOCK, MOBA_BLOCK))], axis=-1)
        p = masked_softmax(jnp.concatenate([s_sel, s_own], axis=-1) * scale, mask).astype(v.dtype)
        nk = n_sel * MOBA_BLOCK
        o = jnp.einsum('bhqm,bhqmd->bhqd', p[..., :nk], vg.reshape(b, C_HEADS, C_QBLOCK, nk, HEAD_DIM))
        return o + jnp.einsum('bhqk,bhkd->bhqd', p[..., nk:], vown)

    o = lax.map(chunk, jnp.arange(sp // C_QBLOCK))
    o = o.transpose(1, 2, 0, 3, 4).reshape(b, C_HEADS, sp, HEAD_DIM)[:, :, :s]
    return merge_heads(o)


def even_mixer(h, w_in, w_out, sinks, pe_k, w1k, w2k, pe_v, w1v, w2v, cos, sin):
    proj = jnp.einsum('bsd,de->bse', h, w_in)
    qa, ka, va, za, qb, kc, vc, ks, vs, kw, vw, gb, zb = jnp.split(proj, _offsets(EVEN_SIZES), axis=-1)
    oa = swa_sink_attention(qa, ka, va, sinks, cos, sin) * jax.nn.silu(za)
    ob = nsa_attention(qb, kc, vc, ks, vs, kw, vw, gb, pe_k, w1k, w2k, pe_v, w1v, w2v, cos, sin) * jax.nn.silu(zb)
    return jnp.einsum('bse,ed->bsd', jnp.concatenate([oa, ob], axis=-1), w_out)


def odd_mixer(h, w_in, w_out, cos, sin):
    proj = jnp.einsum('bsd,de->bse', h, w_in)
    q, k, v, z = jnp.split(proj, _offsets(ODD_SIZES), axis=-1)
    o = moba_attention(q, k, v, cos, sin) * jax.nn.silu(z)
    return jnp.einsum('bse,ed->bsd', o, w_out)


def setup_inputs(seed: int = 0) -> dict:
    key = jax.random.key(seed)
    ks = jax.random.split(key, 20)
    nrm = lambda k, shape, sc: jax.random.normal(k, shape, jnp.float32) * sc
    d = D_MODEL
    fl = CMP_LEN * HEAD_DIM
    return {
        "x": nrm(ks[0], (BATCH, SEQ, d), 1.0),
        "c": nrm(ks[1], (BATCH, d), 1.0),
        "w_ada": nrm(ks[2], (DEPTH, d, 3 * d), 0.5 * d ** -0.5),
        "b_ada": nrm(ks[3], (DEPTH, 3 * d), 0.01),
        "norm_g": 1.0 + nrm(ks[4], (DEPTH, d), 0.02),
        "w_in_even": nrm(ks[5], (N_EVEN, d, EVEN_WIDTH), d ** -0.5),
        "a_sinks": nrm(ks[6], (N_EVEN, A_HEADS), 0.5),
        "cmp_pe_k": nrm(ks[7], (N_EVEN, CMP_LEN, HEAD_DIM), 0.1),
        "cmp_w1_k": nrm(ks[8], (N_EVEN, fl, CMP_HIDDEN), fl ** -0.5),
        "cmp_w2_k": nrm(ks[9], (N_EVEN, CMP_HIDDEN, HEAD_DIM), CMP_HIDDEN ** -0.5),
        "cmp_pe_v": nrm(ks[10], (N_EVEN, CMP_LEN, HEAD_DIM), 0.1),
        "cmp_w1_v": nrm(ks[11], (N_EVEN, fl, CMP_HIDDEN), fl ** -0.5),
        "cmp_w2_v": nrm(ks[12], (N_EVEN, CMP_HIDDEN, HEAD_DIM), CMP_HIDDEN ** -0.5),
        "w_out_even": nrm(ks[13], (N_EVEN, EVEN_OUT, d), EVEN_OUT ** -0.5),
        "w_in_odd": nrm(ks[14], (N_ODD, d, ODD_WIDTH), d ** -0.5),
        "w_out_odd": nrm(ks[15], (N_ODD, C_W, d), C_W ** -0.5),
        "final_g": 1.0 + nrm(ks[16], (d,), 0.02),
    }


def reference(x, c, w_ada, b_ada, norm_g, w_in_even, a_sinks, cmp_pe_k, cmp_w1_k, cmp_w2_k,
              cmp_pe_v, cmp_w1_v, cmp_w2_v, w_out_even, w_in_odd, w_out_odd, final_g):
    cos, sin = rope_tables(x.shape[1])
    c_act = jax.nn.silu(c)
    for layer in range(DEPTH):
        ada = jnp.einsum('bd,de->be', c_act, w_ada[layer]) + b_ada[layer]
        shift, scale, gate = jnp.split(ada, 3, axis=-1)
        h = rms_norm(x, norm_g[layer]) * (1.0 + scale[:, None, :]) + shift[:, None, :]
        i = layer // 2
        if layer % 2 == 0:
            y = even_mixer(h, w_in_even[i], w_out_even[i], a_sinks[i], cmp_pe_k[i], cmp_w1_k[i], cmp_w2_k[i],
                           cmp_pe_v[i], cmp_w1_v[i], cmp_w2_v[i], cos, sin)
        else:
            y = odd_mixer(h, w_in_odd[i], w_out_odd[i], cos, sin)
        x = x + gate[:, None, :] * y
    return rms_norm(x, final_g)
```

```python
import os
import numpy as np
from contextlib import ExitStack
import concourse.bass as bass
import concourse.mybir as mybir
from concourse.bass_utils import run_bass_kernel_spmd

F32 = mybir.dt.float32
BF16 = mybir.dt.bfloat16
ALU = mybir.AluOpType
AF = mybir.ActivationFunctionType
AX = mybir.AxisListType

COMPUTE = ("pe", "act", "dve", "pool")
S = 2048
D = 1024
_PADS = os.environ.get('K_PAD', 'swa,cmp,slc,win,moba,gs').split(',')
def _R(site, small):
    return 128 if site in _PADS else small
NEG = -30000.0


class T:
    __slots__ = ("name", "w", "r", "rd")

    def __init__(self, name):
        self.name = name
        self.w = None
        self.r = {}
        self.rd = []


class Op:
    __slots__ = ("eng", "fn", "reads", "writes", "dma", "idx", "deps", "signal",
                 "count", "waits", "dma_count", "tag", "iname")

    def __init__(self, eng, fn, reads, writes, dma):
        self.eng = eng
        self.fn = fn
        self.reads = reads
        self.writes = writes
        self.dma = dma
        self.signal = False
        self.count = 0
        self.waits = []
        self.dma_count = 0


class Prog:
    def __init__(self, nc):
        self.nc = nc
        self.ops = []
        self.es = ExitStack()

    def sb(self, name, shape, dt):
        return self.es.enter_context(self.nc.sbuf_tensor(name, list(shape), dt))

    def ps(self, name, shape, dt=F32):
        return self.es.enter_context(self.nc.psum_tensor(name, list(shape), dt))

    def add(self, eng, fn, reads=(), writes=(), dma=None):
        op = Op(eng, fn, list(reads), list(writes), dma)
        op.tag = getattr(self, "cur_tag", "")
        op.iname = None
        op.idx = len(self.ops)
        self.ops.append(op)
        return op

    def finalize(self):
        nc = self.nc
        ops = self.ops
        for i, op in enumerate(ops):
            deps = set()
            for t in op.reads:
                if t.w is not None:
                    deps.add(t.w)
            for t in op.writes:
                if t.w is not None:
                    deps.add(t.w)
                deps.update(t.r.values())
                deps.update(t.rd)
            for t in op.reads:
                if op.dma is not None:
                    t.rd.append(i)
                else:
                    t.r[op.eng] = i
            for t in op.writes:
                t.w = i
                t.r = {}
                t.rd = []
            deps.discard(i)
            op.deps = sorted(deps)
        pos = {}
        cnt = {e: 0 for e in COMPUTE}
        for op in ops:
            if op.dma is None and op.eng in COMPUTE:
                pos[op.idx] = cnt[op.eng]
                cnt[op.eng] += 1
        need = []
        for op in ops:
            lst = []
            for d in op.deps:
                dop = ops[d]
                if dop.dma is not None:
                    lst.append(("dma", d))
                    continue
                if dop.eng == op.eng and op.dma is None:
                    if op.eng == "pe":
                        continue
                dop.signal = True
                lst.append(("eng", d))
            need.append(lst)
        ecount = {e: 0 for e in COMPUTE}
        dcount = {}
        for op in ops:
            if op.dma is not None:
                dcount[op.dma] = dcount.get(op.dma, 0) + 16
                op.dma_count = dcount[op.dma]
            elif op.signal:
                ecount[op.eng] += 1
                op.count = ecount[op.eng]
        self.ecount = ecount
        SEM_LIM = int(os.environ.get("K_SEMLIM", "100000"))
        esem = {}
        for e in COMPUTE:
            nep = ecount[e] // SEM_LIM + 1
            esem[e] = [self.es.enter_context(nc.semaphore("s_%s%d" % (e, k))) for k in range(nep)]
        dsem = {}
        for k in dcount:
            dsem[k] = self.es.enter_context(nc.semaphore("d_%s" % (k,)))
        waited = {}
        dcur = {}
        streams = {}
        for op in ops:
            w = waited.setdefault(op.eng, {})
            waits = []
            for kind, d in need[op.idx]:
                dop = ops[d]
                if kind == "dma":
                    key = ("d", dop.dma)
                    val = dcur[dop.dma]
                    sem = dsem[dop.dma]
                else:
                    key = ("e", dop.eng)
                    val = dop.count
                    ep = (val - 1) // SEM_LIM
                    sem = esem[dop.eng][ep]
                if w.get(key, 0) >= val:
                    continue
                w[key] = val
                if kind != "dma":
                    val = val - ep * SEM_LIM
                waits.append((sem, val))
            op.waits = waits
            if op.dma is not None:
                dcur[op.dma] = op.dma_count
            streams.setdefault(op.eng, []).append(op)
        self.streams = streams

        def emit(engine, lst):
            for op in lst:
                for sem, val in op.waits:
                    engine.wait_ge(sem, val)
                if op.fn is None:
                    continue
                ins = op.fn(engine)
                try:
                    op.iname = ins.ins.name
                except Exception:
                    pass
                if op.dma is not None:
                    ins.then_inc(dsem[op.dma], 16)
                elif op.signal:
                    ins.then_inc(esem[op.eng][(op.count - 1) // SEM_LIM], 1)

        with nc.Block() as block:
            if "sp" in streams:
                @block.sync
                def _(e):
                    emit(e, streams["sp"])
            if "pe" in streams:
                @block.tensor
                def _(e):
                    emit(e, streams["pe"])
            if "act" in streams:
                @block.scalar
                def _(e):
                    emit(e, streams["act"])
            if "dve" in streams:
                @block.vector
                def _(e):
                    emit(e, streams["dve"])
            if "pool" in streams:
                @block.gpsimd
                def _(e):
                    emit(e, streams["pool"])
        self.es.close()


class Rot:
    def __init__(self, P, name, n, shape, dt):
        self.bufs = [(P.sb("%s%d" % (name, i), shape, dt), T("%s%d" % (name, i))) for i in range(n)]
        self.i = 0

    def get(self):
        b = self.bufs[self.i % len(self.bufs)]
        self.i += 1
        return b


CV0, BADA0, NG0, FG0, SINK0, IDF0, PBT0, NPT0, FT0, VT0 = (
    0, 8, 104, 136, 144, 160, 288, 416, 544, 1056)
MISC_W = 1568
OV0, W20, PE0 = 0, 64, 576
NB = 96 + 2 * 56 + 2 * 77

A_QW, A_KVW = 512, 64
EVEN_OFF = {}
_o = 0
for _n, _s in (("qa", 512), ("ka", 64), ("va", 64), ("za", 512), ("qb", 512), ("kc", 128), ("vc", 128),
               ("ks", 128), ("vs", 128), ("kw", 128), ("vw", 128), ("gb", 24), ("zb", 512)):
    EVEN_OFF[_n] = _o
    _o += _s


def _sw(cols):
    cols = np.asarray(cols).reshape(-1, 2, 32)
    return cols[:, ::-1, :].reshape(-1)


class Builder:
    def __init__(self, depth=4):
        self.depth = depth
        self.nc = nc = bass.Bass("TRN2", target_bir_lowering=False)
        self.P = P = Prog(nc)
        self.descs = []
        self.dbg_names = []
        self.x_d = nc.dram_tensor("x", [S, D], F32, kind="ExternalInput").ap()
        self.misc_d = nc.dram_tensor("misc", [128, MISC_W], F32, kind="ExternalInput").ap()
        self.misc2_d = nc.dram_tensor("misc2", [128, 1024], F32, kind="ExternalInput").ap()
        self.cs_d = nc.dram_tensor("cs", [128, 2 * S], F32, kind="ExternalInput").ap()
        self.e32_d = nc.dram_tensor("e32", [128, S], F32, kind="ExternalInput").ap()
        self.wblk_d = nc.dram_tensor("wblk", [NB, 128, 1024], F32, kind="ExternalInput").ap()
        self.out_d = nc.dram_tensor("out", [S, D], F32, kind="ExternalOutput").ap()
        self.XT = P.sb("XT", [128, 8, S], F32)
        self.XT_T = [[T("xt%d_%d" % (c, tb)) for tb in range(4)] for c in range(8)]
        self.HT = P.sb("HT", [128, 8, S], BF16)
        self.HT_T = [T("ht%d" % tb) for tb in range(4)]
        self.QTA = P.sb("QTA", [128, 4, S], BF16)
        self.Qh = [T("qh%d" % r) for r in range(4)]
        self.Qa = [T("qa%d" % r) for r in range(4)]
        self.KA = P.sb("KA", [128, S], BF16)
        self.KB = P.sb("KB", [128, S], BF16)
        self.KAh, self.KBh, self.Kaug, self.KBaug = T("kah"), T("kbh"), T("kaug"), T("kbaug")
        self.VA = P.sb("VA", [128, 16, 2, 65], BF16)
        self.VA_T = [T("va0"), T("va1")]
        self.VAa = P.sb("VAa", [128, 16, 65], BF16)
        self.VAa_T = T("vaa")
        self.ZS = P.sb("ZS", [128, 16, 256], BF16)
        self.ZS_T = T("zs")
        self.GSG = P.sb("GSG", [128, 16, 24], F32)
        self.GSG_T = T("gsg")
        self.OGT = P.sb("OGT", [128, 16, 256], BF16)
        self.OGT_T = [T("ogt%d" % t) for t in range(16)]
        self.OGTT = P.sb("OGTT", [128, S], BF16)
        self.OGTT_T = T("ogtt")
        self.CS = P.sb("CS", [128, 2 * S], F32)
        self.CS_T = T("cs")
        self.MISC = P.sb("MISC", [128, MISC_W], F32)
        self.MISC_T = T("misc")
        self.MB = P.sb("MB", [128, 1024], BF16)
        self.MB_T = T("mb")
        self.ADA = P.sb("ADA", [128, 96], F32)
        self.GSC = P.sb("GSC", [128, 32], F32)
        self.ADA_T = T("ada")
        self.CACT = P.sb("CACT", [128, 8, 2], BF16)
        self.CACT_T = T("cact")
        self.ESINK = P.sb("ESINK", [128, 16], F32)
        self.ESINK_T = T("esink")
        self.KCMP = P.sb("KCMP", [128, 128], BF16)
        self.KCMP_T = T("kcmp")
        self.VCX = P.sb("VCX", [128, 97], BF16)
        self.VCX_T = T("vcx")
        self.HID = P.sb("HID", [128, 2, 2, 128], BF16)
        self.HID_T = [T("hidk"), T("hidv")]
        self.KM = P.sb("KM", [128, 2, 8], BF16)
        self.KM_T = T("km")
        self.SELW = P.sb("SELW", [128, 1024], F32)
        self.SELW_T = T("selw")
        self.SELB = P.sb("SELB", [128, 16, 2, 32], BF16)
        self.SELB_T = T("selb")
        self.stage = Rot(P, "st", 2, [128, 1024], F32)
        self.wbp = Rot(P, "wb", 3, [128, 1024], BF16)
        self.ptp = Rot(P, "pt", 3, [128, 512], BF16)
        self.tmp = Rot(P, "tmp", 3, [128, 512], F32)
        self.rsp = Rot(P, "rs", 2, [128, 512], F32)
        self.sml = Rot(P, "sml", 4, [128, 16], F32)
        self.pS = [(P.ps("pS%d" % i, [128, 512]), T("pS%d" % i)) for i in range(2)]
        self.pO = P.ps("pO", [128, 4, 512])
        self.pO_T2 = [[T("pO%d_%d" % (st_, i)) for i in range(4)] for st_ in range(2)]
        self.ocount = 0
        self.pM = [(P.ps("pM%d" % i, [128, 512]), T("pM%d" % i)) for i in range(2)]
        self.si = 0
        self.mi = 0

    def nextS(self):
        b = self.pS[self.si % 2]
        self.si += 1
        return b

    def nextM(self):
        banks = self.pM + self.pS
        b = banks[self.mi % 4]
        self.mi += 1
        return b

    def mm(self, out, lhsT, rhs, start, stop, reads, writes):
        self.P.add("pe", lambda e: e.matmul(out, lhsT=lhsT, rhs=rhs, start=start, stop=stop), reads, writes)

    def trf(self, out, in_, ident, reads, writes):
        self.P.add("pe", lambda e: e.transpose(out, in_, ident), reads, writes)

    def act(self, out, in_, func, reads, writes, **kw):
        self.P.add("act", lambda e: e.activation(out=out, in_=in_, func=func, **kw), reads, writes)

    def tt(self, eng, out, in0, in1, op, reads, writes):
        self.P.add(eng, lambda e: e.tensor_tensor(out=out, in0=in0, in1=in1, op=op), reads, writes)

    def ts(self, eng, out, in0, s1, s2, op0, op1, reads, writes):
        if s2 is None:
            self.P.add(eng, lambda e: e.tensor_scalar(out=out, in0=in0, scalar1=s1, scalar2=None, op0=op0), reads, writes)
        else:
            self.P.add(eng, lambda e: e.tensor_scalar(out=out, in0=in0, scalar1=s1, scalar2=s2, op0=op0, op1=op1), reads, writes)

    def stt(self, out, in0, scalar, in1, op0, op1, reads, writes):
        self.P.add("dve", lambda e: e.scalar_tensor_tensor(out=out, in0=in0, scalar=scalar, in1=in1, op0=op0, op1=op1), reads, writes)

    def cp(self, eng, out, in_, reads, writes):
        if eng == "act":
            self.P.add("act", lambda e: e.activation(out=out, in_=in_, func=AF.Copy), reads, writes)
        else:
            self.P.add(eng, lambda e: e.tensor_copy(out=out, in_=in_), reads, writes)

    def recip(self, out, in_, reads, writes):
        self.P.add("dve", lambda e: e.reciprocal(out=out, in_=in_), reads, writes)

    def memset(self, eng, ap, val, writes):
        self.P.add(eng, lambda e: e.memset(ap, val), (), writes)

    def dma(self, out, in_, reads, writes, key):
        self.P.add("sp", lambda e: e.dma_start(out=out, in_=in_), reads, writes, dma=key)

    def sel(self, ap, pattern, base, cm, rw):
        self.P.add("pool", lambda e: e.affine_select(out=ap, in_=ap, pattern=pattern, compare_op=ALU.is_ge,
                                                     fill=0.0, base=base, channel_multiplier=cm), rw, rw)

    def dbg(self, name, ap, reads, dt=F32):
        if not getattr(self, "debug", False):
            return
        shp = list(ap.shape)
        d = self.nc.dram_tensor("dbg_" + name, shp, dt, kind="ExternalOutput").ap()
        self.dma(d, ap, reads, [], key="dbg_" + name)
        self.dbg_names.append("dbg_" + name)

    def wload(self, desc, cast=True, nparts=128, cast_eng=None):
        idx = len(self.descs)
        self.descs.append(desc)
        st, stT = self.stage.get()
        self.dma(st[0:nparts, :], self.wblk_d[idx, 0:nparts, :], [], [stT], key=stT.name)
        if not cast:
            return st, stT
        wb, wbT = self.wbp.get()
        self.cp(cast_eng or os.environ.get("K_CAST", "act"), wb[0:nparts, :], st[0:nparts, :], [stT], [wbT])
        return wb, wbT

    def phase0(self):
        P = self.P
        P.cur_tag = 'phase0'
        self.dma(self.MISC[:, :], self.misc_d, [], [self.MISC_T], key="misc")
        self.dma(self.CS[:, :], self.cs_d, [], [self.CS_T], key="cs")
        M = self.MISC
        self.memset("pool", self.KA[64:128, :], 0.0, [self.Kaug])
        self.memset("pool", self.KB[64:128, :], 0.0, [self.KBaug])
        self.memset("pool", self.QTA[64:128, :, :], 0.0, self.Qa)
        self.memset("pool", self.KM[:, :, :], 0.0, [self.KM_T])
        for half in range(2):
            st, stT = self.stage.get()
            self.dma(st[64:96, :], self.e32_d[64:96, half * 1024:(half + 1) * 1024], [], [stT], key=stT.name)
            self.cp("pool", self.KA[64:96, half * 1024:(half + 1) * 1024], st[64:96, :], [stT], [self.Kaug])
        MB = self.MB
        self.cp("dve", MB[:, 0:128], M[:, IDF0:IDF0 + 128], [self.MISC_T], [self.MB_T])
        self.memset("dve", MB[:, 128:256], 1.0, [self.MB_T])
        st, stT = self.stage.get()
        self.dma(st[:, :], self.misc2_d, [], [stT], key=stT.name)
        self.cp("dve", MB[:, 256:289], st[:, OV0:OV0 + 33], [stT], [self.MB_T])
        self.cp("dve", MB[:, 320:832], st[:, W20:W20 + 512], [stT], [self.MB_T])
        self.cp("dve", MB[:, 832:960], st[:, PE0:PE0 + 128], [stT], [self.MB_T])
        self.IDB = MB[:, 0:128]
        self.ONESB = MB[:, 128:256]
        self.memset("pool", self.VA[:, :, :, 64:65], 1.0, self.VA_T)
        self.memset("pool", self.VAa[:, :, 64:65], 1.0, [self.VAa_T])
        self.memset("pool", self.KCMP[:, :], 0.0, [self.KCMP_T])
        self.memset("pool", self.VCX[:, :], 0.0, [self.VCX_T])
        self.cp("pool", self.VCX[:, 64:97], MB[:, 256:289], [self.MB_T], [self.VCX_T])
        self.act(self.ESINK[:, :], M[:, SINK0:SINK0 + 16], AF.Exp, [self.MISC_T], [self.ESINK_T])
        for j in range(2):
            self.act(self.CACT[:, :, j], M[:, CV0:CV0 + 8], AF.Silu, [self.MISC_T], [self.CACT_T])
        pm, pmT = self.nextM()
        for l in range(self.depth):
            for nb in range(24):
                st, stT = self.wload(("ada", l, nb), cast=True, cast_eng=("act" if nb % 2 == 0 else "dve"))
                col = 2 * (l * 24 + nb)
                for c in range(8):
                    self.mm(pm[:, col:col + 2], st[:, c * 128:(c + 1) * 128], self.CACT[:, c, :],
                            c == 0, c == 7, [stT, self.CACT_T], [pmT])
        n = 24 * self.depth
        if n > 0:
            self.tt("dve", self.ADA[:, 0:n], pm[:, 0:2 * n].rearrange("p (n two) -> p n two", two=2)[:, :, 0],
                    M[:, BADA0:BADA0 + n], ALU.add, [pmT, self.MISC_T], [self.ADA_T])
        for l in range(self.depth):
            self.stt(self.GSC[:, l * 8:(l + 1) * 8], self.ADA[:, l * 24 + 8:l * 24 + 16], 1.0,
                     M[:, NG0 + l * 8:NG0 + (l + 1) * 8], ALU.add, ALU.mult, [self.ADA_T, self.MISC_T], [self.ADA_T])
        self.dbg("ada", self.ADA[:, :], [self.ADA_T])
        self.dbg("gsc", self.GSC[:, :], [self.ADA_T])
        for _k in range(int(os.environ.get("K_DUMMY", "0"))):
            self.memset("dve", self.SELW[:, 0:8], 0.0, [self.SELW_T])
        IDF = M[:, IDF0:IDF0 + 128]
        for tt_ in range(16):
            st, stT = self.stage.get()
            self.dma(st[:, :], self.x_d[tt_ * 128:(tt_ + 1) * 128, :], [], [stT], key=stT.name)
            for half in range(2):
                pm, pmT = self.nextM()
                for cc in range(4):
                    c = half * 4 + cc
                    self.trf(pm[:, cc * 128:(cc + 1) * 128], st[:, c * 128:(c + 1) * 128], IDF,
                             [stT, self.MISC_T], [pmT])
                tb = tt_ // 4
                eng = "act" if half == 0 else "dve"
                self.cp(eng, self.XT[:, half * 4:half * 4 + 4, tt_ * 128:(tt_ + 1) * 128],
                        pm[:, :].rearrange("p (c t) -> p c t", c=4), [pmT],
                        [self.XT_T[half * 4 + cc][tb] for cc in range(4)])

    def rstd_block(self, tb):
        pm, pmT = self.nextM()
        for c in range(8):
            sq, sqT = self.ptp.get()
            xsl = self.XT[:, c, tb * 512:(tb + 1) * 512]
            if c % 2 == 0 or os.environ.get("K_H", "new") == "old":
                self.act(sq[:, :], xsl, AF.Square, [self.XT_T[c][tb]], [sqT])
            else:
                self.tt("dve", sq[:, :], xsl, xsl, ALU.mult, [self.XT_T[c][tb]], [sqT])
            self.mm(pm[:, :], self.ONESB, sq[:, :], c == 0, c == 7, [sqT, self.MB_T], [pmT])
        r, rT = self.rsp.get()
        self.ts("dve", r[:, :], pm[:, :], 1.0 / D, 1e-6, ALU.mult, ALU.add, [pmT], [rT])
        self.act(r[:, :], r[:, :], AF.Sqrt, [rT], [rT])
        self.recip(r[:, :], r[:, :], [rT], [rT])
        return r, rT

    def make_h(self, l):
        self.P.cur_tag = 'L%d.h' % l
        rs = {0: self.rstd_block(0)}
        for tb in range(4):
            if tb + 1 < 4:
                rs[tb + 1] = self.rstd_block(tb + 1)
            r, rT = rs.pop(tb)
            for c in range(8):
                t, tT = self.tmp.get()
                self.tt("pool" if (c % 3 == 0 or os.environ.get("K_H", "new") == "old") else "dve", t[:, :], self.XT[:, c, tb * 512:(tb + 1) * 512], r[:, :], ALU.mult,
                        [self.XT_T[c][tb], rT], [tT])
                self.act(self.HT[:, c, tb * 512:(tb + 1) * 512], t[:, :], AF.Identity, [tT, self.ADA_T], [self.HT_T[tb]],
                         scale=self.GSC[:, l * 8 + c:l * 8 + c + 1], bias=self.ADA[:, l * 24 + c:l * 24 + c + 1])

    def final(self):
        M = self.MISC
        self.P.cur_tag = 'final'
        IDF = M[:, IDF0:IDF0 + 128]
        for tb in range(4):
            r, rT = self.rstd_block(tb)
            for c in range(8):
                xs = self.XT[:, c, tb * 512:(tb + 1) * 512]
                self.stt(xs, xs, M[:, FG0 + c:FG0 + c + 1], r[:, :], ALU.mult, ALU.mult,
                         [self.XT_T[c][tb], rT, self.MISC_T], [self.XT_T[c][tb]])
            for a in range(4):
                tt_ = tb * 4 + a
                st, stT = self.stage.get()
                for half in range(2):
                    pm, pmT = self.nextM()
                    for cc in range(4):
                        c = half * 4 + cc
                        self.trf(pm[:, cc * 128:(cc + 1) * 128], self.XT[:, c, tt_ * 128:(tt_ + 1) * 128], IDF,
                                 [self.XT_T[c][tb], self.MISC_T], [pmT])
                    self.cp("act" if half == 0 else "dve", st[:, half * 512:(half + 1) * 512], pm[:, :], [pmT], [stT])
                self.dma(self.out_d[tt_ * 128:(tt_ + 1) * 128, :], st[:, :], [stT], [], key=stT.name)
        self.P.add("sp", None, (), [b[1] for b in self.stage.bufs])

    def proj_fm(self, wb, wbT, wsw, wswT, dst, M0=128):
        CC, SS = self.CS[:, 0:S], self.CS[:, S:2 * S]
        for tb in range(4):
            tsl = slice(tb * 512, (tb + 1) * 512)
            p1, p1T = self.nextM()
            for c in range(8):
                self.mm(p1[0:M0, :], wb[:, c * 128:c * 128 + M0], self.HT[:, c, tsl], c == 0, c == 7,
                        [wbT, self.HT_T[tb]], [p1T])
            anyrope = any(d[3] for d in dst)
            if anyrope:
                msw = 128 if (len(dst) > 1 and dst[1][3]) else 64
                p2, p2T = self.nextM()
                for c in range(8):
                    self.mm(p2[0:msw, :], wsw[:, c * 128:c * 128 + msw], self.HT[:, c, tsl], c == 0, c == 7,
                            [wswT, self.HT_T[tb]], [p2T])
                t1, t1T = self.tmp.get()
                t2, t2T = self.tmp.get()
                self.tt("dve", t1[0:msw, :], p1[0:msw, :], CC[0:msw, tsl], ALU.mult, [p1T, self.CS_T], [t1T])
                self.tt("dve", t2[0:msw, :], p2[0:msw, :], SS[0:msw, tsl], ALU.mult, [p2T, self.CS_T], [t2T])
            for (r0, apfn, dT, rope) in dst:
                if rope:
                    self.tt(os.environ.get("K_ROPE", "dve"), apfn(tb), t1[r0:r0 + 64, :], t2[r0:r0 + 64, :], ALU.add, [t1T, t2T], [dT])
                else:
                    self.cp("act", apfn(tb), p1[r0:r0 + 64, :], [p1T], [dT])

    def proj_tm(self, wb, wbT, evac):
        for tg in range(4):
            pm, pmT = self.nextM()
            for a in range(4):
                tt_ = tg * 4 + a
                for c in range(8):
                    self.mm(pm[:, a * 128:(a + 1) * 128], self.HT[:, c, tt_ * 128:(tt_ + 1) * 128],
                            wb[:, c * 128:(c + 1) * 128], c == 0, c == 7, [wbT, self.HT_T[tt_ // 4]], [pmT])
            evac(tg, pm[:, :].rearrange("p (a n) -> p a n", a=4), pmT)

    def run_calls(self, calls):
        steps = [(ci, kt) for ci, c in enumerate(calls) for kt in c["kts"]]
        state = {}

        def emit_scores(sidx):
            ci, kt = steps[sidx]
            c = calls[ci]
            kp = c.get("kparts", 128)
            ps, psT = self.nextS()
            lhsT, lreads = c["lhs_fn"](kt)
            c0 = c["c0"](kt) if "c0" in c else 0
            pso = ps[0:kp, c0:512]
            rhs = c["rhs"] if c0 == 0 else c["rhs"][:, c0:512]
            if len(rhs.shape) == 3:
                pso = pso.rearrange("p (a n) -> p a n", a=4)
            self.mm(pso, lhsT, rhs, True, True, lreads + c["rhs_reads"], [psT])
            state[sidx] = (ps, psT)
        if steps:
            emit_scores(0)
        for sidx in range(len(steps)):
            if sidx + 1 < len(steps):
                emit_scores(sidx + 1)
            ci, kt = steps[sidx]
            c = calls[ci]
            kp = c.get("kparts", 128)
            oset = ((self.ocount + ci) % 2) if os.environ.get("K_OSET", "0") == "1" else 0
            off = oset * 256
            poT = self.pO_T2[oset]
            ps, psT = state.pop(sidx)
            pt, ptT = self.ptp.get()
            c0 = c["c0"](kt) if "c0" in c else 0
            self.act(pt[0:kp, c0:512], ps[0:kp, c0:512], AF.Exp, [psT], [ptT], scale=0.125)
            m = c["mask_fn"](kt)
            if m is not None:
                for (pattern, base, cm) in m:
                    n_el = 1
                    for st_, nn in pattern:
                        n_el *= nn
                    self.sel(pt[0:kp, c0:c0 + n_el], pattern, base, cm, [ptT])
            vrhs, vreads = c["pv_fn"](kt)
            ncols = c["ncols"]
            for a in range(4):
                if not c["valid"](a, kt):
                    continue
                self.mm(self.pO[:, a, off:off + ncols], pt[0:kp, a * 128:(a + 1) * 128], vrhs,
                        kt == c["kts"][0], kt == c["lastk"](a), [ptT] + vreads, [poT[a]])
            if kt == c["kts"][-1]:
                cpb, cpT = self.tmp.get()
                cpv = cpb[:, :].rearrange("p (a n) -> p a n", a=4)
                self.cp("dve", cpv[:, :, 0:ncols], self.pO[:, :, off:off + ncols], poT, [cpT])
                c["after"](cpv, [cpT] * 4)
        self.ocount += len(calls)

    def finish_pairs(self, l, pair_specs):
        assert len(pair_specs) == 2
        ogtts = [(self.OGTT[:, :], self.OGTT_T),
                 (self.ZS[:, 0:8, :].rearrange("p a b -> p (a b)"), self.ZS_T)]
        wos = []
        for k, (coff, wdesc) in enumerate(pair_specs):
            wo, woT = self.wload(wdesc)
            wos.append((wo, woT))
            og, ogT = ogtts[k]
            for tg in range(2):
                for hh in range(2):
                    pm, pmT = self.nextM()
                    for a in range(4):
                        tt_ = tg * 8 + hh * 4 + a
                        self.mm(pm[:, a * 128:(a + 1) * 128], self.OGT[:, tt_, coff:coff + 128], self.IDB, True, True,
                                [self.OGT_T[tt_], self.MB_T], [pmT])
                    c0 = (tg * 8 + hh * 4) * 128
                    self.cp("act", og[:, c0:c0 + 512], pm[:, :], [pmT], [ogT])
        for nb in range(8):
            for tb in range(4):
                pm, pmT = self.nextM()
                for k in range(2):
                    wo, woT = wos[k]
                    og, ogT = ogtts[k]
                    self.mm(pm[:, :], wo[:, nb * 128:(nb + 1) * 128], og[:, tb * 512:(tb + 1) * 512], k == 0, k == 1,
                            [woT, ogT], [pmT])
                xs = self.XT[:, nb, tb * 512:(tb + 1) * 512]
                self.stt(xs, pm[:, :], self.ADA[:, l * 24 + 16 + nb:l * 24 + 17 + nb], xs, ALU.mult, ALU.add,
                         [pmT, self.ADA_T, self.XT_T[nb][tb]], [self.XT_T[nb][tb]])

    def qproj(self, name, li, col0, slots):
        cols = col0 + np.arange(128)
        wb, wbT = self.wload(("cols", name, li, cols))
        ws, wsT = self.wload(("cols", name, li, _sw(cols)))
        dst = []
        for k, r in enumerate(slots):
            dst.append((64 * k, (lambda tb, r=r: self.QTA[0:64, r, tb * 512:(tb + 1) * 512]), self.Qh[r], True))
        self.proj_fm(wb, wbT, ws, wsT, dst)

    def odd_layer(self, l):
        li = l // 2
        M = self.MISC
        self.cp("pool", self.KB[64:96, :], self.KA[64:96, :], [self.Kaug], [self.KBaug])
        self.make_h(l)
        for hp in range(8):
            base = hp * 128
            zoff = (hp % 2) * 128
            self.P.cur_tag = 'L%d.odd.proj' % l
            self.qproj("w_in_odd", li, base, (0, 1))
            cols = 1024 + base + np.arange(128)
            wb, wbT = self.wload(("cols", "w_in_odd", li, cols))
            ws, wsT = self.wload(("cols", "w_in_odd", li, _sw(cols)))
            self.proj_fm(wb, wbT, ws, wsT, [
                (0, (lambda tb: self.KA[0:64, tb * 512:(tb + 1) * 512]), self.KAh, True),
                (64, (lambda tb: self.KB[0:64, tb * 512:(tb + 1) * 512]), self.KBh, True)])
            Ks = [(self.KA, self.KAh), (self.KB, self.KBh)]
            self.P.cur_tag = 'L%d.odd.sel' % l
            kmf, kmfT = self.sml.get()
            for h in range(2):
                Kt, KT_ = Ks[h]
                self.P.add("dve", (lambda e, Kt=Kt, h=h, kmf=kmf: e.tensor_reduce(
                    out=kmf[0:64, h * 8:(h + 1) * 8], in_=Kt[0:64, :].rearrange("p (j k) -> p j k", j=8),
                    axis=AX.X, op=ALU.add)), [KT_], [kmfT])
            self.cp("dve", self.KM[0:64, :, :], kmf[0:64, :].rearrange("p (h j) -> p h j", h=2), [kmfT], [self.KM_T])
            wv, wvT = self.wload(("cols", "w_in_odd", li, 2048 + base + np.arange(128)))

            def evv(tg, ps3, pT):
                for h in range(2):
                    self.cp("act", self.VA[:, tg * 4:(tg + 1) * 4, h, 0:64], ps3[:, :, h * 64:(h + 1) * 64], [pT],
                            [self.VA_T[h]])
            self.proj_tm(wv, wvT, evv)
            pm, pmT = self.nextM()
            for tt_ in range(16):
                for h in range(2):
                    col = (tt_ * 2 + h) * 8
                    self.mm(pm[:, col:col + 8], self.QTA[0:_R('gs', 64), h, tt_ * 128:(tt_ + 1) * 128], self.KM[0:_R('gs', 64), h, :],
                            True, True, [self.Qh[h], self.Qa[h], self.KM_T], [pmT])
            W = self.SELW
            gsm = W[:, 0:256].rearrange("p (t h j) -> p t h j", t=16, h=2)
            pb = M[:, PBT0:PBT0 + 128].rearrange("p (t j) -> p t j", t=16).unsqueeze(2).to_broadcast([128, 16, 2, 8])
            self.tt("dve", gsm, pm[:, 0:256].rearrange("p (t h j) -> p t h j", t=16, h=2), pb, ALU.add,
                    [pmT, self.MISC_T], [self.SELW_T])
            g3v = W[:, 0:256].rearrange("p (k j) -> p k j", j=8)
            cur = g3v
            mxs = W[:, 256:352]
            wk = W[:, 512:768].rearrange("p (k j) -> p k j", j=8)
            wk2 = W[:, 768:1024].rearrange("p (k j) -> p k j", j=8)
            for rnd in range(3):
                mcol = mxs[:, rnd * 32:(rnd + 1) * 32]
                self.P.add("dve", (lambda e, cur=cur, mcol=mcol: e.tensor_reduce(out=mcol, in_=cur, axis=AX.X, op=ALU.max)),
                           [self.SELW_T], [self.SELW_T])
                if rnd == 2:
                    break
                eq = wk if rnd == 0 else wk2
                self.tt("dve", eq, cur, mcol.unsqueeze(2).to_broadcast([128, 32, 8]), ALU.is_ge, [self.SELW_T], [self.SELW_T])
                self.stt(eq, eq, -1e30, cur, ALU.mult, ALU.add, [self.SELW_T], [self.SELW_T])
                cur = eq
            lt = W[:, 512:768]
            thr = mxs[:, 64:96].unsqueeze(2).to_broadcast([128, 32, 8])
            self.tt("dve", lt.rearrange("p (k j) -> p k j", j=8), W[:, 0:256].rearrange("p (k j) -> p k j", j=8), thr,
                    ALU.is_lt, [self.SELW_T], [self.SELW_T])
            npb = M[:, NPT0:NPT0 + 128].rearrange("p (t j) -> p t j", t=16).unsqueeze(2).to_broadcast([128, 16, 2, 8])
            sb8 = W[:, 768:1024]
            self.tt("dve", sb8.rearrange("p (t h j) -> p t h j", t=16, h=2),
                    lt.rearrange("p (t h j) -> p t h j", t=16, h=2), npb, ALU.mult, [self.SELW_T, self.MISC_T],
                    [self.SELW_T])
            self.cp("dve", self.SELB[:, :, :, :].rearrange("p t h (j f) -> p (t h) j f", f=4),
                    sb8.rearrange("p (k j) -> p k j", j=8).unsqueeze(3).to_broadcast([128, 32, 8, 4]),
                    [self.SELW_T], [self.SELB_T])
            wz, wzT = self.wload(("cols", "w_in_odd", li, 3072 + base + np.arange(128)))

            def evz(tg, ps3, pT, zoff=zoff):
                self.act(self.ZS[:, tg * 4:(tg + 1) * 4, zoff:zoff + 128], ps3, AF.Silu, [pT], [self.ZS_T])
            self.proj_tm(wz, wzT, evz)
            for h in range(2):
                for g4 in range(4):
                    pm, pmT = self.nextM()
                    for a in range(4):
                        tt_ = g4 * 4 + a
                        self.mm(pm[0:32, a * 128:(a + 1) * 128], self.SELB[:, tt_, h, :], self.IDB, True, True,
                                [self.SELB_T, self.MB_T], [pmT])
                    self.cp("act", self.QTA[64:96, h, g4 * 512:(g4 + 1) * 512], pm[0:32, :], [pmT], [self.Qa[h]])
            self.P.cur_tag = 'L%d.odd.attn' % l
            calls = []
            for h in range(2):
                Kt, KT_ = Ks[h]
                for Q in range(4):
                    def lhs_fn(kt, Kt=Kt, KT_=KT_):
                        return Kt[0:_R('moba', 96), kt * 128:(kt + 1) * 128], [KT_, self.Kaug, self.KBaug]

                    def mask_fn(kt, Q=Q):
                        if kt < 4 * Q:
                            return None
                        return [([[1, 128]], 0, -1)]

                    def pv_fn(kt, h=h):
                        return self.VA[:, kt, h, :], [self.VA_T[h]]

                    def after(po, poT, h=h, Q=Q, zoff=zoff):
                        rc, rcT = self.sml.get()
                        self.recip(rc[:, 0:4], po[:, :, 64], poT, [rcT])
                        t, tT = self.tmp.get()
                        t3 = t[:, 0:256].rearrange("p (a d) -> p a d", a=4)
                        self.tt("dve", t3, po[:, :, 0:64], rc[:, 0:4].unsqueeze(2).to_broadcast([128, 4, 64]), ALU.mult,
                                poT + [rcT], [tT])
                        self.tt("pool", self.OGT[:, Q * 4:(Q + 1) * 4, zoff + h * 64:zoff + (h + 1) * 64], t3,
                                self.ZS[:, Q * 4:(Q + 1) * 4, zoff + h * 64:zoff + (h + 1) * 64], ALU.mult, [tT, self.ZS_T],
                                self.OGT_T[Q * 4:(Q + 1) * 4])
                    calls.append(dict(lhs_fn=lhs_fn, rhs=self.QTA[0:_R('moba', 96), h, Q * 512:(Q + 1) * 512],
                                      rhs_reads=[self.Qh[h], self.Qa[h]], kts=list(range(4 * Q + 4)), mask_fn=mask_fn,
                                      pv_fn=pv_fn, ncols=65, valid=(lambda a, kt, Q=Q: kt <= 4 * Q + a),
                                      c0=(lambda kt, Q=Q: max(0, kt - 4 * Q) * 128),
                                      lastk=(lambda a, Q=Q: 4 * Q + a), after=after))
            self.run_calls(calls)
            if l == 1 and hp == 0:
                self.dbg("o_q", self.QTA[:, :, :], self.Qh + self.Qa, BF16)
                self.dbg("o_ka", self.KA[:, :], [self.KAh], BF16)
                self.dbg("o_kb", self.KB[:, :], [self.KBh], BF16)
                self.dbg("o_w", self.SELW[:, :], [self.SELW_T])
                self.dbg("o_ogt", self.OGT[:, :, :], self.OGT_T, BF16)
                self.dbg("o_va", self.VA[:, :, :, :], self.VA_T, BF16)
                self.dbg("o_zs", self.ZS[:, :, :], [self.ZS_T], BF16)
                self.dbg("o_xt", self.XT[:, :, :], [t for row in self.XT_T for t in row])
            self.P.cur_tag = 'L%d.odd.fin' % l
            if hp % 2 == 1:
                self.finish_pairs(l, [(0, ("rows", "w_out_odd", li, base - 128)), (128, ("rows", "w_out_odd", li, base))])

    def even_layer(self, l):
        li = l // 2
        M = self.MISC
        EO = EVEN_OFF
        self.memset("pool", self.KB[64:96, :], 0.0, [self.KBaug])
        self.make_h(l)
        if l == 0:
            self.dbg("ht", self.HT[:, :, :], self.HT_T, BF16)
        self.P.cur_tag = 'L%d.A.common' % l
        ka = EO["ka"] + np.arange(64)
        cols = np.concatenate([ka, ka])
        wb, wbT = self.wload(("cols", "w_in_even", li, cols))
        ws, wsT = self.wload(("cols", "w_in_even", li, _sw(cols)))
        self.proj_fm(wb, wbT, ws, wsT, [(0, (lambda tb: self.KB[0:64, tb * 512:(tb + 1) * 512]), self.KBh, True)])
        cols = np.concatenate([EO["va"] + np.arange(64), EO["gb"] + np.arange(24)])
        wv, wvT = self.wload(("cols", "w_in_even", li, cols))

        def evv(tg, ps3, pT):
            self.cp("act", self.VAa[:, tg * 4:(tg + 1) * 4, 0:64], ps3[:, :, 0:64], [pT], [self.VAa_T])
            self.act(self.GSG[:, tg * 4:(tg + 1) * 4, :], ps3[:, :, 64:88], AF.Sigmoid, [pT], [self.GSG_T])
        self.proj_tm(wv, wvT, evv)
        for ag in range(2):
            self.P.cur_tag = 'L%d.A.proj' % l
            for pr in range(2):
                self.qproj("w_in_even", li, EO["qa"] + (ag * 2 + pr) * 128, (pr * 2, pr * 2 + 1))
            for pr in range(2):
                wz, wzT = self.wload(("cols", "w_in_even", li, EO["za"] + (ag * 2 + pr) * 128 + np.arange(128)))

                def evz(tg, ps3, pT, pr=pr):
                    self.act(self.ZS[:, tg * 4:(tg + 1) * 4, pr * 128:(pr + 1) * 128], ps3, AF.Silu, [pT], [self.ZS_T])
                self.proj_tm(wz, wzT, evz)
            if l == 0 and ag == 0:
                self.dbg("qa", self.QTA[:, :, :], self.Qh, BF16)
                self.dbg("kb", self.KB[:, :], [self.KBh], BF16)
                self.dbg("vaa", self.VAa[:, :, :], [self.VAa_T], BF16)
                self.dbg("zs", self.ZS[:, :, :], [self.ZS_T], BF16)
                self.dbg("gsg", self.GSG[:, :, :], [self.GSG_T])
            self.P.cur_tag = 'L%d.A.attn' % l
            calls = []
            for i in range(16):
                kts = [i - 1, i] if i > 0 else [0]

                def lhs_fn(kt):
                    return self.KB[0:_R('swa', 64), kt * 128:(kt + 1) * 128], [self.KBh, self.KBaug]

                def mask_fn(kt, i=i):
                    if kt == i:
                        return [([[0, 4], [1, 128]], 0, -1)]
                    return [([[0, 4], [-1, 128]], -1, 1)]

                def pv_fn(kt):
                    return self.VAa[:, kt, :], [self.VAa_T]

                def after(po, poT, i=i, ag=ag):
                    rc, rcT = self.sml.get()
                    self.tt("dve", rc[:, 0:4], po[:, :, 64], self.ESINK[:, li * 8 + ag * 4:li * 8 + ag * 4 + 4], ALU.add,
                            poT + [self.ESINK_T], [rcT])
                    self.recip(rc[:, 4:8], rc[:, 0:4], [rcT], [rcT])
                    t, tT = self.tmp.get()
                    t3 = t[:, 0:256].rearrange("p (a d) -> p a d", a=4)
                    self.tt("dve", t3, po[:, :, 0:64], rc[:, 4:8].unsqueeze(2).to_broadcast([128, 4, 64]), ALU.mult,
                            poT + [rcT], [tT])
                    self.tt("pool", self.OGT[:, i, :], t[:, 0:256], self.ZS[:, i, :], ALU.mult, [tT, self.ZS_T],
                            [self.OGT_T[i]])
                calls.append(dict(lhs_fn=lhs_fn, rhs=self.QTA[0:_R('swa', 64), :, i * 128:(i + 1) * 128], rhs_reads=self.Qh + self.Qa, kts=kts,
                                  mask_fn=mask_fn, pv_fn=pv_fn, ncols=65, valid=(lambda a, kt: True),
                                  lastk=(lambda a, i=i: i), after=after))
            self.run_calls(calls)
            if l == 0:
                self.dbg("ogt_a%d" % ag, self.OGT[:, :, :], self.OGT_T, BF16)
            self.P.cur_tag = 'L%d.A.fin' % l
            self.finish_pairs(l, [(pr * 128, ("rows", "w_out_even", li, (ag * 2 + pr) * 128)) for pr in range(2)])
            if l == 0 and ag == 0:
                self.dbg("xt_a0", self.XT[:, :, :], [t for row in self.XT_T for t in row])
        for g in range(2):
            self.P.cur_tag = 'L%d.B.proj' % l
            for pr in range(2):
                self.qproj("w_in_even", li, EO["qb"] + (g * 2 + pr) * 128, (pr * 2, pr * 2 + 1))
            kc = EO["kc"] + g * 64 + np.arange(64)
            vc = EO["vc"] + g * 64 + np.arange(64)
            wb, wbT = self.wload(("cols", "w_in_even", li, np.concatenate([kc, vc])))
            ws, wsT = self.wload(("cols", "w_in_even", li, np.concatenate([_sw(kc), _sw(kc)])))
            self.proj_fm(wb, wbT, ws, wsT, [
                (0, (lambda tb: self.KA[0:64, tb * 512:(tb + 1) * 512]), self.KAh, True),
                (64, (lambda tb: self.KB[0:64, tb * 512:(tb + 1) * 512]), self.KBh, False)])
            self.P.cur_tag = 'L%d.B.cmpmlp' % l
            for kv in range(2):
                SRC, SRC_T = (self.KA, self.KAh) if kv == 0 else (self.KB, self.KBh)
                pe = self.MB[0:64, 832 + li * 64 + kv * 32:832 + li * 64 + kv * 32 + 32]
                w2 = self.MB[:, 320 + li * 256 + kv * 128:320 + li * 256 + (kv + 1) * 128]
                ph = [self.pM[0], self.pM[1]]
                pbis = [self.pS[0], self.pS[1]]
                for lc in range(8):
                    w1, w1T = self.wload(("w1", "cmp_w1_k" if kv == 0 else "cmp_w1_v", li, lc * 4), nparts=64)
                    for ll in range(4):
                        lidx = lc * 4 + ll
                        for hc in range(2):
                            lhsT = w1[0:64, ll * 256 + hc * 128:ll * 256 + (hc + 1) * 128]
                            self.mm(ph[hc][0][:, 0:127], lhsT, SRC[0:64, lidx:lidx + 16 * 126 + 1:16], lidx == 0, lidx == 31,
                                    [w1T, SRC_T], [ph[hc][1]])
                            self.mm(pbis[hc][0][:, 0:1], lhsT, pe[:, lidx:lidx + 1], lidx == 0, lidx == 31,
                                    [w1T, self.MB_T], [pbis[hc][1]])
                bi, biT = self.sml.get()
                for hc in range(2):
                    self.cp("dve", bi[:, 2 * hc:2 * hc + 1], pbis[hc][0][:, 0:1], [pbis[hc][1]], [biT])
                for hc in range(2):
                    xx, xT = self.tmp.get()
                    self.act(xx[:, 0:127], ph[hc][0][:, 0:127], AF.Identity, [ph[hc][1], biT], [xT],
                             bias=bi[:, 2 * hc:2 * hc + 1])
                    u, uT = self.tmp.get()
                    self.tt("dve", u[:, 0:127], xx[:, 0:127], xx[:, 0:127], ALU.mult, [xT], [uT])
                    self.ts("dve", u[:, 0:127], u[:, 0:127], 0.044715, 1.0, ALU.mult, ALU.add, [uT], [uT])
                    self.tt("dve", u[:, 0:127], u[:, 0:127], xx[:, 0:127], ALU.mult, [uT, xT], [uT])
                    self.act(u[:, 0:127], u[:, 0:127], AF.Sigmoid, [uT], [uT], scale=1.5957691216057308)
                    self.tt("dve", self.HID[:, kv, hc, 0:127], u[:, 0:127], xx[:, 0:127], ALU.mult, [uT, xT],
                            [self.HID_T[kv]])
                pm, pmT = self.nextM()
                if kv == 0:
                    for hc in range(2):
                        self.mm(pm[0:64, 0:127], w2[:, hc * 64:(hc + 1) * 64], self.HID[:, 0, hc, 0:127], hc == 0, hc == 1,
                                [self.MB_T, self.HID_T[0]], [pmT])
                    self.cp("act", self.KCMP[0:64, 0:127], pm[0:64, 0:127], [pmT], [self.KCMP_T])
                else:
                    for hc in range(2):
                        self.mm(pm[0:127, 0:64], self.HID[:, 1, hc, 0:127], w2[:, hc * 64:(hc + 1) * 64], hc == 0, hc == 1,
                                [self.MB_T, self.HID_T[1]], [pmT])
                    self.cp("act", self.VCX[0:127, 0:64], pm[0:127, 0:64], [pmT], [self.VCX_T])
            self.P.cur_tag = 'L%d.B.proj2' % l
            for pr in range(2):
                wz, wzT = self.wload(("cols", "w_in_even", li, EO["zb"] + (g * 2 + pr) * 128 + np.arange(128)))

                def evz(tg, ps3, pT, pr=pr):
                    self.act(self.ZS[:, tg * 4:(tg + 1) * 4, pr * 128:(pr + 1) * 128], ps3, AF.Silu, [pT], [self.ZS_T])
                self.proj_tm(wz, wzT, evz)
            cols = np.concatenate([EO["vs"] + g * 64 + np.arange(64), EO["vw"] + g * 64 + np.arange(64)])
            wv, wvT = self.wload(("cols", "w_in_even", li, cols))

            def evv2(tg, ps3, pT):
                for h in range(2):
                    self.cp("act", self.VA[:, tg * 4:(tg + 1) * 4, h, 0:64], ps3[:, :, h * 64:(h + 1) * 64], [pT],
                            [self.VA_T[h]])
            self.proj_tm(wv, wvT, evv2)
            self.P.cur_tag = 'L%d.B.cmpattn' % l
            W = self.SELW
            calls = []
            for i in range(16):
                def lhs_fn(kt):
                    return self.KCMP[0:_R('cmp', 64), 0:127], [self.KCMP_T]

                def mask_fn(kt, i=i):
                    return [([[0, 4], [1, 128]], 128 * i - 31, -16)]

                def pv_fn(kt):
                    return self.VCX[0:127, :], [self.VCX_T]

                def after(po, poT, i=i, g=g):
                    rc, rcT = self.sml.get()
                    self.ts("dve", rc[:, 0:4], po[:, :, 64], 1e-30, None, ALU.max, None, poT, [rcT])
                    self.recip(rc[:, 4:8], rc[:, 0:4], [rcT], [rcT])
                    imp = W[:, 0:32]
                    for r in range(4):
                        if r == 0:
                            self.ts("dve", imp, po[:, 0, 65:97], rc[:, 4:5], None, ALU.mult, None, [poT[0], rcT],
                                    [self.SELW_T])
                        else:
                            self.stt(imp, po[:, r, 65:97], rc[:, 4 + r:5 + r], imp, ALU.mult, ALU.add,
                                     [poT[r], rcT, self.SELW_T], [self.SELW_T])
                    gsl = self.GSG[:, i, :].rearrange("p (h b) -> p h b", b=3)[:, g * 4:(g + 1) * 4, 0]
                    self.tt("dve", rc[:, 8:12], rc[:, 4:8], gsl, ALU.mult, [rcT, self.GSG_T], [rcT])
                    self.tt("dve", self.OGT[:, i, :].rearrange("p (a d) -> p a d", a=4), po[:, :, 0:64],
                            rc[:, 8:12].unsqueeze(2).to_broadcast([128, 4, 64]), ALU.mult, poT + [rcT], [self.OGT_T[i]])
                    self.tt("dve", W[:, 32:64], imp, M[:, FT0 + i * 32:FT0 + (i + 1) * 32], ALU.max,
                            [self.SELW_T, self.MISC_T], [self.SELW_T])
                    self.tt("dve", W[:, 64:96], W[:, 32:64], M[:, VT0 + i * 32:VT0 + (i + 1) * 32], ALU.min,
                            [self.SELW_T, self.MISC_T], [self.SELW_T])
                    self.P.add("dve", lambda e: e.max(out=W[:, 96:104], in_=W[:, 64:96]), [self.SELW_T], [self.SELW_T])
                    self.ts("dve", self.SELB[:, i, 0, :], W[:, 64:96], W[:, 103:104], NEG, ALU.is_lt, ALU.mult,
                            [self.SELW_T], [self.SELB_T])
                calls.append(dict(lhs_fn=lhs_fn, rhs=self.QTA[0:_R('cmp', 64), :, i * 128:(i + 1) * 128], rhs_reads=self.Qh + self.Qa, kts=[0],
                                  mask_fn=mask_fn, pv_fn=pv_fn, ncols=97, valid=(lambda a, kt: True),
                                  lastk=(lambda a: 0), kparts=127, after=after))
            self.run_calls(calls)
            for g4 in range(4):
                pm, pmT = self.nextM()
                for a in range(4):
                    tt_ = g4 * 4 + a
                    self.mm(pm[0:32, a * 128:(a + 1) * 128], self.SELB[:, tt_, 0, :], self.IDB, True, True,
                            [self.SELB_T, self.MB_T], [pmT])
                self.cp("act", self.QTA[64:96, :, g4 * 512:(g4 + 1) * 512],
                        pm[0:32, :].unsqueeze(1).to_broadcast([32, 4, 512]), [pmT], self.Qa)
            if l == 0 and g == 0:
                self.dbg("kcmp", self.KCMP[:, :], [self.KCMP_T], BF16)
                self.dbg("vcx", self.VCX[:, :], [self.VCX_T], BF16)
                self.dbg("ogt_cmp", self.OGT[:, :, :], self.OGT_T, BF16)
                self.dbg("selb", self.SELB[:, :, :, :], [self.SELB_T], BF16)
                self.dbg("qaug", self.QTA[:, :, :], self.Qh + self.Qa, BF16)
            self.P.cur_tag = 'L%d.B.kskw' % l
            ks = EO["ks"] + g * 64 + np.arange(64)
            kw = EO["kw"] + g * 64 + np.arange(64)
            cols = np.concatenate([ks, kw])
            wb, wbT = self.wload(("cols", "w_in_even", li, cols))
            ws, wsT = self.wload(("cols", "w_in_even", li, _sw(cols)))
            self.proj_fm(wb, wbT, ws, wsT, [
                (0, (lambda tb: self.KA[0:64, tb * 512:(tb + 1) * 512]), self.KAh, True),
                (64, (lambda tb: self.KB[0:64, tb * 512:(tb + 1) * 512]), self.KBh, True)])
            self.P.cur_tag = 'L%d.B.slcwin' % l
            calls = []
            for i in range(16):
                for br in (1, 2):
                    if br == 1:
                        kts = list(range(i + 1))

                        def lhs_fn(kt):
                            return self.KA[0:_R('slc', 96), kt * 128:(kt + 1) * 128], [self.KAh, self.Kaug]
                        rhs = self.QTA[0:_R('slc', 96), :, i * 128:(i + 1) * 128]
                        rreads = self.Qh + self.Qa

                        def mask_fn(kt, i=i):
                            if kt == i:
                                return [([[0, 4], [1, 128]], 0, -1)]
                            return None

                        def pv_fn(kt):
                            return self.VA[:, kt, 0, :], [self.VA_T[0]]
                    else:
                        kts = list(range(max(0, i - 4), i + 1))

                        def lhs_fn(kt):
                            return self.KB[0:_R('win', 64), kt * 128:(kt + 1) * 128], [self.KBh, self.KBaug]
                        rhs = self.QTA[0:_R('win', 64), :, i * 128:(i + 1) * 128]
                        rreads = self.Qh + self.Qa

                        def mask_fn(kt, i=i):
                            if kt == i:
                                return [([[0, 4], [1, 128]], 0, -1)]
                            if kt == i - 4:
                                return [([[0, 4], [-1, 128]], -1, 1)]
                            return None

                        def pv_fn(kt):
                            return self.VA[:, kt, 1, :], [self.VA_T[1]]

                    def after(po, poT, i=i, br=br, g=g):
                        rc, rcT = self.sml.get()
                        self.recip(rc[:, 0:4], po[:, :, 64], poT, [rcT])
                        gsl = self.GSG[:, i, :].rearrange("p (h b) -> p h b", b=3)[:, g * 4:(g + 1) * 4, br]
                        self.tt("dve", rc[:, 4:8], rc[:, 0:4], gsl, ALU.mult, [rcT, self.GSG_T], [rcT])
                        t, tT = self.tmp.get()
                        self.tt("dve", t[:, 0:256].rearrange("p (a d) -> p a d", a=4), po[:, :, 0:64],
                                rc[:, 4:8].unsqueeze(2).to_broadcast([128, 4, 64]), ALU.mult, poT + [rcT], [tT])
                        self.tt("pool", self.OGT[:, i, :], t[:, 0:256], self.OGT[:, i, :], ALU.add, [tT, self.OGT_T[i]],
                                [self.OGT_T[i]])
                        if br == 2:
                            self.tt("pool", self.OGT[:, i, :], self.OGT[:, i, :], self.ZS[:, i, :], ALU.mult,
                                    [self.OGT_T[i], self.ZS_T], [self.OGT_T[i]])
                    calls.append(dict(lhs_fn=lhs_fn, rhs=rhs, rhs_reads=rreads, kts=kts, mask_fn=mask_fn, pv_fn=pv_fn,
                                      ncols=65, valid=(lambda a, kt: True), lastk=(lambda a, i=i: i), after=after))
            self.run_calls(calls)
            if l == 0:
                self.dbg("ogt_b%d" % g, self.OGT[:, :, :], self.OGT_T, BF16)
            self.P.cur_tag = 'L%d.B.fin' % l
            self.finish_pairs(l, [(pr * 128, ("rows", "w_out_even", li, 512 + (g * 2 + pr) * 128)) for pr in range(2)])

    def build(self):
        self.phase0()
        for l in range(self.depth):
            if l % 2 == 0:
                self.even_layer(l)
            else:
                self.odd_layer(l)
        self.final()
        self.P.finalize()
        return self.nc


def _rope_tables():
    inv = (np.float32(10000.0) ** (-np.arange(0, 64, 2, dtype=np.float32) / np.float32(64))).astype(np.float32)
    ang = (np.arange(S, dtype=np.float32)[:, None] * inv[None, :]).astype(np.float32)
    cos = np.cos(ang).astype(np.float32).T
    sin = np.sin(ang).astype(np.float32).T
    CC = np.concatenate([cos] * 4, axis=0)
    SS = np.concatenate([-sin, sin, -sin, sin], axis=0)
    return np.ascontiguousarray(np.concatenate([CC, SS], axis=1).astype(np.float32))


def _static_misc():
    m = np.zeros((128, MISC_W), np.float32)
    m[:, IDF0:IDF0 + 128] = np.eye(128, dtype=np.float32)
    pb = np.zeros((16, 8), np.float32)
    npb = np.zeros((16, 8), np.float32)
    for i in range(16):
        for j in range(8):
            if j < i // 2:
                npb[i, j] = NEG
            else:
                pb[i, j] = -1e30
    m[:, PBT0:PBT0 + 128] = pb.reshape(1, 128)
    m[:, NPT0:NPT0 + 128] = npb.reshape(1, 128)
    t = np.arange(S)
    tb = t // 64
    jj = np.arange(32)
    valid = jj[None, :] <= tb[:, None]
    forced = (jj[None, :] == 0) | (jj[None, :] == tb[:, None]) | (jj[None, :] == tb[:, None] - 1)
    Ft = np.where(forced, np.float32(1e4), np.float32(0.0)).astype(np.float32)
    Vt = np.where(valid, np.float32(1e30), np.float32(-1e30)).astype(np.float32)
    m[:, FT0:FT0 + 512] = Ft.reshape(16, 128, 32).transpose(1, 0, 2).reshape(128, 512)
    m[:, VT0:VT0 + 512] = Vt.reshape(16, 128, 32).transpose(1, 0, 2).reshape(128, 512)
    ncmp = 127
    cst = np.arange(ncmp) * 16
    ov = ((cst[:, None] < (jj[None, :] + 1) * 64) & (cst[:, None] + 32 > jj[None, :] * 64)).astype(np.float32)
    m2 = np.zeros((128, 1024), np.float32)
    m2[0:127, OV0] = 1.0
    m2[0:127, OV0 + 1:OV0 + 33] = ov
    return m, m2


def _e32():
    e = np.zeros((128, S), np.float32)
    k = np.arange(S)
    for j in range(32):
        e[64 + j, :] = (k // 64 == j)
    return e


def _build_wblk(descs, inputs):
    wb = np.zeros((NB, 128, 1024), np.float32)
    for n, d in enumerate(descs):
        kind = d[0]
        if kind == "ada":
            _, l, nb = d
            W = inputs["w_ada"][l][:, nb * 128:(nb + 1) * 128]
            wb[n] = W.reshape(8, 128, 128).transpose(1, 0, 2).reshape(128, 1024)
        elif kind == "cols":
            _, name, li, cols = d
            W = inputs[name][li][:, cols]
            k = len(cols)
            blk = wb[n].reshape(128, 8, 128)
            blk[:, :, 0:k] = W.reshape(8, 128, k).transpose(1, 0, 2)
        elif kind == "rows":
            _, name, li, r0 = d
            wb[n] = inputs[name][li][r0:r0 + 128, :]
        elif kind == "w1":
            _, name, li, l0 = d
            W = inputs[name][li][l0 * 64:(l0 + 4) * 64, :]
            wb[n, 0:64, :] = W.reshape(4, 64, 256).transpose(1, 0, 2).reshape(64, 1024)
    return wb


_CACHE = {}


def kernel(**inputs):
    inputs = {k: np.asarray(v) for k, v in inputs.items()}
    if "b" not in _CACHE:
        b = Builder(4)
        nc = b.build()
        assert len(b.descs) == NB, (len(b.descs), NB)
        _CACHE["b"] = (b, nc)
    b, nc = _CACHE["b"]
    wblk = _build_wblk(b.descs, inputs)
    cs = _rope_tables()
    e32 = _e32()
    base, m2 = _static_misc()
    base[:, NG0:NG0 + 32] = inputs["norm_g"].reshape(4, 8, 128).transpose(2, 0, 1).reshape(128, 32)
    base[:, FG0:FG0 + 8] = inputs["final_g"].reshape(8, 128).T
    base[:, BADA0:BADA0 + 96] = inputs["b_ada"].reshape(4, 24, 128).transpose(2, 0, 1).reshape(128, 96)
    base[:, SINK0:SINK0 + 16] = inputs["a_sinks"].reshape(1, 16)
    for i in range(2):
        m2[:, W20 + i * 256:W20 + i * 256 + 128] = inputs["cmp_w2_k"][i].reshape(2, 128, 64).transpose(1, 0, 2).reshape(128, 128)
        m2[:, W20 + i * 256 + 128:W20 + (i + 1) * 256] = inputs["cmp_w2_v"][i].reshape(2, 128, 64).transpose(1, 0, 2).reshape(128, 128)
        m2[0:64, PE0 + i * 64:PE0 + i * 64 + 32] = inputs["cmp_pe_k"][i].T
        m2[0:64, PE0 + i * 64 + 32:PE0 + (i + 1) * 64] = inputs["cmp_pe_v"][i].T
    in_maps = []
    for core in range(8):
        m = base.copy()
        m[:, CV0:CV0 + 8] = inputs["c"][core].reshape(8, 128).T
        in_maps.append({"x": np.ascontiguousarray(inputs["x"][core]), "misc": m, "misc2": m2, "cs": cs, "e32": e32, "wblk": wblk})
    res = run_bass_kernel_spmd(nc, in_maps, core_ids=list(range(8)))
    return np.stack([np.asarray(r["out"], dtype=np.float32) for r in res.results], axis=0)
```

```python
import os
import numpy as np
from contextlib import ExitStack
import concourse.bass as bass
import concourse.mybir as mybir
from concourse.bass_utils import run_bass_kernel_spmd

F32 = mybir.dt.float32
BF16 = mybir.dt.bfloat16
ALU = mybir.AluOpType
AF = mybir.ActivationFunctionType
AX = mybir.AxisListType

COMPUTE = ("pe", "act", "dve", "pool")
S = 2048
D = 1024
_PADS = os.environ.get('K_PAD', 'swa,cmp,slc,win,moba,gs').split(',')
def _R(site, small):
    return 128 if site in _PADS else small
NEG = -30000.0


class T:
    __slots__ = ("name", "w", "r", "rd")

    def __init__(self, name):
        self.name = name
        self.w = None
        self.r = {}
        self.rd = []


class Op:
    __slots__ = ("eng", "fn", "reads", "writes", "dma", "idx", "deps", "signal",
                 "count", "waits", "dma_count", "tag", "iname")

    def __init__(self, eng, fn, reads, writes, dma):
        self.eng = eng
        self.fn = fn
        self.reads = reads
        self.writes = writes
        self.dma = dma
        self.signal = False
        self.count = 0
        self.waits = []
        self.dma_count = 0


class Prog:
    def __init__(self, nc):
        self.nc = nc
        self.ops = []
        self.es = ExitStack()

    def sb(self, name, shape, dt):
        return self.es.enter_context(self.nc.sbuf_tensor(name, list(shape), dt))

    def ps(self, name, shape, dt=F32):
        return self.es.enter_context(self.nc.psum_tensor(name, list(shape), dt))

    def add(self, eng, fn, reads=(), writes=(), dma=None):
        op = Op(eng, fn, list(reads), list(writes), dma)
        op.tag = getattr(self, "cur_tag", "")
        op.iname = None
        op.idx = len(self.ops)
        self.ops.append(op)
        return op

    def finalize(self):
        nc = self.nc
        ops = self.ops
        for i, op in enumerate(ops):
            deps = set()
            for t in op.reads:
                if t.w is not None:
                    deps.add(t.w)
            for t in op.writes:
                if t.w is not None:
                    deps.add(t.w)
                deps.update(t.r.values())
                deps.update(t.rd)
            for t in op.reads:
                if op.dma is not None:
                    t.rd.append(i)
                else:
                    t.r[op.eng] = i
            for t in op.writes:
                t.w = i
                t.r = {}
                t.rd = []
            deps.discard(i)
            op.deps = sorted(deps)
        pos = {}
        cnt = {e: 0 for e in COMPUTE}
        for op in ops:
            if op.dma is None and op.eng in COMPUTE:
                pos[op.idx] = cnt[op.eng]
                cnt[op.eng] += 1
        need = []
        for op in ops:
            lst = []
            for d in op.deps:
                dop = ops[d]
                if dop.dma is not None:
                    lst.append(("dma", d))
                    continue
                if dop.eng == op.eng and op.dma is None:
                    if op.eng == "pe":
                        continue
                dop.signal = True
                lst.append(("eng", d))
            need.append(lst)
        ecount = {e: 0 for e in COMPUTE}
        dcount = {}
        for op in ops:
            if op.dma is not None:
                dcount[op.dma] = dcount.get(op.dma, 0) + 16
                op.dma_count = dcount[op.dma]
            elif op.signal:
                ecount[op.eng] += 1
                op.count = ecount[op.eng]
        self.ecount = ecount
        SEM_LIM = int(os.environ.get("K_SEMLIM", "100000"))
        esem = {}
        for e in COMPUTE:
            nep = ecount[e] // SEM_LIM + 1
            esem[e] = [self.es.enter_context(nc.semaphore("s_%s%d" % (e, k))) for k in range(nep)]
        dsem = {}
        for k in dcount:
            dsem[k] = self.es.enter_context(nc.semaphore("d_%s" % (k,)))
        waited = {}
        dcur = {}
        streams = {}
        for op in ops:
            w = waited.setdefault(op.eng, {})
            waits = []
            for kind, d in need[op.idx]:
                dop = ops[d]
                if kind == "dma":
                    key = ("d", dop.dma)
                    val = dcur[dop.dma]
                    sem = dsem[dop.dma]
                else:
                    key = ("e", dop.eng)
                    val = dop.count
                    ep = (val - 1) // SEM_LIM
                    sem = esem[dop.eng][ep]
                if w.get(key, 0) >= val:
                    continue
                w[key] = val
                if kind != "dma":
                    val = val - ep * SEM_LIM
                waits.append((sem, val))
            op.waits = waits
            if op.dma is not None:
                dcur[op.dma] = op.dma_count
            streams.setdefault(op.eng, []).append(op)
        self.streams = streams

        def emit(engine, lst):
            for op in lst:
                for sem, val in op.waits:
                    engine.wait_ge(sem, val)
                if op.fn is None:
                    continue
                ins = op.fn(engine)
                try:
                    op.iname = ins.ins.name
                except Exception:
                    pass
                if op.dma is not None:
                    ins.then_inc(dsem[op.dma], 16)
                elif op.signal:
                    ins.then_inc(esem[op.eng][(op.count - 1) // SEM_LIM], 1)

        with nc.Block() as block:
            if "sp" in streams:
                @block.sync
                def _(e):
                    emit(e, streams["sp"])
            if "pe" in streams:
                @block.tensor
                def _(e):
                    emit(e, streams["pe"])
            if "act" in streams:
                @block.scalar
                def _(e):
                    emit(e, streams["act"])
            if "dve" in streams:
                @block.vector
                def _(e):
                    emit(e, streams["dve"])
            if "pool" in streams:
                @block.gpsimd
                def _(e):
                    emit(e, streams["pool"])
        self.es.close()


class Rot:
    def __init__(self, P, name, n, shape, dt):
        self.bufs = [(P.sb("%s%d" % (name, i), shape, dt), T("%s%d" % (name, i))) for i in range(n)]
        self.i = 0

    def get(self):
        b = self.bufs[self.i % len(self.bufs)]
        self.i += 1
        return b


CV0, BADA0, NG0, FG0, SINK0, IDF0, PBT0, NPT0, FT0, VT0 = (
    0, 8, 104, 136, 144, 160, 288, 416, 544, 1056)
MISC_W = 1568
OV0, W20, PE0 = 0, 64, 576
NB = 96 + 2 * 56 + 2 * 77

A_QW, A_KVW = 512, 64
EVEN_OFF = {}
_o = 0
for _n, _s in (("qa", 512), ("ka", 64), ("va", 64), ("za", 512), ("qb", 512), ("kc", 128), ("vc", 128),
               ("ks", 128), ("vs", 128), ("kw", 128), ("vw", 128), ("gb", 24), ("zb", 512)):
    EVEN_OFF[_n] = _o
    _o += _s


def _sw(cols):
    cols = np.asarray(cols).reshape(-1, 2, 32)
    return cols[:, ::-1, :].reshape(-1)


class Builder:
    def __init__(self, depth=4):
        self.depth = depth
        self.nc = nc = bass.Bass("TRN2", target_bir_lowering=False)
        self.P = P = Prog(nc)
        self.descs = []
        self.dbg_names = []
        self.x_d = nc.dram_tensor("x", [S, D], F32, kind="ExternalInput").ap()
        self.misc_d = nc.dram_tensor("misc", [128, MISC_W], F32, kind="ExternalInput").ap()
        self.misc2_d = nc.dram_tensor("misc2", [128, 1024], F32, kind="ExternalInput").ap()
        self.cs_d = nc.dram_tensor("cs", [128, 2 * S], F32, kind="ExternalInput").ap()
        self.e32_d = nc.dram_tensor("e32", [128, S], F32, kind="ExternalInput").ap()
        self.wblk_d = nc.dram_tensor("wblk", [NB, 128, 1024], F32, kind="ExternalInput").ap()
        self.out_d = nc.dram_tensor("out", [S, D], F32, kind="ExternalOutput").ap()
        self.XT = P.sb("XT", [128, 8, S], F32)
        self.XT_T = [[T("xt%d_%d" % (c, tb)) for tb in range(4)] for c in range(8)]
        self.HT = P.sb("HT", [128, 8, S], BF16)
        self.HT_T = [T("ht%d" % tb) for tb in range(4)]
        self.QTA = P.sb("QTA", [128, 4, S], BF16)
        self.Qh = [T("qh%d" % r) for r in range(4)]
        self.Qa = [T("qa%d" % r) for r in range(4)]
        self.KA = P.sb("KA", [128, S], BF16)
        self.KB = P.sb("KB", [128, S], BF16)
        self.KAh, self.KBh, self.Kaug, self.KBaug = T("kah"), T("kbh"), T("kaug"), T("kbaug")
        self.VA = P.sb("VA", [128, 16, 2, 65], BF16)
        self.VA_T = [T("va0"), T("va1")]
        self.VAa = P.sb("VAa", [128, 16, 65], BF16)
        self.VAa_T = T("vaa")
        self.ZS = P.sb("ZS", [128, 16, 256], BF16)
        self.ZS_T = T("zs")
        self.GSG = P.sb("GSG", [128, 16, 24], F32)
        self.GSG_T = T("gsg")
        self.OGT = P.sb("OGT", [128, 16, 256], BF16)
        self.OGT_T = [T("ogt%d" % t) for t in range(16)]
        self.OGTT = P.sb("OGTT", [128, S], BF16)
        self.OGTT_T = T("ogtt")
        self.CS = P.sb("CS", [128, 2 * S], F32)
        self.CS_T = T("cs")
        self.MISC = P.sb("MISC", [128, MISC_W], F32)
        self.MISC_T = T("misc")
        self.MB = P.sb("MB", [128, 1024], BF16)
        self.MB_T = T("mb")
        self.ADA = P.sb("ADA", [128, 96], F32)
        self.GSC = P.sb("GSC", [128, 32], F32)
        self.ADA_T = T("ada")
        self.CACT = P.sb("CACT", [128, 8, 2], BF16)
        self.CACT_T = T("cact")
        self.ESINK = P.sb("ESINK", [128, 16], F32)
        self.ESINK_T = T("esink")
        self.KCMP = P.sb("KCMP", [128, 128], BF16)
        self.KCMP_T = T("kcmp")
        self.VCX = P.sb("VCX", [128, 97], BF16)
        self.VCX_T = T("vcx")
        self.HID = P.sb("HID", [128, 2, 2, 128], BF16)
        self.HID_T = [T("hidk"), T("hidv")]
        self.KM = P.sb("KM", [128, 2, 8], BF16)
        self.KM_T = T("km")
        self.SELW = P.sb("SELW", [128, 1024], F32)
        self.SELW_T = T("selw")
        self.SELB = P.sb("SELB", [128, 16, 2, 32], BF16)
        self.SELB_T = T("selb")
        self.stage = Rot(P, "st", 2, [128, 1024], F32)
        self.wbp = Rot(P, "wb", 3, [128, 1024], BF16)
        self.ptp = Rot(P, "pt", 3, [128, 512], BF16)
        self.tmp = Rot(P, "tmp", 3, [128, 512], F32)
        self.rsp = Rot(P, "rs", 2, [128, 512], F32)
        self.sml = Rot(P, "sml", 4, [128, 16], F32)
        self.pS = [(P.ps("pS%d" % i, [128, 512]), T("pS%d" % i)) for i in range(2)]
        self.pO = P.ps("pO", [128, 4, 512])
        self.pO_T2 = [[T("pO%d_%d" % (st_, i)) for i in range(4)] for st_ in range(2)]
        self.ocount = 0
        self.pM = [(P.ps("pM%d" % i, [128, 512]), T("pM%d" % i)) for i in range(2)]
        self.si = 0
        self.mi = 0

    def nextS(self):
        b = self.pS[self.si % 2]
        self.si += 1
        return b

    def nextM(self):
        banks = self.pM + self.pS
        b = banks[self.mi % 4]
        self.mi += 1
        return b

    def mm(self, out, lhsT, rhs, start, stop, reads, writes):
        self.P.add("pe", lambda e: e.matmul(out, lhsT=lhsT, rhs=rhs, start=start, stop=stop), reads, writes)

    def trf(self, out, in_, ident, reads, writes):
        self.P.add("pe", lambda e: e.transpose(out, in_, ident), reads, writes)

    def act(self, out, in_, func, reads, writes, **kw):
        self.P.add("act", lambda e: e.activation(out=out, in_=in_, func=func, **kw), reads, writes)

    def tt(self, eng, out, in0, in1, op, reads, writes):
        self.P.add(eng, lambda e: e.tensor_tensor(out=out, in0=in0, in1=in1, op=op), reads, writes)

    def ts(self, eng, out, in0, s1, s2, op0, op1, reads, writes):
        if s2 is None:
            self.P.add(eng, lambda e: e.tensor_scalar(out=out, in0=in0, scalar1=s1, scalar2=None, op0=op0), reads, writes)
        else:
            self.P.add(eng, lambda e: e.tensor_scalar(out=out, in0=in0, scalar1=s1, scalar2=s2, op0=op0, op1=op1), reads, writes)

    def stt(self, out, in0, scalar, in1, op0, op1, reads, writes):
        self.P.add("dve", lambda e: e.scalar_tensor_tensor(out=out, in0=in0, scalar=scalar, in1=in1, op0=op0, op1=op1), reads, writes)

    def cp(self, eng, out, in_, reads, writes):
        if eng == "act":
            self.P.add("act", lambda e: e.activation(out=out, in_=in_, func=AF.Copy), reads, writes)
        else:
            self.P.add(eng, lambda e: e.tensor_copy(out=out, in_=in_), reads, writes)

    def recip(self, out, in_, reads, writes):
        self.P.add("dve", lambda e: e.reciprocal(out=out, in_=in_), reads, writes)

    def memset(self, eng, ap, val, writes):
        self.P.add(eng, lambda e: e.memset(ap, val), (), writes)

    def dma(self, out, in_, reads, writes, key):
        self.P.add("sp", lambda e: e.dma_start(out=out, in_=in_), reads, writes, dma=key)

    def sel(self, ap, pattern, base, cm, rw):
        self.P.add("pool", lambda e: e.affine_select(out=ap, in_=ap, pattern=pattern, compare_op=ALU.is_ge,
                                                     fill=0.0, base=base, channel_multiplier=cm), rw, rw)

    def dbg(self, name, ap, reads, dt=F32):
        if not getattr(self, "debug", False):
            return
        shp = list(ap.shape)
        d = self.nc.dram_tensor("dbg_" + name, shp, dt, kind="ExternalOutput").ap()
        self.dma(d, ap, reads, [], key="dbg_" + name)
        self.dbg_names.append("dbg_" + name)

    def wload(self, desc, cast=True, nparts=128, cast_eng=None):
        idx = len(self.descs)
        self.descs.append(desc)
        st, stT = self.stage.get()
        self.dma(st[0:nparts, :], self.wblk_d[idx, 0:nparts, :], [], [stT], key=stT.name)
        if not cast:
            return st, stT
        wb, wbT = self.wbp.get()
        self.cp(cast_eng or os.environ.get("K_CAST", "act"), wb[0:nparts, :], st[0:nparts, :], [stT], [wbT])
        return wb, wbT

    def phase0(self):
        P = self.P
        P.cur_tag = 'phase0'
        self.dma(self.MISC[:, :], self.misc_d, [], [self.MISC_T], key="misc")
        self.dma(self.CS[:, :], self.cs_d, [], [self.CS_T], key="cs")
        M = self.MISC
        self.memset("pool", self.KA[64:128, :], 0.0, [self.Kaug])
        self.memset("pool", self.KB[64:128, :], 0.0, [self.KBaug])
        self.memset("pool", self.QTA[64:128, :, :], 0.0, self.Qa)
        self.memset("pool", self.KM[:, :, :], 0.0, [self.KM_T])
        for half in range(2):
            st, stT = self.stage.get()
            self.dma(st[64:96, :], self.e32_d[64:96, half * 1024:(half + 1) * 1024], [], [stT], key=stT.name)
            self.cp("pool", self.KA[64:96, half * 1024:(half + 1) * 1024], st[64:96, :], [stT], [self.Kaug])
        MB = self.MB
        self.cp("dve", MB[:, 0:128], M[:, IDF0:IDF0 + 128], [self.MISC_T], [self.MB_T])
        self.memset("dve", MB[:, 128:256], 1.0, [self.MB_T])
        st, stT = self.stage.get()
        self.dma(st[:, :], self.misc2_d, [], [stT], key=stT.name)
        self.cp("dve", MB[:, 256:289], st[:, OV0:OV0 + 33], [stT], [self.MB_T])
        self.cp("dve", MB[:, 320:832], st[:, W20:W20 + 512], [stT], [self.MB_T])
        self.cp("dve", MB[:, 832:960], st[:, PE0:PE0 + 128], [stT], [self.MB_T])
        self.IDB = MB[:, 0:128]
        self.ONESB = MB[:, 128:256]
        self.memset("pool", self.VA[:, :, :, 64:65], 1.0, self.VA_T)
        self.memset("pool", self.VAa[:, :, 64:65], 1.0, [self.VAa_T])
        self.memset("pool", self.KCMP[:, :], 0.0, [self.KCMP_T])
        self.memset("pool", self.VCX[:, :], 0.0, [self.VCX_T])
        self.cp("pool", self.VCX[:, 64:97], MB[:, 256:289], [self.MB_T], [self.VCX_T])
        self.act(self.ESINK[:, :], M[:, SINK0:SINK0 + 16], AF.Exp, [self.MISC_T], [self.ESINK_T])
        for j in range(2):
            self.act(self.CACT[:, :, j], M[:, CV0:CV0 + 8], AF.Silu, [self.MISC_T], [self.CACT_T])
        pm, pmT = self.nextM()
        for l in range(self.depth):
            for nb in range(24):
                st, stT = self.wload(("ada", l, nb), cast=True, cast_eng=("act" if nb % 2 == 0 else "dve"))
                col = 2 * (l * 24 + nb)
                for c in range(8):
                    self.mm(pm[:, col:col + 2], st[:, c * 128:(c + 1) * 128], self.CACT[:, c, :],
                            c == 0, c == 7, [stT, self.CACT_T], [pmT])
        n = 24 * self.depth
        if n > 0:
            self.tt("dve", self.ADA[:, 0:n], pm[:, 0:2 * n].rearrange("p (n two) -> p n two", two=2)[:, :, 0],
                    M[:, BADA0:BADA0 + n], ALU.add, [pmT, self.MISC_T], [self.ADA_T])
        for l in range(self.depth):
            self.stt(self.GSC[:, l * 8:(l + 1) * 8], self.ADA[:, l * 24 + 8:l * 24 + 16], 1.0,
                     M[:, NG0 + l * 8:NG0 + (l + 1) * 8], ALU.add, ALU.mult, [self.ADA_T, self.MISC_T], [self.ADA_T])
        self.dbg("ada", self.ADA[:, :], [self.ADA_T])
        self.dbg("gsc", self.GSC[:, :], [self.ADA_T])
        for _k in range(int(os.environ.get("K_DUMMY", "0"))):
            self.memset("dve", self.SELW[:, 0:8], 0.0, [self.SELW_T])
        IDF = M[:, IDF0:IDF0 + 128]
        for tt_ in range(16):
            st, stT = self.stage.get()
            self.dma(st[:, :], self.x_d[tt_ * 128:(tt_ + 1) * 128, :], [], [stT], key=stT.name)
            for half in range(2):
                pm, pmT = self.nextM()
                for cc in range(4):
                    c = half * 4 + cc
                    self.trf(pm[:, cc * 128:(cc + 1) * 128], st[:, c * 128:(c + 1) * 128], IDF,
                             [stT, self.MISC_T], [pmT])
                tb = tt_ // 4
                eng = "act" if half == 0 else "dve"
                self.cp(eng, self.XT[:, half * 4:half * 4 + 4, tt_ * 128:(tt_ + 1) * 128],
                        pm[:, :].rearrange("p (c t) -> p c t", c=4), [pmT],
                        [self.XT_T[half * 4 + cc][tb] for cc in range(4)])

    def rstd_block(self, tb):
        pm, pmT = self.nextM()
        for c in range(8):
            sq, sqT = self.ptp.get()
            xsl = self.XT[:, c, tb * 512:(tb + 1) * 512]
            if c % 2 == 0 or os.environ.get("K_H", "new") == "old":
                self.act(sq[:, :], xsl, AF.Square, [self.XT_T[c][tb]], [sqT])
            else:
                self.tt("dve", sq[:, :], xsl, xsl, ALU.mult, [self.XT_T[c][tb]], [sqT])
            self.mm(pm[:, :], self.ONESB, sq[:, :], c == 0, c == 7, [sqT, self.MB_T], [pmT])
        r, rT = self.rsp.get()
        self.ts("dve", r[:, :], pm[:, :], 1.0 / D, 1e-6, ALU.mult, ALU.add, [pmT], [rT])
        self.act(r[:, :], r[:, :], AF.Sqrt, [rT], [rT])
        self.recip(r[:, :], r[:, :], [rT], [rT])
        return r, rT

    def make_h(self, l):
        self.P.cur_tag = 'L%d.h' % l
        rs = {0: self.rstd_block(0)}
        for tb in range(4):
            if tb + 1 < 4:
                rs[tb + 1] = self.rstd_block(tb + 1)
            r, rT = rs.pop(tb)
            for c in range(8):
                t, tT = self.tmp.get()
                self.tt("pool" if (c % 3 == 0 or os.environ.get("K_H", "new") == "old") else "dve", t[:, :], self.XT[:, c, tb * 512:(tb + 1) * 512], r[:, :], ALU.mult,
                        [self.XT_T[c][tb], rT], [tT])
                self.act(self.HT[:, c, tb * 512:(tb + 1) * 512], t[:, :], AF.Identity, [tT, self.ADA_T], [self.HT_T[tb]],
                         scale=self.GSC[:, l * 8 + c:l * 8 + c + 1], bias=self.ADA[:, l * 24 + c:l * 24 + c + 1])

    def final(self):
        M = self.MISC
        self.P.cur_tag = 'final'
        IDF = M[:, IDF0:IDF0 + 128]
        for tb in range(4):
            r, rT = self.rstd_block(tb)
            for c in range(8):
                xs = self.XT[:, c, tb * 512:(tb + 1) * 512]
                self.stt(xs, xs, M[:, FG0 + c:FG0 + c + 1], r[:, :], ALU.mult, ALU.mult,
                         [self.XT_T[c][tb], rT, self.MISC_T], [self.XT_T[c][tb]])
            for a in range(4):
                tt_ = tb * 4 + a
                st, stT = self.stage.get()
                for half in range(2):
                    pm, pmT = self.nextM()
                    for cc in range(4):
                        c = half * 4 + cc
                        self.trf(pm[:, cc * 128:(cc + 1) * 128], self.XT[:, c, tt_ * 128:(tt_ + 1) * 128], IDF,
                                 [self.XT_T[c][tb], self.MISC_T], [pmT])
                    self.cp("act" if half == 0 else "dve", st[:, half * 512:(half + 1) * 512], pm[:, :], [pmT], [stT])
                self.dma(self.out_d[tt_ * 128:(tt_ + 1) * 128, :], st[:, :], [stT], [], key=stT.name)
        self.P.add("sp", None, (), [b[1] for b in self.stage.bufs])

    def proj_fm(self, wb, wbT, wsw, wswT, dst, M0=128):
        CC, SS = self.CS[:, 0:S], self.CS[:, S:2 * S]
        for tb in range(4):
            tsl = slice(tb * 512, (tb + 1) * 512)
            p1, p1T = self.nextM()
            for c in range(8):
                self.mm(p1[0:M0, :], wb[:, c * 128:c * 128 + M0], self.HT[:, c, tsl], c == 0, c == 7,
                        [wbT, self.HT_T[tb]], [p1T])
            anyrope = any(d[3] for d in dst)
            if anyrope:
                msw = 128 if (len(dst) > 1 and dst[1][3]) else 64
                p2, p2T = self.nextM()
                for c in range(8):
                    self.mm(p2[0:msw, :], wsw[:, c * 128:c * 128 + msw], self.HT[:, c, tsl], c == 0, c == 7,
                            [wswT, self.HT_T[tb]], [p2T])
                t1, t1T = self.tmp.get()
                t2, t2T = self.tmp.get()
                self.tt("dve", t1[0:msw, :], p1[0:msw, :], CC[0:msw, tsl], ALU.mult, [p1T, self.CS_T], [t1T])
                self.tt("dve", t2[0:msw, :], p2[0:msw, :], SS[0:msw, tsl], ALU.mult, [p2T, self.CS_T], [t2T])
            for (r0, apfn, dT, rope) in dst:
                if rope:
                    self.tt(os.environ.get("K_ROPE", "dve"), apfn(tb), t1[r0:r0 + 64, :], t2[r0:r0 + 64, :], ALU.add, [t1T, t2T], [dT])
                else:
                    self.cp("act", apfn(tb), p1[r0:r0 + 64, :], [p1T], [dT])

    def proj_tm(self, wb, wbT, evac):
        for tg in range(4):
            pm, pmT = self.nextM()
            for a in range(4):
                tt_ = tg * 4 + a
                for c in range(8):
                    self.mm(pm[:, a * 128:(a + 1) * 128], self.HT[:, c, tt_ * 128:(tt_ + 1) * 128],
                            wb[:, c * 128:(c + 1) * 128], c == 0, c == 7, [wbT, self.HT_T[tt_ // 4]], [pmT])
            evac(tg, pm[:, :].rearrange("p (a n) -> p a n", a=4), pmT)

    def run_calls(self, calls):
        steps = [(ci, kt) for ci, c in enumerate(calls) for kt in c["kts"]]
        state = {}

        def emit_scores(sidx):
            ci, kt = steps[sidx]
            c = calls[ci]
            kp = c.get("kparts", 128)
            banks4 = self.pS + self.pM
            ps, psT = banks4[self.si % 4]
            self.si += 1
            lhsT, lreads = c["lhs_fn"](kt)
            c0 = c["c0"](kt) if "c0" in c else 0
            pso = ps[0:kp, c0:512]
            rhs = c["rhs"] if c0 == 0 else c["rhs"][:, c0:512]
            if len(rhs.shape) == 3:
                pso = pso.rearrange("p (a n) -> p a n", a=4)
            self.mm(pso, lhsT, rhs, True, True, lreads + c["rhs_reads"], [psT])
            state[sidx] = (ps, psT)
        LA = int(os.environ.get("K_LA", "2"))
        for k in range(min(LA, len(steps))):
            emit_scores(k)
        for sidx in range(len(steps)):
            if sidx + LA < len(steps):
                emit_scores(sidx + LA)
            ci, kt = steps[sidx]
            c = calls[ci]
            kp = c.get("kparts", 128)
            oset = ((self.ocount + ci) % 2) if os.environ.get("K_OSET", "0") == "1" else 0
            off = oset * 256
            poT = self.pO_T2[oset]
            ps, psT = state.pop(sidx)
            pt, ptT = self.ptp.get()
            c0 = c["c0"](kt) if "c0" in c else 0
            self.act(pt[0:kp, c0:512], ps[0:kp, c0:512], AF.Exp, [psT], [ptT], scale=0.125)
            m = c["mask_fn"](kt)
            if m is not None:
                for (pattern, base, cm) in m:
                    n_el = 1
                    for st_, nn in pattern:
                        n_el *= nn
                    self.sel(pt[0:kp, c0:c0 + n_el], pattern, base, cm, [ptT])
            vrhs, vreads = c["pv_fn"](kt)
            ncols = c["ncols"]
            for a in range(4):
                if not c["valid"](a, kt):
                    continue
                self.mm(self.pO[:, a, off:off + ncols], pt[0:kp, a * 128:(a + 1) * 128], vrhs,
                        kt == c["kts"][0], kt == c["lastk"](a), [ptT] + vreads, [poT[a]])
            if kt == c["kts"][-1]:
                c["after"](self.pO[:, :, off:off + 128], poT)
        self.ocount += len(calls)

    def finish_pairs(self, l, pair_specs):
        assert len(pair_specs) == 2
        ogtts = [(self.OGTT[:, :], self.OGTT_T),
                 (self.ZS[:, 0:8, :].rearrange("p a b -> p (a b)"), self.ZS_T)]
        wos = []
        for k, (coff, wdesc) in enumerate(pair_specs):
            wo, woT = self.wload(wdesc)
            wos.append((wo, woT))
            og, ogT = ogtts[k]
            for tg in range(2):
                for hh in range(2):
                    pm, pmT = self.nextM()
                    for a in range(4):
                        tt_ = tg * 8 + hh * 4 + a
                        self.mm(pm[:, a * 128:(a + 1) * 128], self.OGT[:, tt_, coff:coff + 128], self.IDB, True, True,
                                [self.OGT_T[tt_], self.MB_T], [pmT])
                    c0 = (tg * 8 + hh * 4) * 128
                    self.cp("act", og[:, c0:c0 + 512], pm[:, :], [pmT], [ogT])
        for nb in range(8):
            for tb in range(4):
                pm, pmT = self.nextM()
                for k in range(2):
                    wo, woT = wos[k]
                    og, ogT = ogtts[k]
                    self.mm(pm[:, :], wo[:, nb * 128:(nb + 1) * 128], og[:, tb * 512:(tb + 1) * 512], k == 0, k == 1,
                            [woT, ogT], [pmT])
                xs = self.XT[:, nb, tb * 512:(tb + 1) * 512]
                self.stt(xs, pm[:, :], self.ADA[:, l * 24 + 16 + nb:l * 24 + 17 + nb], xs, ALU.mult, ALU.add,
                         [pmT, self.ADA_T, self.XT_T[nb][tb]], [self.XT_T[nb][tb]])

    def qproj(self, name, li, col0, slots):
        cols = col0 + np.arange(128)
        wb, wbT = self.wload(("cols", name, li, cols))
        ws, wsT = self.wload(("cols", name, li, _sw(cols)))
        dst = []
        for k, r in enumerate(slots):
            dst.append((64 * k, (lambda tb, r=r: self.QTA[0:64, r, tb * 512:(tb + 1) * 512]), self.Qh[r], True))
        self.proj_fm(wb, wbT, ws, wsT, dst)

    def odd_layer(self, l):
        li = l // 2
        M = self.MISC
        self.cp("pool", self.KB[64:96, :], self.KA[64:96, :], [self.Kaug], [self.KBaug])
        self.make_h(l)
        for hp in range(8):
            base = hp * 128
            zoff = (hp % 2) * 128
            self.P.cur_tag = 'L%d.odd.proj' % l
            self.qproj("w_in_odd", li, base, (0, 1))
            cols = 1024 + base + np.arange(128)
            wb, wbT = self.wload(("cols", "w_in_odd", li, cols))
            ws, wsT = self.wload(("cols", "w_in_odd", li, _sw(cols)))
            self.proj_fm(wb, wbT, ws, wsT, [
                (0, (lambda tb: self.KA[0:64, tb * 512:(tb + 1) * 512]), self.KAh, True),
                (64, (lambda tb: self.KB[0:64, tb * 512:(tb + 1) * 512]), self.KBh, True)])
            Ks = [(self.KA, self.KAh), (self.KB, self.KBh)]
            self.P.cur_tag = 'L%d.odd.sel' % l
            kmf, kmfT = self.sml.get()
            for h in range(2):
                Kt, KT_ = Ks[h]
                self.P.add("dve", (lambda e, Kt=Kt, h=h, kmf=kmf: e.tensor_reduce(
                    out=kmf[0:64, h * 8:(h + 1) * 8], in_=Kt[0:64, :].rearrange("p (j k) -> p j k", j=8),
                    axis=AX.X, op=ALU.add)), [KT_], [kmfT])
            self.cp("dve", self.KM[0:64, :, :], kmf[0:64, :].rearrange("p (h j) -> p h j", h=2), [kmfT], [self.KM_T])
            wv, wvT = self.wload(("cols", "w_in_odd", li, 2048 + base + np.arange(128)))

            def evv(tg, ps3, pT):
                for h in range(2):
                    self.cp("act", self.VA[:, tg * 4:(tg + 1) * 4, h, 0:64], ps3[:, :, h * 64:(h + 1) * 64], [pT],
                            [self.VA_T[h]])
            self.proj_tm(wv, wvT, evv)
            pm, pmT = self.nextM()
            for tt_ in range(16):
                for h in range(2):
                    col = (tt_ * 2 + h) * 8
                    self.mm(pm[:, col:col + 8], self.QTA[0:_R('gs', 64), h, tt_ * 128:(tt_ + 1) * 128], self.KM[0:_R('gs', 64), h, :],
                            True, True, [self.Qh[h], self.Qa[h], self.KM_T], [pmT])
            W = self.SELW
            gsm = W[:, 0:256].rearrange("p (t h j) -> p t h j", t=16, h=2)
            pb = M[:, PBT0:PBT0 + 128].rearrange("p (t j) -> p t j", t=16).unsqueeze(2).to_broadcast([128, 16, 2, 8])
            self.tt("dve", gsm, pm[:, 0:256].rearrange("p (t h j) -> p t h j", t=16, h=2), pb, ALU.add,
                    [pmT, self.MISC_T], [self.SELW_T])
            g3v = W[:, 0:256].rearrange("p (k j) -> p k j", j=8)
            cur = g3v
            mxs = W[:, 256:352]
            wk = W[:, 512:768].rearrange("p (k j) -> p k j", j=8)
            wk2 = W[:, 768:1024].rearrange("p (k j) -> p k j", j=8)
            for rnd in range(3):
                mcol = mxs[:, rnd * 32:(rnd + 1) * 32]
                self.P.add("dve", (lambda e, cur=cur, mcol=mcol: e.tensor_reduce(out=mcol, in_=cur, axis=AX.X, op=ALU.max)),
                           [self.SELW_T], [self.SELW_T])
                if rnd == 2:
                    break
                eq = wk if rnd == 0 else wk2
                self.tt("dve", eq, cur, mcol.unsqueeze(2).to_broadcast([128, 32, 8]), ALU.is_ge, [self.SELW_T], [self.SELW_T])
                self.stt(eq, eq, -1e30, cur, ALU.mult, ALU.add, [self.SELW_T], [self.SELW_T])
                cur = eq
            lt = W[:, 512:768]
            thr = mxs[:, 64:96].unsqueeze(2).to_broadcast([128, 32, 8])
            self.tt("dve", lt.rearrange("p (k j) -> p k j", j=8), W[:, 0:256].rearrange("p (k j) -> p k j", j=8), thr,
                    ALU.is_lt, [self.SELW_T], [self.SELW_T])
            npb = M[:, NPT0:NPT0 + 128].rearrange("p (t j) -> p t j", t=16).unsqueeze(2).to_broadcast([128, 16, 2, 8])
            sb8 = W[:, 768:1024]
            self.tt("dve", sb8.rearrange("p (t h j) -> p t h j", t=16, h=2),
                    lt.rearrange("p (t h j) -> p t h j", t=16, h=2), npb, ALU.mult, [self.SELW_T, self.MISC_T],
                    [self.SELW_T])
            self.cp("dve", self.SELB[:, :, :, :].rearrange("p t h (j f) -> p (t h) j f", f=4),
                    sb8.rearrange("p (k j) -> p k j", j=8).unsqueeze(3).to_broadcast([128, 32, 8, 4]),
                    [self.SELW_T], [self.SELB_T])
            wz, wzT = self.wload(("cols", "w_in_odd", li, 3072 + base + np.arange(128)))

            def evz(tg, ps3, pT, zoff=zoff):
                self.act(self.ZS[:, tg * 4:(tg + 1) * 4, zoff:zoff + 128], ps3, AF.Silu, [pT], [self.ZS_T])
            self.proj_tm(wz, wzT, evz)
            for h in range(2):
                for g4 in range(4):
                    pm, pmT = self.nextM()
                    for a in range(4):
                        tt_ = g4 * 4 + a
                        self.mm(pm[0:32, a * 128:(a + 1) * 128], self.SELB[:, tt_, h, :], self.IDB, True, True,
                                [self.SELB_T, self.MB_T], [pmT])
                    self.cp("act", self.QTA[64:96, h, g4 * 512:(g4 + 1) * 512], pm[0:32, :], [pmT], [self.Qa[h]])
            self.P.cur_tag = 'L%d.odd.attn' % l
            calls = []
            for h in range(2):
                Kt, KT_ = Ks[h]
                for Q in range(4):
                    def lhs_fn(kt, Kt=Kt, KT_=KT_):
                        return Kt[0:_R('moba', 96), kt * 128:(kt + 1) * 128], [KT_, self.Kaug, self.KBaug]

                    def mask_fn(kt, Q=Q):
                        if kt < 4 * Q:
                            return None
                        return [([[1, 128]], 0, -1)]

                    def pv_fn(kt, h=h):
                        return self.VA[:, kt, h, :], [self.VA_T[h]]

                    def after(po, poT, h=h, Q=Q, zoff=zoff):
                        rc, rcT = self.sml.get()
                        self.recip(rc[:, 0:4], po[:, :, 64], poT, [rcT])
                        t, tT = self.tmp.get()
                        t3 = t[:, 0:256].rearrange("p (a d) -> p a d", a=4)
                        self.tt("dve", t3, po[:, :, 0:64], rc[:, 0:4].unsqueeze(2).to_broadcast([128, 4, 64]), ALU.mult,
                                poT + [rcT], [tT])
                        self.tt("pool", self.OGT[:, Q * 4:(Q + 1) * 4, zoff + h * 64:zoff + (h + 1) * 64], t3,
                                self.ZS[:, Q * 4:(Q + 1) * 4, zoff + h * 64:zoff + (h + 1) * 64], ALU.mult, [tT, self.ZS_T],
                                self.OGT_T[Q * 4:(Q + 1) * 4])
                    calls.append(dict(lhs_fn=lhs_fn, rhs=self.QTA[0:_R('moba', 96), h, Q * 512:(Q + 1) * 512],
                                      rhs_reads=[self.Qh[h], self.Qa[h]], kts=list(range(4 * Q + 4)), mask_fn=mask_fn,
                                      pv_fn=pv_fn, ncols=65, valid=(lambda a, kt, Q=Q: kt <= 4 * Q + a),
                                      c0=(lambda kt, Q=Q: max(0, kt - 4 * Q) * 128),
                                      lastk=(lambda a, Q=Q: 4 * Q + a), after=after))
            self.run_calls(calls)
            if l == 1 and hp == 0:
                self.dbg("o_q", self.QTA[:, :, :], self.Qh + self.Qa, BF16)
                self.dbg("o_ka", self.KA[:, :], [self.KAh], BF16)
                self.dbg("o_kb", self.KB[:, :], [self.KBh], BF16)
                self.dbg("o_w", self.SELW[:, :], [self.SELW_T])
                self.dbg("o_ogt", self.OGT[:, :, :], self.OGT_T, BF16)
                self.dbg("o_va", self.VA[:, :, :, :], self.VA_T, BF16)
                self.dbg("o_zs", self.ZS[:, :, :], [self.ZS_T], BF16)
                self.dbg("o_xt", self.XT[:, :, :], [t for row in self.XT_T for t in row])
            self.P.cur_tag = 'L%d.odd.fin' % l
            if hp % 2 == 1:
                self.finish_pairs(l, [(0, ("rows", "w_out_odd", li, base - 128)), (128, ("rows", "w_out_odd", li, base))])

    def even_layer(self, l):
        li = l // 2
        M = self.MISC
        EO = EVEN_OFF
        self.memset("pool", self.KB[64:96, :], 0.0, [self.KBaug])
        self.make_h(l)
        if l == 0:
            self.dbg("ht", self.HT[:, :, :], self.HT_T, BF16)
        self.P.cur_tag = 'L%d.A.common' % l
        ka = EO["ka"] + np.arange(64)
        cols = np.concatenate([ka, ka])
        wb, wbT = self.wload(("cols", "w_in_even", li, cols))
        ws, wsT = self.wload(("cols", "w_in_even", li, _sw(cols)))
        self.proj_fm(wb, wbT, ws, wsT, [(0, (lambda tb: self.KB[0:64, tb * 512:(tb + 1) * 512]), self.KBh, True)])
        cols = np.concatenate([EO["va"] + np.arange(64), EO["gb"] + np.arange(24)])
        wv, wvT = self.wload(("cols", "w_in_even", li, cols))

        def evv(tg, ps3, pT):
            self.cp("act", self.VAa[:, tg * 4:(tg + 1) * 4, 0:64], ps3[:, :, 0:64], [pT], [self.VAa_T])
            self.act(self.GSG[:, tg * 4:(tg + 1) * 4, :], ps3[:, :, 64:88], AF.Sigmoid, [pT], [self.GSG_T])
        self.proj_tm(wv, wvT, evv)
        for ag in range(2):
            self.P.cur_tag = 'L%d.A.proj' % l
            for pr in range(2):
                self.qproj("w_in_even", li, EO["qa"] + (ag * 2 + pr) * 128, (pr * 2, pr * 2 + 1))
            for pr in range(2):
                wz, wzT = self.wload(("cols", "w_in_even", li, EO["za"] + (ag * 2 + pr) * 128 + np.arange(128)))

                def evz(tg, ps3, pT, pr=pr):
                    self.act(self.ZS[:, tg * 4:(tg + 1) * 4, pr * 128:(pr + 1) * 128], ps3, AF.Silu, [pT], [self.ZS_T])
                self.proj_tm(wz, wzT, evz)
            if l == 0 and ag == 0:
                self.dbg("qa", self.QTA[:, :, :], self.Qh, BF16)
                self.dbg("kb", self.KB[:, :], [self.KBh], BF16)
                self.dbg("vaa", self.VAa[:, :, :], [self.VAa_T], BF16)
                self.dbg("zs", self.ZS[:, :, :], [self.ZS_T], BF16)
                self.dbg("gsg", self.GSG[:, :, :], [self.GSG_T])
            self.P.cur_tag = 'L%d.A.attn' % l
            calls = []
            for i in range(16):
                kts = [i - 1, i] if i > 0 else [0]

                def lhs_fn(kt):
                    return self.KB[0:_R('swa', 64), kt * 128:(kt + 1) * 128], [self.KBh, self.KBaug]

                def mask_fn(kt, i=i):
                    if kt == i:
                        return [([[0, 4], [1, 128]], 0, -1)]
                    return [([[0, 4], [-1, 128]], -1, 1)]

                def pv_fn(kt):
                    return self.VAa[:, kt, :], [self.VAa_T]

                def after(po, poT, i=i, ag=ag):
                    rc, rcT = self.sml.get()
                    self.tt("dve", rc[:, 0:4], po[:, :, 64], self.ESINK[:, li * 8 + ag * 4:li * 8 + ag * 4 + 4], ALU.add,
                            poT + [self.ESINK_T], [rcT])
                    self.recip(rc[:, 4:8], rc[:, 0:4], [rcT], [rcT])
                    t, tT = self.tmp.get()
                    t3 = t[:, 0:256].rearrange("p (a d) -> p a d", a=4)
                    self.tt("dve", t3, po[:, :, 0:64], rc[:, 4:8].unsqueeze(2).to_broadcast([128, 4, 64]), ALU.mult,
                            poT + [rcT], [tT])
                    self.tt("pool", self.OGT[:, i, :], t[:, 0:256], self.ZS[:, i, :], ALU.mult, [tT, self.ZS_T],
                            [self.OGT_T[i]])
                calls.append(dict(lhs_fn=lhs_fn, rhs=self.QTA[0:_R('swa', 64), :, i * 128:(i + 1) * 128], rhs_reads=self.Qh + self.Qa, kts=kts,
                                  mask_fn=mask_fn, pv_fn=pv_fn, ncols=65, valid=(lambda a, kt: True),
                                  lastk=(lambda a, i=i: i), after=after))
            self.run_calls(calls)
            if l == 0:
                self.dbg("ogt_a%d" % ag, self.OGT[:, :, :], self.OGT_T, BF16)
            self.P.cur_tag = 'L%d.A.fin' % l
            self.finish_pairs(l, [(pr * 128, ("rows", "w_out_even", li, (ag * 2 + pr) * 128)) for pr in range(2)])
            if l == 0 and ag == 0:
                self.dbg("xt_a0", self.XT[:, :, :], [t for row in self.XT_T for t in row])
        for g in range(2):
            self.P.cur_tag = 'L%d.B.proj' % l
            for pr in range(2):
                self.qproj("w_in_even", li, EO["qb"] + (g * 2 + pr) * 128, (pr * 2, pr * 2 + 1))
            kc = EO["kc"] + g * 64 + np.arange(64)
            vc = EO["vc"] + g * 64 + np.arange(64)
            wb, wbT = self.wload(("cols", "w_in_even", li, np.concatenate([kc, vc])))
            ws, wsT = self.wload(("cols", "w_in_even", li, np.concatenate([_sw(kc), _sw(kc)])))
            self.proj_fm(wb, wbT, ws, wsT, [
                (0, (lambda tb: self.KA[0:64, tb * 512:(tb + 1) * 512]), self.KAh, True),
                (64, (lambda tb: self.KB[0:64, tb * 512:(tb + 1) * 512]), self.KBh, False)])
            self.P.cur_tag = 'L%d.B.cmpmlp' % l
            for kv in range(2):
                SRC, SRC_T = (self.KA, self.KAh) if kv == 0 else (self.KB, self.KBh)
                pe = self.MB[0:64, 832 + li * 64 + kv * 32:832 + li * 64 + kv * 32 + 32]
                w2 = self.MB[:, 320 + li * 256 + kv * 128:320 + li * 256 + (kv + 1) * 128]
                ph = [self.pM[0], self.pM[1]]
                pbis = [self.pS[0], self.pS[1]]
                for lc in range(8):
                    w1, w1T = self.wload(("w1", "cmp_w1_k" if kv == 0 else "cmp_w1_v", li, lc * 4), nparts=64)
                    for ll in range(4):
                        lidx = lc * 4 + ll
                        for hc in range(2):
                            lhsT = w1[0:64, ll * 256 + hc * 128:ll * 256 + (hc + 1) * 128]
                            self.mm(ph[hc][0][:, 0:127], lhsT, SRC[0:64, lidx:lidx + 16 * 126 + 1:16], lidx == 0, lidx == 31,
                                    [w1T, SRC_T], [ph[hc][1]])
                            self.mm(pbis[hc][0][:, 0:1], lhsT, pe[:, lidx:lidx + 1], lidx == 0, lidx == 31,
                                    [w1T, self.MB_T], [pbis[hc][1]])
                bi, biT = self.sml.get()
                for hc in range(2):
                    self.cp("dve", bi[:, 2 * hc:2 * hc + 1], pbis[hc][0][:, 0:1], [pbis[hc][1]], [biT])
                for hc in range(2):
                    xx, xT = self.tmp.get()
                    self.act(xx[:, 0:127], ph[hc][0][:, 0:127], AF.Identity, [ph[hc][1], biT], [xT],
                             bias=bi[:, 2 * hc:2 * hc + 1])
                    u, uT = self.tmp.get()
                    self.tt("dve", u[:, 0:127], xx[:, 0:127], xx[:, 0:127], ALU.mult, [xT], [uT])
                    self.ts("dve", u[:, 0:127], u[:, 0:127], 0.044715, 1.0, ALU.mult, ALU.add, [uT], [uT])
                    self.tt("dve", u[:, 0:127], u[:, 0:127], xx[:, 0:127], ALU.mult, [uT, xT], [uT])
                    self.act(u[:, 0:127], u[:, 0:127], AF.Sigmoid, [uT], [uT], scale=1.5957691216057308)
                    self.tt("dve", self.HID[:, kv, hc, 0:127], u[:, 0:127], xx[:, 0:127], ALU.mult, [uT, xT],
                            [self.HID_T[kv]])
                pm, pmT = self.nextM()
                if kv == 0:
                    for hc in range(2):
                        self.mm(pm[0:64, 0:127], w2[:, hc * 64:(hc + 1) * 64], self.HID[:, 0, hc, 0:127], hc == 0, hc == 1,
                                [self.MB_T, self.HID_T[0]], [pmT])
                    self.cp("act", self.KCMP[0:64, 0:127], pm[0:64, 0:127], [pmT], [self.KCMP_T])
                else:
                    for hc in range(2):
                        self.mm(pm[0:127, 0:64], self.HID[:, 1, hc, 0:127], w2[:, hc * 64:(hc + 1) * 64], hc == 0, hc == 1,
                                [self.MB_T, self.HID_T[1]], [pmT])
                    self.cp("act", self.VCX[0:127, 0:64], pm[0:127, 0:64], [pmT], [self.VCX_T])
            self.P.cur_tag = 'L%d.B.proj2' % l
            for pr in range(2):
                wz, wzT = self.wload(("cols", "w_in_even", li, EO["zb"] + (g * 2 + pr) * 128 + np.arange(128)))

                def evz(tg, ps3, pT, pr=pr):
                    self.act(self.ZS[:, tg * 4:(tg + 1) * 4, pr * 128:(pr + 1) * 128], ps3, AF.Silu, [pT], [self.ZS_T])
                self.proj_tm(wz, wzT, evz)
            cols = np.concatenate([EO["vs"] + g * 64 + np.arange(64), EO["vw"] + g * 64 + np.arange(64)])
            wv, wvT = self.wload(("cols", "w_in_even", li, cols))

            def evv2(tg, ps3, pT):
                for h in range(2):
                    self.cp("act", self.VA[:, tg * 4:(tg + 1) * 4, h, 0:64], ps3[:, :, h * 64:(h + 1) * 64], [pT],
                            [self.VA_T[h]])
            self.proj_tm(wv, wvT, evv2)
            self.P.cur_tag = 'L%d.B.cmpattn' % l
            W = self.SELW
            calls = []
            for i in range(16):
                def lhs_fn(kt):
                    return self.KCMP[0:_R('cmp', 64), 0:127], [self.KCMP_T]

                def mask_fn(kt, i=i):
                    return [([[0, 4], [1, 128]], 128 * i - 31, -16)]

                def pv_fn(kt):
                    return self.VCX[0:127, :], [self.VCX_T]

                def after(po, poT, i=i, g=g):
                    rc, rcT = self.sml.get()
                    self.ts("dve", rc[:, 0:4], po[:, :, 64], 1e-30, None, ALU.max, None, poT, [rcT])
                    self.recip(rc[:, 4:8], rc[:, 0:4], [rcT], [rcT])
                    imp = W[:, 0:32]
                    for r in range(4):
                        if r == 0:
                            self.ts("dve", imp, po[:, 0, 65:97], rc[:, 4:5], None, ALU.mult, None, [poT[0], rcT],
                                    [self.SELW_T])
                        else:
                            self.stt(imp, po[:, r, 65:97], rc[:, 4 + r:5 + r], imp, ALU.mult, ALU.add,
                                     [poT[r], rcT, self.SELW_T], [self.SELW_T])
                    gsl = self.GSG[:, i, :].rearrange("p (h b) -> p h b", b=3)[:, g * 4:(g + 1) * 4, 0]
                    self.tt("dve", rc[:, 8:12], rc[:, 4:8], gsl, ALU.mult, [rcT, self.GSG_T], [rcT])
                    self.tt("dve", self.OGT[:, i, :].rearrange("p (a d) -> p a d", a=4), po[:, :, 0:64],
                            rc[:, 8:12].unsqueeze(2).to_broadcast([128, 4, 64]), ALU.mult, poT + [rcT], [self.OGT_T[i]])
                    self.tt("dve", W[:, 32:64], imp, M[:, FT0 + i * 32:FT0 + (i + 1) * 32], ALU.max,
                            [self.SELW_T, self.MISC_T], [self.SELW_T])
                    self.tt("dve", W[:, 64:96], W[:, 32:64], M[:, VT0 + i * 32:VT0 + (i + 1) * 32], ALU.min,
                            [self.SELW_T, self.MISC_T], [self.SELW_T])
                    self.P.add("dve", lambda e: e.max(out=W[:, 96:104], in_=W[:, 64:96]), [self.SELW_T], [self.SELW_T])
                    self.ts("dve", self.SELB[:, i, 0, :], W[:, 64:96], W[:, 103:104], NEG, ALU.is_lt, ALU.mult,
                            [self.SELW_T], [self.SELB_T])
                calls.append(dict(lhs_fn=lhs_fn, rhs=self.QTA[0:_R('cmp', 64), :, i * 128:(i + 1) * 128], rhs_reads=self.Qh + self.Qa, kts=[0],
                                  mask_fn=mask_fn, pv_fn=pv_fn, ncols=97, valid=(lambda a, kt: True),
                                  lastk=(lambda a: 0), kparts=127, after=after))
            self.run_calls(calls)
            for g4 in range(4):
                pm, pmT = self.nextM()
                for a in range(4):
                    tt_ = g4 * 4 + a
                    self.mm(pm[0:32, a * 128:(a + 1) * 128], self.SELB[:, tt_, 0, :], self.IDB, True, True,
                            [self.SELB_T, self.MB_T], [pmT])
                self.cp("act", self.QTA[64:96, :, g4 * 512:(g4 + 1) * 512],
                        pm[0:32, :].unsqueeze(1).to_broadcast([32, 4, 512]), [pmT], self.Qa)
            if l == 0 and g == 0:
                self.dbg("kcmp", self.KCMP[:, :], [self.KCMP_T], BF16)
                self.dbg("vcx", self.VCX[:, :], [self.VCX_T], BF16)
                self.dbg("ogt_cmp", self.OGT[:, :, :], self.OGT_T, BF16)
                self.dbg("selb", self.SELB[:, :, :, :], [self.SELB_T], BF16)
                self.dbg("qaug", self.QTA[:, :, :], self.Qh + self.Qa, BF16)
            self.P.cur_tag = 'L%d.B.kskw' % l
            ks = EO["ks"] + g * 64 + np.arange(64)
            kw = EO["kw"] + g * 64 + np.arange(64)
            cols = np.concatenate([ks, kw])
            wb, wbT = self.wload(("cols", "w_in_even", li, cols))
            ws, wsT = self.wload(("cols", "w_in_even", li, _sw(cols)))
            self.proj_fm(wb, wbT, ws, wsT, [
                (0, (lambda tb: self.KA[0:64, tb * 512:(tb + 1) * 512]), self.KAh, True),
                (64, (lambda tb: self.KB[0:64, tb * 512:(tb + 1) * 512]), self.KBh, True)])
            self.P.cur_tag = 'L%d.B.slcwin' % l
            calls = []
            for i in range(16):
                for br in (1, 2):
                    if br == 1:
                        kts = list(range(i + 1))

                        def lhs_fn(kt):
                            return self.KA[0:_R('slc', 96), kt * 128:(kt + 1) * 128], [self.KAh, self.Kaug]
                        rhs = self.QTA[0:_R('slc', 96), :, i * 128:(i + 1) * 128]
                        rreads = self.Qh + self.Qa

                        def mask_fn(kt, i=i):
                            if kt == i:
                                return [([[0, 4], [1, 128]], 0, -1)]
                            return None

                        def pv_fn(kt):
                            return self.VA[:, kt, 0, :], [self.VA_T[0]]
                    else:
                        kts = list(range(max(0, i - 4), i + 1))

                        def lhs_fn(kt):
                            return self.KB[0:_R('win', 64), kt * 128:(kt + 1) * 128], [self.KBh, self.KBaug]
                        rhs = self.QTA[0:_R('win', 64), :, i * 128:(i + 1) * 128]
                        rreads = self.Qh + self.Qa

                        def mask_fn(kt, i=i):
                            if kt == i:
                                return [([[0, 4], [1, 128]], 0, -1)]
                            if kt == i - 4:
                                return [([[0, 4], [-1, 128]], -1, 1)]
                            return None

                        def pv_fn(kt):
                            return self.VA[:, kt, 1, :], [self.VA_T[1]]

                    def after(po, poT, i=i, br=br, g=g):
                        rc, rcT = self.sml.get()
                        self.recip(rc[:, 0:4], po[:, :, 64], poT, [rcT])
                        gsl = self.GSG[:, i, :].rearrange("p (h b) -> p h b", b=3)[:, g * 4:(g + 1) * 4, br]
                        self.tt("dve", rc[:, 4:8], rc[:, 0:4], gsl, ALU.mult, [rcT, self.GSG_T], [rcT])
                        t, tT = self.tmp.get()
                        self.tt("dve", t[:, 0:256].rearrange("p (a d) -> p a d", a=4), po[:, :, 0:64],
                                rc[:, 4:8].unsqueeze(2).to_broadcast([128, 4, 64]), ALU.mult, poT + [rcT], [tT])
                        self.tt("pool", self.OGT[:, i, :], t[:, 0:256], self.OGT[:, i, :], ALU.add, [tT, self.OGT_T[i]],
                                [self.OGT_T[i]])
                        if br == 2:
                            self.tt("pool", self.OGT[:, i, :], self.OGT[:, i, :], self.ZS[:, i, :], ALU.mult,
                                    [self.OGT_T[i], self.ZS_T], [self.OGT_T[i]])
                    calls.append(dict(lhs_fn=lhs_fn, rhs=rhs, rhs_reads=rreads, kts=kts, mask_fn=mask_fn, pv_fn=pv_fn,
                                      ncols=65, valid=(lambda a, kt: True), lastk=(lambda a, i=i: i), after=after))
            self.run_calls(calls)
            if l == 0:
                self.dbg("ogt_b%d" % g, self.OGT[:, :, :], self.OGT_T, BF16)
            self.P.cur_tag = 'L%d.B.fin' % l
            self.finish_pairs(l, [(pr * 128, ("rows", "w_out_even", li, 512 + (g * 2 + pr) * 128)) for pr in range(2)])

    def build(self):
        self.phase0()
        for l in range(self.depth):
            if l % 2 == 0:
                self.even_layer(l)
            else:
                self.odd_layer(l)
        self.final()
        self.P.finalize()
        return self.nc


def _rope_tables():
    inv = (np.float32(10000.0) ** (-np.arange(0, 64, 2, dtype=np.float32) / np.float32(64))).astype(np.float32)
    ang = (np.arange(S, dtype=np.float32)[:, None] * inv[None, :]).astype(np.float32)
    cos = np.cos(ang).astype(np.float32).T
    sin = np.sin(ang).astype(np.float32).T
    CC = np.concatenate([cos] * 4, axis=0)
    SS = np.concatenate([-sin, sin, -sin, sin], axis=0)
    return np.ascontiguousarray(np.concatenate([CC, SS], axis=1).astype(np.float32))


def _static_misc():
    m = np.zeros((128, MISC_W), np.float32)
    m[:, IDF0:IDF0 + 128] = np.eye(128, dtype=np.float32)
    pb = np.zeros((16, 8), np.float32)
    npb = np.zeros((16, 8), np.float32)
    for i in range(16):
        for j in range(8):
            if j < i // 2:
                npb[i, j] = NEG
            else:
                pb[i, j] = -1e30
    m[:, PBT0:PBT0 + 128] = pb.reshape(1, 128)
    m[:, NPT0:NPT0 + 128] = npb.reshape(1, 128)
    t = np.arange(S)
    tb = t // 64
    jj = np.arange(32)
    valid = jj[None, :] <= tb[:, None]
    forced = (jj[None, :] == 0) | (jj[None, :] == tb[:, None]) | (jj[None, :] == tb[:, None] - 1)
    Ft = np.where(forced, np.float32(1e4), np.float32(0.0)).astype(np.float32)
    Vt = np.where(valid, np.float32(1e30), np.float32(-1e30)).astype(np.float32)
    m[:, FT0:FT0 + 512] = Ft.reshape(16, 128, 32).transpose(1, 0, 2).reshape(128, 512)
    m[:, VT0:VT0 + 512] = Vt.reshape(16, 128, 32).transpose(1, 0, 2).reshape(128, 512)
    ncmp = 127
    cst = np.arange(ncmp) * 16
    ov = ((cst[:, None] < (jj[None, :] + 1) * 64) & (cst[:, None] + 32 > jj[None, :] * 64)).astype(np.float32)
    m2 = np.zeros((128, 1024), np.float32)
    m2[0:127, OV0] = 1.0
    m2[0:127, OV0 + 1:OV0 + 33] = ov
    return m, m2


def _e32():
    e = np.zeros((128, S), np.float32)
    k = np.arange(S)
    for j in range(32):
        e[64 + j, :] = (k // 64 == j)
    return e


def _build_wblk(descs, inputs):
    wb = np.zeros((NB, 128, 1024), np.float32)
    for n, d in enumerate(descs):
        kind = d[0]
        if kind == "ada":
            _, l, nb = d
            W = inputs["w_ada"][l][:, nb * 128:(nb + 1) * 128]
            wb[n] = W.reshape(8, 128, 128).transpose(1, 0, 2).reshape(128, 1024)
        elif kind == "cols":
            _, name, li, cols = d
            W = inputs[name][li][:, cols]
            k = len(cols)
            blk = wb[n].reshape(128, 8, 128)
            blk[:, :, 0:k] = W.reshape(8, 128, k).transpose(1, 0, 2)
        elif kind == "rows":
            _, name, li, r0 = d
            wb[n] = inputs[name][li][r0:r0 + 128, :]
        elif kind == "w1":
            _, name, li, l0 = d
            W = inputs[name][li][l0 * 64:(l0 + 4) * 64, :]
            wb[n, 0:64, :] = W.reshape(4, 64, 256).transpose(1, 0, 2).reshape(64, 1024)
    return wb


_CACHE = {}


def kernel(**inputs):
    inputs = {k: np.asarray(v) for k, v in inputs.items()}
    if "b" not in _CACHE:
        b = Builder(4)
        nc = b.build()
        assert len(b.descs) == NB, (len(b.descs), NB)
        _CACHE["b"] = (b, nc)
    b, nc = _CACHE["b"]
    wblk = _build_wblk(b.descs, inputs)
    cs = _rope_tables()
    e32 = _e32()
    base, m2 = _static_misc()
    base[:, NG0:NG0 + 32] = inputs["norm_g"].reshape(4, 8, 128).transpose(2, 0, 1).reshape(128, 32)
    base[:, FG0:FG0 + 8] = inputs["final_g"].reshape(8, 128).T
    base[:, BADA0:BADA0 + 96] = inputs["b_ada"].reshape(4, 24, 128).transpose(2, 0, 1).reshape(128, 96)
    base[:, SINK0:SINK0 + 16] = inputs["a_sinks"].reshape(1, 16)
    for i in range(2):
        m2[:, W20 + i * 256:W20 + i * 256 + 128] = inputs["cmp_w2_k"][i].reshape(2, 128, 64).transpose(1, 0, 2).reshape(128, 128)
        m2[:, W20 + i * 256 + 128:W20 + (i + 1) * 256] = inputs["cmp_w2_v"][i].reshape(2, 128, 64).transpose(1, 0, 2).reshape(128, 128)
        m2[0:64, PE0 + i * 64:PE0 + i * 64 + 32] = inputs["cmp_pe_k"][i].T
        m2[0:64, PE0 + i * 64 + 32:PE0 + (i + 1) * 64] = inputs["cmp_pe_v"][i].T
    in_maps = []
    for core in range(8):
        m = base.copy()
        m[:, CV0:CV0 + 8] = inputs["c"][core].reshape(8, 128).T
        in_maps.append({"x": np.ascontiguousarray(inputs["x"][core]), "misc": m, "misc2": m2, "cs": cs, "e32": e32, "wblk": wblk})
    res = run_bass_kernel_spmd(nc, in_maps, core_ids=list(range(8)))
    return np.stack([np.asarray(r["out"], dtype=np.float32) for r in res.results], axis=0)
```

```python
import os
import numpy as np
from contextlib import ExitStack
import concourse.bass as bass
import concourse.mybir as mybir
from concourse.bass_utils import run_bass_kernel_spmd

F32 = mybir.dt.float32
BF16 = mybir.dt.bfloat16
ALU = mybir.AluOpType
AF = mybir.ActivationFunctionType
AX = mybir.AxisListType

COMPUTE = ("pe", "act", "dve", "pool")
S = 2048
D = 1024
_PADS = os.environ.get('K_PAD', 'swa,cmp,slc,win,moba,gs').split(',')
def _R(site, small):
    return 128 if site in _PADS else small
NEG = -30000.0


class T:
    __slots__ = ("name", "w", "r", "rd")

    def __init__(self, name):
        self.name = name
        self.w = None
        self.r = {}
        self.rd = []


class Op:
    __slots__ = ("eng", "fn", "reads", "writes", "dma", "idx", "deps", "signal",
                 "count", "waits", "dma_count", "tag", "iname")

    def __init__(self, eng, fn, reads, writes, dma):
        self.eng = eng
        self.fn = fn
        self.reads = reads
        self.writes = writes
        self.dma = dma
        self.signal = False
        self.count = 0
        self.waits = []
        self.dma_count = 0


class Prog:
    def __init__(self, nc):
        self.nc = nc
        self.ops = []
        self.es = ExitStack()

    def sb(self, name, shape, dt):
        return self.es.enter_context(self.nc.sbuf_tensor(name, list(shape), dt))

    def ps(self, name, shape, dt=F32):
        return self.es.enter_context(self.nc.psum_tensor(name, list(shape), dt))

    def add(self, eng, fn, reads=(), writes=(), dma=None):
        op = Op(eng, fn, list(reads), list(writes), dma)
        op.tag = getattr(self, "cur_tag", "")
        op.iname = None
        op.idx = len(self.ops)
        self.ops.append(op)
        return op

    def finalize(self):
        nc = self.nc
        ops = self.ops
        for i, op in enumerate(ops):
            deps = set()
            for t in op.reads:
                if t.w is not None:
                    deps.add(t.w)
            for t in op.writes:
                if t.w is not None:
                    deps.add(t.w)
                deps.update(t.r.values())
                deps.update(t.rd)
            for t in op.reads:
                if op.dma is not None:
                    t.rd.append(i)
                else:
                    t.r[op.eng] = i
            for t in op.writes:
                t.w = i
                t.r = {}
                t.rd = []
            deps.discard(i)
            op.deps = sorted(deps)
        pos = {}
        cnt = {e: 0 for e in COMPUTE}
        for op in ops:
            if op.dma is None and op.eng in COMPUTE:
                pos[op.idx] = cnt[op.eng]
                cnt[op.eng] += 1
        need = []
        for op in ops:
            lst = []
            for d in op.deps:
                dop = ops[d]
                if dop.dma is not None:
                    lst.append(("dma", d))
                    continue
                if dop.eng == op.eng and op.dma is None:
                    if op.eng == "pe":
                        continue
                dop.signal = True
                lst.append(("eng", d))
            need.append(lst)
        ecount = {e: 0 for e in COMPUTE}
        dcount = {}
        for op in ops:
            if op.dma is not None:
                dcount[op.dma] = dcount.get(op.dma, 0) + 16
                op.dma_count = dcount[op.dma]
            elif op.signal:
                ecount[op.eng] += 1
                op.count = ecount[op.eng]
        self.ecount = ecount
        SEM_LIM = int(os.environ.get("K_SEMLIM", "100000"))
        esem = {}
        for e in COMPUTE:
            nep = ecount[e] // SEM_LIM + 1
            esem[e] = [self.es.enter_context(nc.semaphore("s_%s%d" % (e, k))) for k in range(nep)]
        dsem = {}
        for k in dcount:
            dsem[k] = self.es.enter_context(nc.semaphore("d_%s" % (k,)))
        waited = {}
        dcur = {}
        streams = {}
        for op in ops:
            w = waited.setdefault(op.eng, {})
            waits = []
            for kind, d in need[op.idx]:
                dop = ops[d]
                if kind == "dma":
                    key = ("d", dop.dma)
                    val = dcur[dop.dma]
                    sem = dsem[dop.dma]
                else:
                    key = ("e", dop.eng)
                    val = dop.count
                    ep = (val - 1) // SEM_LIM
                    sem = esem[dop.eng][ep]
                if w.get(key, 0) >= val:
                    continue
                w[key] = val
                if kind != "dma":
                    val = val - ep * SEM_LIM
                waits.append((sem, val))
            op.waits = waits
            if op.dma is not None:
                dcur[op.dma] = op.dma_count
            streams.setdefault(op.eng, []).append(op)
        self.streams = streams

        def emit(engine, lst):
            for op in lst:
                for sem, val in op.waits:
                    engine.wait_ge(sem, val)
                if op.fn is None:
                    continue
                ins = op.fn(engine)
                try:
                    op.iname = ins.ins.name
                except Exception:
                    pass
                if op.dma is not None:
                    ins.then_inc(dsem[op.dma], 16)
                elif op.signal:
                    ins.then_inc(esem[op.eng][(op.count - 1) // SEM_LIM], 1)

        with nc.Block() as block:
            if "sp" in streams:
                @block.sync
                def _(e):
                    emit(e, streams["sp"])
            if "pe" in streams:
                @block.tensor
                def _(e):
                    emit(e, streams["pe"])
            if "act" in streams:
                @block.scalar
                def _(e):
                    emit(e, streams["act"])
            if "dve" in streams:
                @block.vector
                def _(e):
                    emit(e, streams["dve"])
            if "pool" in streams:
                @block.gpsimd
                def _(e):
                    emit(e, streams["pool"])
        self.es.close()


class Rot:
    def __init__(self, P, name, n, shape, dt):
        self.bufs = [(P.sb("%s%d" % (name, i), shape, dt), T("%s%d" % (name, i))) for i in range(n)]
        self.i = 0

    def get(self):
        b = self.bufs[self.i % len(self.bufs)]
        self.i += 1
        return b


CV0, BADA0, NG0, FG0, SINK0, IDF0, PBT0, NPT0, FT0, VT0 = (
    0, 8, 104, 136, 144, 160, 288, 416, 544, 1056)
MISC_W = 1568
OV0, W20, PE0 = 0, 64, 576
NB = 96 + 2 * 56 + 2 * 77

A_QW, A_KVW = 512, 64
EVEN_OFF = {}
_o = 0
for _n, _s in (("qa", 512), ("ka", 64), ("va", 64), ("za", 512), ("qb", 512), ("kc", 128), ("vc", 128),
               ("ks", 128), ("vs", 128), ("kw", 128), ("vw", 128), ("gb", 24), ("zb", 512)):
    EVEN_OFF[_n] = _o
    _o += _s


def _sw(cols):
    cols = np.asarray(cols).reshape(-1, 2, 32)
    return cols[:, ::-1, :].reshape(-1)


class Builder:
    def __init__(self, depth=4):
        self.depth = depth
        self.nc = nc = bass.Bass("TRN2", target_bir_lowering=False)
        self.P = P = Prog(nc)
        self.descs = []
        self.dbg_names = []
        self.x_d = nc.dram_tensor("x", [S, D], F32, kind="ExternalInput").ap()
        self.misc_d = nc.dram_tensor("misc", [128, MISC_W], F32, kind="ExternalInput").ap()
        self.misc2_d = nc.dram_tensor("misc2", [128, 1024], F32, kind="ExternalInput").ap()
        self.cs_d = nc.dram_tensor("cs", [128, 2 * S], F32, kind="ExternalInput").ap()
        self.e32_d = nc.dram_tensor("e32", [128, S], F32, kind="ExternalInput").ap()
        self.wblk_d = nc.dram_tensor("wblk", [NB, 128, 1024], F32, kind="ExternalInput").ap()
        self.out_d = nc.dram_tensor("out", [S, D], F32, kind="ExternalOutput").ap()
        self.XT = P.sb("XT", [128, 8, S], F32)
        self.XT_T = [[T("xt%d_%d" % (c, tb)) for tb in range(4)] for c in range(8)]
        self.HT = P.sb("HT", [128, 8, S], BF16)
        self.HT_T = [T("ht%d" % tb) for tb in range(4)]
        self.QTA = P.sb("QTA", [128, 4, S], BF16)
        self.Qh = [T("qh%d" % r) for r in range(4)]
        self.Qa = [T("qa%d" % r) for r in range(4)]
        self.KA = P.sb("KA", [128, S], BF16)
        self.KB = P.sb("KB", [128, S], BF16)
        self.KAh, self.KBh, self.Kaug, self.KBaug = T("kah"), T("kbh"), T("kaug"), T("kbaug")
        self.VA = P.sb("VA", [128, 16, 2, 65], BF16)
        self.VA_T = [T("va0"), T("va1")]
        self.VAa = P.sb("VAa", [128, 16, 65], BF16)
        self.VAa_T = T("vaa")
        self.ZS = P.sb("ZS", [128, 16, 256], BF16)
        self.ZS_T = T("zs")
        self.GSG = P.sb("GSG", [128, 16, 24], F32)
        self.GSG_T = T("gsg")
        self.OGT = P.sb("OGT", [128, 16, 256], BF16)
        self.OGT_T = [T("ogt%d" % t) for t in range(16)]
        self.OGTT = P.sb("OGTT", [128, S], BF16)
        self.OGTT_T = T("ogtt")
        self.CS = P.sb("CS", [128, 2 * S], F32)
        self.CS_T = T("cs")
        self.MISC = P.sb("MISC", [128, MISC_W], F32)
        self.MISC_T = T("misc")
        self.MB = P.sb("MB", [128, 1024], BF16)
        self.MB_T = T("mb")
        self.ADA = P.sb("ADA", [128, 96], F32)
        self.GSC = P.sb("GSC", [128, 32], F32)
        self.ADA_T = T("ada")
        self.CACT = P.sb("CACT", [128, 8, 2], BF16)
        self.CACT_T = T("cact")
        self.ESINK = P.sb("ESINK", [128, 16], F32)
        self.ESINK_T = T("esink")
        self.KCMP = P.sb("KCMP", [128, 128], BF16)
        self.KCMP_T = T("kcmp")
        self.VCX = P.sb("VCX", [128, 97], BF16)
        self.VCX_T = T("vcx")
        self.HID = P.sb("HID", [128, 2, 2, 128], BF16)
        self.HID_T = [T("hidk"), T("hidv")]
        self.KM = P.sb("KM", [128, 2, 8], BF16)
        self.KM_T = T("km")
        self.SELW = P.sb("SELW", [128, 1024], F32)
        self.SELW_T = T("selw")
        self.SELB = P.sb("SELB", [128, 16, 2, 32], BF16)
        self.SELB_T = T("selb")
        self.stage = Rot(P, "st", 2, [128, 1024], F32)
        self.wbp = Rot(P, "wb", 3, [128, 1024], BF16)
        self.ptp = Rot(P, "pt", 3, [128, 512], BF16)
        self.tmp = Rot(P, "tmp", 3, [128, 512], F32)
        self.rsp = Rot(P, "rs", 2, [128, 512], F32)
        self.sml = Rot(P, "sml", 4, [128, 16], F32)
        self.pS = [(P.ps("pS%d" % i, [128, 512]), T("pS%d" % i)) for i in range(2)]
        self.pO = P.ps("pO", [128, 4, 512])
        self.pO_T2 = [[T("pO%d_%d" % (st_, i)) for i in range(4)] for st_ in range(2)]
        self.ocount = 0
        self.pM = [(P.ps("pM%d" % i, [128, 512]), T("pM%d" % i)) for i in range(2)]
        self.si = 0
        self.mi = 0

    def nextS(self):
        b = self.pS[self.si % 2]
        self.si += 1
        return b

    def nextM(self):
        banks = self.pM + self.pS
        b = banks[self.mi % 4]
        self.mi += 1
        return b

    def mm(self, out, lhsT, rhs, start, stop, reads, writes):
        self.P.add("pe", lambda e: e.matmul(out, lhsT=lhsT, rhs=rhs, start=start, stop=stop), reads, writes)

    def trf(self, out, in_, ident, reads, writes):
        self.P.add("pe", lambda e: e.transpose(out, in_, ident), reads, writes)

    def act(self, out, in_, func, reads, writes, **kw):
        self.P.add("act", lambda e: e.activation(out=out, in_=in_, func=func, **kw), reads, writes)

    def tt(self, eng, out, in0, in1, op, reads, writes):
        self.P.add(eng, lambda e: e.tensor_tensor(out=out, in0=in0, in1=in1, op=op), reads, writes)

    def ts(self, eng, out, in0, s1, s2, op0, op1, reads, writes):
        if s2 is None:
            self.P.add(eng, lambda e: e.tensor_scalar(out=out, in0=in0, scalar1=s1, scalar2=None, op0=op0), reads, writes)
        else:
            self.P.add(eng, lambda e: e.tensor_scalar(out=out, in0=in0, scalar1=s1, scalar2=s2, op0=op0, op1=op1), reads, writes)

    def stt(self, out, in0, scalar, in1, op0, op1, reads, writes):
        self.P.add("dve", lambda e: e.scalar_tensor_tensor(out=out, in0=in0, scalar=scalar, in1=in1, op0=op0, op1=op1), reads, writes)

    def cp(self, eng, out, in_, reads, writes):
        if eng == "act":
            self.P.add("act", lambda e: e.activation(out=out, in_=in_, func=AF.Copy), reads, writes)
        else:
            self.P.add(eng, lambda e: e.tensor_copy(out=out, in_=in_), reads, writes)

    def recip(self, out, in_, reads, writes):
        self.P.add("dve", lambda e: e.reciprocal(out=out, in_=in_), reads, writes)

    def memset(self, eng, ap, val, writes):
        self.P.add(eng, lambda e: e.memset(ap, val), (), writes)

    def dma(self, out, in_, reads, writes, key):
        self.P.add("sp", lambda e: e.dma_start(out=out, in_=in_), reads, writes, dma=key)

    def sel(self, ap, pattern, base, cm, rw):
        self.P.add("pool", lambda e: e.affine_select(out=ap, in_=ap, pattern=pattern, compare_op=ALU.is_ge,
                                                     fill=0.0, base=base, channel_multiplier=cm), rw, rw)

    def dbg(self, name, ap, reads, dt=F32):
        if not getattr(self, "debug", False):
            return
        shp = list(ap.shape)
        d = self.nc.dram_tensor("dbg_" + name, shp, dt, kind="ExternalOutput").ap()
        self.dma(d, ap, reads, [], key="dbg_" + name)
        self.dbg_names.append("dbg_" + name)

    def wload(self, desc, cast=True, nparts=128, cast_eng=None):
        idx = len(self.descs)
        self.descs.append(desc)
        st, stT = self.stage.get()
        self.dma(st[0:nparts, :], self.wblk_d[idx, 0:nparts, :], [], [stT], key=stT.name)
        if not cast:
            return st, stT
        wb, wbT = self.wbp.get()
        self.cp(cast_eng or os.environ.get("K_CAST", "act"), wb[0:nparts, :], st[0:nparts, :], [stT], [wbT])
        return wb, wbT

    def phase0(self):
        P = self.P
        P.cur_tag = 'phase0'
        self.dma(self.MISC[:, :], self.misc_d, [], [self.MISC_T], key="misc")
        self.dma(self.CS[:, :], self.cs_d, [], [self.CS_T], key="cs")
        M = self.MISC
        self.memset("pool", self.KA[64:128, :], 0.0, [self.Kaug])
        self.memset("pool", self.KB[64:128, :], 0.0, [self.KBaug])
        self.memset("pool", self.QTA[64:128, :, :], 0.0, self.Qa)
        self.memset("pool", self.KM[:, :, :], 0.0, [self.KM_T])
        for half in range(2):
            st, stT = self.stage.get()
            self.dma(st[64:96, :], self.e32_d[64:96, half * 1024:(half + 1) * 1024], [], [stT], key=stT.name)
            self.cp("pool", self.KA[64:96, half * 1024:(half + 1) * 1024], st[64:96, :], [stT], [self.Kaug])
        MB = self.MB
        self.cp("dve", MB[:, 0:128], M[:, IDF0:IDF0 + 128], [self.MISC_T], [self.MB_T])
        self.memset("dve", MB[:, 128:256], 1.0, [self.MB_T])
        st, stT = self.stage.get()
        self.dma(st[:, :], self.misc2_d, [], [stT], key=stT.name)
        self.cp("dve", MB[:, 256:289], st[:, OV0:OV0 + 33], [stT], [self.MB_T])
        self.cp("dve", MB[:, 320:832], st[:, W20:W20 + 512], [stT], [self.MB_T])
        self.cp("dve", MB[:, 832:960], st[:, PE0:PE0 + 128], [stT], [self.MB_T])
        self.IDB = MB[:, 0:128]
        self.ONESB = MB[:, 128:256]
        self.memset("pool", self.VA[:, :, :, 64:65], 1.0, self.VA_T)
        self.memset("pool", self.VAa[:, :, 64:65], 1.0, [self.VAa_T])
        self.memset("pool", self.KCMP[:, :], 0.0, [self.KCMP_T])
        self.memset("pool", self.VCX[:, :], 0.0, [self.VCX_T])
        self.cp("pool", self.VCX[:, 64:97], MB[:, 256:289], [self.MB_T], [self.VCX_T])
        self.act(self.ESINK[:, :], M[:, SINK0:SINK0 + 16], AF.Exp, [self.MISC_T], [self.ESINK_T])
        for j in range(2):
            self.act(self.CACT[:, :, j], M[:, CV0:CV0 + 8], AF.Silu, [self.MISC_T], [self.CACT_T])
        pm, pmT = self.nextM()
        for l in range(self.depth):
            for nb in range(24):
                st, stT = self.wload(("ada", l, nb), cast=True, cast_eng=("act" if nb % 2 == 0 else "dve"))
                col = 2 * (l * 24 + nb)
                for c in range(8):
                    self.mm(pm[:, col:col + 2], st[:, c * 128:(c + 1) * 128], self.CACT[:, c, :],
                            c == 0, c == 7, [stT, self.CACT_T], [pmT])
        n = 24 * self.depth
        if n > 0:
            self.tt("dve", self.ADA[:, 0:n], pm[:, 0:2 * n].rearrange("p (n two) -> p n two", two=2)[:, :, 0],
                    M[:, BADA0:BADA0 + n], ALU.add, [pmT, self.MISC_T], [self.ADA_T])
        for l in range(self.depth):
            self.stt(self.GSC[:, l * 8:(l + 1) * 8], self.ADA[:, l * 24 + 8:l * 24 + 16], 1.0,
                     M[:, NG0 + l * 8:NG0 + (l + 1) * 8], ALU.add, ALU.mult, [self.ADA_T, self.MISC_T], [self.ADA_T])
        self.dbg("ada", self.ADA[:, :], [self.ADA_T])
        self.dbg("gsc", self.GSC[:, :], [self.ADA_T])
        for _k in range(int(os.environ.get("K_DUMMY", "0"))):
            self.memset("dve", self.SELW[:, 0:8], 0.0, [self.SELW_T])
        IDF = M[:, IDF0:IDF0 + 128]
        for tt_ in range(16):
            st, stT = self.stage.get()
            self.dma(st[:, :], self.x_d[tt_ * 128:(tt_ + 1) * 128, :], [], [stT], key=stT.name)
            for half in range(2):
                pm, pmT = self.nextM()
                for cc in range(4):
                    c = half * 4 + cc
                    self.trf(pm[:, cc * 128:(cc + 1) * 128], st[:, c * 128:(c + 1) * 128], IDF,
                             [stT, self.MISC_T], [pmT])
                tb = tt_ // 4
                eng = "act" if half == 0 else "dve"
                self.cp(eng, self.XT[:, half * 4:half * 4 + 4, tt_ * 128:(tt_ + 1) * 128],
                        pm[:, :].rearrange("p (c t) -> p c t", c=4), [pmT],
                        [self.XT_T[half * 4 + cc][tb] for cc in range(4)])

    def rstd_block(self, tb):
        pm, pmT = self.nextM()
        for c in range(8):
            sq, sqT = self.ptp.get()
            xsl = self.XT[:, c, tb * 512:(tb + 1) * 512]
            if c % 2 == 0 or os.environ.get("K_H", "new") == "old":
                self.act(sq[:, :], xsl, AF.Square, [self.XT_T[c][tb]], [sqT])
            else:
                self.tt("dve", sq[:, :], xsl, xsl, ALU.mult, [self.XT_T[c][tb]], [sqT])
            self.mm(pm[:, :], self.ONESB, sq[:, :], c == 0, c == 7, [sqT, self.MB_T], [pmT])
        r, rT = self.rsp.get()
        self.ts("dve", r[:, :], pm[:, :], 1.0 / D, 1e-6, ALU.mult, ALU.add, [pmT], [rT])
        self.act(r[:, :], r[:, :], AF.Sqrt, [rT], [rT])
        self.recip(r[:, :], r[:, :], [rT], [rT])
        return r, rT

    def make_h(self, l):
        self.P.cur_tag = 'L%d.h' % l
        rs = {0: self.rstd_block(0)}
        for tb in range(4):
            if tb + 1 < 4:
                rs[tb + 1] = self.rstd_block(tb + 1)
            r, rT = rs.pop(tb)
            for c in range(8):
                t, tT = self.tmp.get()
                self.tt("pool" if (c % 3 == 0 or os.environ.get("K_H", "new") == "old") else "dve", t[:, :], self.XT[:, c, tb * 512:(tb + 1) * 512], r[:, :], ALU.mult,
                        [self.XT_T[c][tb], rT], [tT])
                self.act(self.HT[:, c, tb * 512:(tb + 1) * 512], t[:, :], AF.Identity, [tT, self.ADA_T], [self.HT_T[tb]],
                         scale=self.GSC[:, l * 8 + c:l * 8 + c + 1], bias=self.ADA[:, l * 24 + c:l * 24 + c + 1])

    def final(self):
        M = self.MISC
        self.P.cur_tag = 'final'
        IDF = M[:, IDF0:IDF0 + 128]
        for tb in range(4):
            r, rT = self.rstd_block(tb)
            for c in range(8):
                xs = self.XT[:, c, tb * 512:(tb + 1) * 512]
                self.stt(xs, xs, M[:, FG0 + c:FG0 + c + 1], r[:, :], ALU.mult, ALU.mult,
                         [self.XT_T[c][tb], rT, self.MISC_T], [self.XT_T[c][tb]])
            for a in range(4):
                tt_ = tb * 4 + a
                st, stT = self.stage.get()
                for half in range(2):
                    pm, pmT = self.nextM()
                    for cc in range(4):
                        c = half * 4 + cc
                        self.trf(pm[:, cc * 128:(cc + 1) * 128], self.XT[:, c, tt_ * 128:(tt_ + 1) * 128], IDF,
                                 [self.XT_T[c][tb], self.MISC_T], [pmT])
                    self.cp("act" if half == 0 else "dve", st[:, half * 512:(half + 1) * 512], pm[:, :], [pmT], [stT])
                self.dma(self.out_d[tt_ * 128:(tt_ + 1) * 128, :], st[:, :], [stT], [], key=stT.name)
        self.P.add("sp", None, (), [b[1] for b in self.stage.bufs])

    def proj_fm(self, wb, wbT, wsw, wswT, dst, M0=128):
        CC, SS = self.CS[:, 0:S], self.CS[:, S:2 * S]
        for tb in range(4):
            tsl = slice(tb * 512, (tb + 1) * 512)
            p1, p1T = self.nextM()
            for c in range(8):
                self.mm(p1[0:M0, :], wb[:, c * 128:c * 128 + M0], self.HT[:, c, tsl], c == 0, c == 7,
                        [wbT, self.HT_T[tb]], [p1T])
            anyrope = any(d[3] for d in dst)
            if anyrope:
                msw = 128 if (len(dst) > 1 and dst[1][3]) else 64
                p2, p2T = self.nextM()
                for c in range(8):
                    self.mm(p2[0:msw, :], wsw[:, c * 128:c * 128 + msw], self.HT[:, c, tsl], c == 0, c == 7,
                            [wswT, self.HT_T[tb]], [p2T])
                t1, t1T = self.tmp.get()
                t2, t2T = self.tmp.get()
                self.tt("dve", t1[0:msw, :], p1[0:msw, :], CC[0:msw, tsl], ALU.mult, [p1T, self.CS_T], [t1T])
                self.tt("dve", t2[0:msw, :], p2[0:msw, :], SS[0:msw, tsl], ALU.mult, [p2T, self.CS_T], [t2T])
            for (r0, apfn, dT, rope) in dst:
                if rope:
                    self.tt(os.environ.get("K_ROPE", "dve"), apfn(tb), t1[r0:r0 + 64, :], t2[r0:r0 + 64, :], ALU.add, [t1T, t2T], [dT])
                else:
                    self.cp("act", apfn(tb), p1[r0:r0 + 64, :], [p1T], [dT])

    def proj_tm(self, wb, wbT, evac):
        for tg in range(4):
            pm, pmT = self.nextM()
            for a in range(4):
                tt_ = tg * 4 + a
                for c in range(8):
                    self.mm(pm[:, a * 128:(a + 1) * 128], self.HT[:, c, tt_ * 128:(tt_ + 1) * 128],
                            wb[:, c * 128:(c + 1) * 128], c == 0, c == 7, [wbT, self.HT_T[tt_ // 4]], [pmT])
            evac(tg, pm[:, :].rearrange("p (a n) -> p a n", a=4), pmT)

    def run_calls(self, calls):
        steps = [(ci, kt) for ci, c in enumerate(calls) for kt in c["kts"]]
        state = {}

        def emit_scores(sidx):
            ci, kt = steps[sidx]
            c = calls[ci]
            kp = c.get("kparts", 128)
            banks4 = self.pS + self.pM
            ps, psT = banks4[self.si % 4]
            self.si += 1
            lhsT, lreads = c["lhs_fn"](kt)
            c0 = c["c0"](kt) if "c0" in c else 0
            pso = ps[0:kp, c0:512]
            rhs = c["rhs"] if c0 == 0 else c["rhs"][:, c0:512]
            if len(rhs.shape) == 3:
                pso = pso.rearrange("p (a n) -> p a n", a=4)
            self.mm(pso, lhsT, rhs, True, True, lreads + c["rhs_reads"], [psT])
            state[sidx] = (ps, psT)
        LA = int(os.environ.get("K_LA", "3"))
        for k in range(min(LA, len(steps))):
            emit_scores(k)
        for sidx in range(len(steps)):
            if sidx + LA < len(steps):
                emit_scores(sidx + LA)
            ci, kt = steps[sidx]
            c = calls[ci]
            kp = c.get("kparts", 128)
            oset = ((self.ocount + ci) % 2) if os.environ.get("K_OSET", "0") == "1" else 0
            off = oset * 256
            poT = self.pO_T2[oset]
            ps, psT = state.pop(sidx)
            pt, ptT = self.ptp.get()
            c0 = c["c0"](kt) if "c0" in c else 0
            self.act(pt[0:kp, c0:512], ps[0:kp, c0:512], AF.Exp, [psT], [ptT], scale=0.125)
            m = c["mask_fn"](kt)
            if m is not None:
                for (pattern, base, cm) in m:
                    n_el = 1
                    for st_, nn in pattern:
                        n_el *= nn
                    self.sel(pt[0:kp, c0:c0 + n_el], pattern, base, cm, [ptT])
            vrhs, vreads = c["pv_fn"](kt)
            ncols = c["ncols"]
            for a in range(4):
                if not c["valid"](a, kt):
                    continue
                self.mm(self.pO[:, a, off:off + ncols], pt[0:kp, a * 128:(a + 1) * 128], vrhs,
                        kt == c["kts"][0], kt == c["lastk"](a), [ptT] + vreads, [poT[a]])
            if kt == c["kts"][-1]:
                c["after"](self.pO[:, :, off:off + 128], poT)
        self.ocount += len(calls)

    def finish_pairs(self, l, pair_specs):
        assert len(pair_specs) == 2
        ogtts = [(self.OGTT[:, :], self.OGTT_T),
                 (self.ZS[:, 0:8, :].rearrange("p a b -> p (a b)"), self.ZS_T)]
        wos = []
        for k, (coff, wdesc) in enumerate(pair_specs):
            wo, woT = self.wload(wdesc)
            wos.append((wo, woT))
            og, ogT = ogtts[k]
            for tg in range(2):
                for hh in range(2):
                    pm, pmT = self.nextM()
                    for a in range(4):
                        tt_ = tg * 8 + hh * 4 + a
                        self.mm(pm[:, a * 128:(a + 1) * 128], self.OGT[:, tt_, coff:coff + 128], self.IDB, True, True,
                                [self.OGT_T[tt_], self.MB_T], [pmT])
                    c0 = (tg * 8 + hh * 4) * 128
                    self.cp("act", og[:, c0:c0 + 512], pm[:, :], [pmT], [ogT])
        for nb in range(8):
            for tb in range(4):
                pm, pmT = self.nextM()
                for k in range(2):
                    wo, woT = wos[k]
                    og, ogT = ogtts[k]
                    self.mm(pm[:, :], wo[:, nb * 128:(nb + 1) * 128], og[:, tb * 512:(tb + 1) * 512], k == 0, k == 1,
                            [woT, ogT], [pmT])
                xs = self.XT[:, nb, tb * 512:(tb + 1) * 512]
                self.stt(xs, pm[:, :], self.ADA[:, l * 24 + 16 + nb:l * 24 + 17 + nb], xs, ALU.mult, ALU.add,
                         [pmT, self.ADA_T, self.XT_T[nb][tb]], [self.XT_T[nb][tb]])

    def qproj(self, name, li, col0, slots):
        cols = col0 + np.arange(128)
        wb, wbT = self.wload(("cols", name, li, cols))
        ws, wsT = self.wload(("cols", name, li, _sw(cols)))
        dst = []
        for k, r in enumerate(slots):
            dst.append((64 * k, (lambda tb, r=r: self.QTA[0:64, r, tb * 512:(tb + 1) * 512]), self.Qh[r], True))
        self.proj_fm(wb, wbT, ws, wsT, dst)

    def odd_layer(self, l):
        li = l // 2
        M = self.MISC
        self.cp("pool", self.KB[64:96, :], self.KA[64:96, :], [self.Kaug], [self.KBaug])
        self.make_h(l)
        for hp in range(8):
            base = hp * 128
            zoff = (hp % 2) * 128
            self.P.cur_tag = 'L%d.odd.proj' % l
            self.qproj("w_in_odd", li, base, (0, 1))
            cols = 1024 + base + np.arange(128)
            wb, wbT = self.wload(("cols", "w_in_odd", li, cols))
            ws, wsT = self.wload(("cols", "w_in_odd", li, _sw(cols)))
            self.proj_fm(wb, wbT, ws, wsT, [
                (0, (lambda tb: self.KA[0:64, tb * 512:(tb + 1) * 512]), self.KAh, True),
                (64, (lambda tb: self.KB[0:64, tb * 512:(tb + 1) * 512]), self.KBh, True)])
            Ks = [(self.KA, self.KAh), (self.KB, self.KBh)]
            self.P.cur_tag = 'L%d.odd.sel' % l
            kmf, kmfT = self.sml.get()
            for h in range(2):
                Kt, KT_ = Ks[h]
                self.P.add("dve", (lambda e, Kt=Kt, h=h, kmf=kmf: e.tensor_reduce(
                    out=kmf[0:64, h * 8:(h + 1) * 8], in_=Kt[0:64, :].rearrange("p (j k) -> p j k", j=8),
                    axis=AX.X, op=ALU.add)), [KT_], [kmfT])
            self.cp("dve", self.KM[0:64, :, :], kmf[0:64, :].rearrange("p (h j) -> p h j", h=2), [kmfT], [self.KM_T])
            wv, wvT = self.wload(("cols", "w_in_odd", li, 2048 + base + np.arange(128)))

            def evv(tg, ps3, pT):
                for h in range(2):
                    self.cp("act", self.VA[:, tg * 4:(tg + 1) * 4, h, 0:64], ps3[:, :, h * 64:(h + 1) * 64], [pT],
                            [self.VA_T[h]])
            self.proj_tm(wv, wvT, evv)
            pm, pmT = self.nextM()
            for tt_ in range(16):
                for h in range(2):
                    col = (tt_ * 2 + h) * 8
                    self.mm(pm[:, col:col + 8], self.QTA[0:_R('gs', 64), h, tt_ * 128:(tt_ + 1) * 128], self.KM[0:_R('gs', 64), h, :],
                            True, True, [self.Qh[h], self.Qa[h], self.KM_T], [pmT])
            W = self.SELW
            gsm = W[:, 0:256].rearrange("p (t h j) -> p t h j", t=16, h=2)
            pb = M[:, PBT0:PBT0 + 128].rearrange("p (t j) -> p t j", t=16).unsqueeze(2).to_broadcast([128, 16, 2, 8])
            self.tt("dve", gsm, pm[:, 0:256].rearrange("p (t h j) -> p t h j", t=16, h=2), pb, ALU.add,
                    [pmT, self.MISC_T], [self.SELW_T])
            g3v = W[:, 0:256].rearrange("p (k j) -> p k j", j=8)
            cur = g3v
            mxs = W[:, 256:352]
            wk = W[:, 512:768].rearrange("p (k j) -> p k j", j=8)
            wk2 = W[:, 768:1024].rearrange("p (k j) -> p k j", j=8)
            for rnd in range(3):
                mcol = mxs[:, rnd * 32:(rnd + 1) * 32]
                self.P.add("dve", (lambda e, cur=cur, mcol=mcol: e.tensor_reduce(out=mcol, in_=cur, axis=AX.X, op=ALU.max)),
                           [self.SELW_T], [self.SELW_T])
                if rnd == 2:
                    break
                eq = wk if rnd == 0 else wk2
                self.tt("dve", eq, cur, mcol.unsqueeze(2).to_broadcast([128, 32, 8]), ALU.is_ge, [self.SELW_T], [self.SELW_T])
                self.stt(eq, eq, -1e30, cur, ALU.mult, ALU.add, [self.SELW_T], [self.SELW_T])
                cur = eq
            lt = W[:, 512:768]
            thr = mxs[:, 64:96].unsqueeze(2).to_broadcast([128, 32, 8])
            self.tt("dve", lt.rearrange("p (k j) -> p k j", j=8), W[:, 0:256].rearrange("p (k j) -> p k j", j=8), thr,
                    ALU.is_lt, [self.SELW_T], [self.SELW_T])
            npb = M[:, NPT0:NPT0 + 128].rearrange("p (t j) -> p t j", t=16).unsqueeze(2).to_broadcast([128, 16, 2, 8])
            sb8 = W[:, 768:1024]
            self.tt("dve", sb8.rearrange("p (t h j) -> p t h j", t=16, h=2),
                    lt.rearrange("p (t h j) -> p t h j", t=16, h=2), npb, ALU.mult, [self.SELW_T, self.MISC_T],
                    [self.SELW_T])
            self.cp("dve", self.SELB[:, :, :, :].rearrange("p t h (j f) -> p (t h) j f", f=4),
                    sb8.rearrange("p (k j) -> p k j", j=8).unsqueeze(3).to_broadcast([128, 32, 8, 4]),
                    [self.SELW_T], [self.SELB_T])
            wz, wzT = self.wload(("cols", "w_in_odd", li, 3072 + base + np.arange(128)))

            def evz(tg, ps3, pT, zoff=zoff):
                self.act(self.ZS[:, tg * 4:(tg + 1) * 4, zoff:zoff + 128], ps3, AF.Silu, [pT], [self.ZS_T])
            self.proj_tm(wz, wzT, evz)
            for h in range(2):
                for g4 in range(4):
                    pm, pmT = self.nextM()
                    for a in range(4):
                        tt_ = g4 * 4 + a
                        self.mm(pm[0:32, a * 128:(a + 1) * 128], self.SELB[:, tt_, h, :], self.IDB, True, True,
                                [self.SELB_T, self.MB_T], [pmT])
                    self.cp("act", self.QTA[64:96, h, g4 * 512:(g4 + 1) * 512], pm[0:32, :], [pmT], [self.Qa[h]])
            self.P.cur_tag = 'L%d.odd.attn' % l
            calls = []
            for h in range(2):
                Kt, KT_ = Ks[h]
                for Q in range(4):
                    def lhs_fn(kt, Kt=Kt, KT_=KT_):
                        return Kt[0:_R('moba', 96), kt * 128:(kt + 1) * 128], [KT_, self.Kaug, self.KBaug]

                    def mask_fn(kt, Q=Q):
                        if kt < 4 * Q:
                            return None
                        return [([[1, 128]], 0, -1)]

                    def pv_fn(kt, h=h):
                        return self.VA[:, kt, h, :], [self.VA_T[h]]

                    def after(po, poT, h=h, Q=Q, zoff=zoff):
                        rc, rcT = self.sml.get()
                        self.recip(rc[:, 0:4], po[:, :, 64], poT, [rcT])
                        t, tT = self.tmp.get()
                        t3 = t[:, 0:256].rearrange("p (a d) -> p a d", a=4)
                        self.tt("dve", t3, po[:, :, 0:64], rc[:, 0:4].unsqueeze(2).to_broadcast([128, 4, 64]), ALU.mult,
                                poT + [rcT], [tT])
                        self.tt("pool", self.OGT[:, Q * 4:(Q + 1) * 4, zoff + h * 64:zoff + (h + 1) * 64], t3,
                                self.ZS[:, Q * 4:(Q + 1) * 4, zoff + h * 64:zoff + (h + 1) * 64], ALU.mult, [tT, self.ZS_T],
                                self.OGT_T[Q * 4:(Q + 1) * 4])
                    calls.append(dict(lhs_fn=lhs_fn, rhs=self.QTA[0:_R('moba', 96), h, Q * 512:(Q + 1) * 512],
                                      rhs_reads=[self.Qh[h], self.Qa[h]], kts=list(range(4 * Q + 4)), mask_fn=mask_fn,
                                      pv_fn=pv_fn, ncols=65, valid=(lambda a, kt, Q=Q: kt <= 4 * Q + a),
                                      c0=(lambda kt, Q=Q: max(0, kt - 4 * Q) * 128),
                                      lastk=(lambda a, Q=Q: 4 * Q + a), after=after))
            self.run_calls(calls)
            if l == 1 and hp == 0:
                self.dbg("o_q", self.QTA[:, :, :], self.Qh + self.Qa, BF16)
                self.dbg("o_ka", self.KA[:, :], [self.KAh], BF16)
                self.dbg("o_kb", self.KB[:, :], [self.KBh], BF16)
                self.dbg("o_w", self.SELW[:, :], [self.SELW_T])
                self.dbg("o_ogt", self.OGT[:, :, :], self.OGT_T, BF16)
                self.dbg("o_va", self.VA[:, :, :, :], self.VA_T, BF16)
                self.dbg("o_zs", self.ZS[:, :, :], [self.ZS_T], BF16)
                self.dbg("o_xt", self.XT[:, :, :], [t for row in self.XT_T for t in row])
            self.P.cur_tag = 'L%d.odd.fin' % l
            if hp % 2 == 1:
                self.finish_pairs(l, [(0, ("rows", "w_out_odd", li, base - 128)), (128, ("rows", "w_out_odd", li, base))])

    def even_layer(self, l):
        li = l // 2
        M = self.MISC
        EO = EVEN_OFF
        self.memset("pool", self.KB[64:96, :], 0.0, [self.KBaug])
        self.make_h(l)
        if l == 0:
            self.dbg("ht", self.HT[:, :, :], self.HT_T, BF16)
        self.P.cur_tag = 'L%d.A.common' % l
        ka = EO["ka"] + np.arange(64)
        cols = np.concatenate([ka, ka])
        wb, wbT = self.wload(("cols", "w_in_even", li, cols))
        ws, wsT = self.wload(("cols", "w_in_even", li, _sw(cols)))
        self.proj_fm(wb, wbT, ws, wsT, [(0, (lambda tb: self.KB[0:64, tb * 512:(tb + 1) * 512]), self.KBh, True)])
        cols = np.concatenate([EO["va"] + np.arange(64), EO["gb"] + np.arange(24)])
        wv, wvT = self.wload(("cols", "w_in_even", li, cols))

        def evv(tg, ps3, pT):
            self.cp("act", self.VAa[:, tg * 4:(tg + 1) * 4, 0:64], ps3[:, :, 0:64], [pT], [self.VAa_T])
            self.act(self.GSG[:, tg * 4:(tg + 1) * 4, :], ps3[:, :, 64:88], AF.Sigmoid, [pT], [self.GSG_T])
        self.proj_tm(wv, wvT, evv)
        for ag in range(2):
            self.P.cur_tag = 'L%d.A.proj' % l
            for pr in range(2):
                self.qproj("w_in_even", li, EO["qa"] + (ag * 2 + pr) * 128, (pr * 2, pr * 2 + 1))
            for pr in range(2):
                wz, wzT = self.wload(("cols", "w_in_even", li, EO["za"] + (ag * 2 + pr) * 128 + np.arange(128)))

                def evz(tg, ps3, pT, pr=pr):
                    self.act(self.ZS[:, tg * 4:(tg + 1) * 4, pr * 128:(pr + 1) * 128], ps3, AF.Silu, [pT], [self.ZS_T])
                self.proj_tm(wz, wzT, evz)
            if l == 0 and ag == 0:
                self.dbg("qa", self.QTA[:, :, :], self.Qh, BF16)
                self.dbg("kb", self.KB[:, :], [self.KBh], BF16)
                self.dbg("vaa", self.VAa[:, :, :], [self.VAa_T], BF16)
                self.dbg("zs", self.ZS[:, :, :], [self.ZS_T], BF16)
                self.dbg("gsg", self.GSG[:, :, :], [self.GSG_T])
            self.P.cur_tag = 'L%d.A.attn' % l
            calls = []
            for i in range(16):
                kts = [i - 1, i] if i > 0 else [0]

                def lhs_fn(kt):
                    return self.KB[0:_R('swa', 64), kt * 128:(kt + 1) * 128], [self.KBh, self.KBaug]

                def mask_fn(kt, i=i):
                    if kt == i:
                        return [([[0, 4], [1, 128]], 0, -1)]
                    return [([[0, 4], [-1, 128]], -1, 1)]

                def pv_fn(kt):
                    return self.VAa[:, kt, :], [self.VAa_T]

                def after(po, poT, i=i, ag=ag):
                    rc, rcT = self.sml.get()
                    self.tt("dve", rc[:, 0:4], po[:, :, 64], self.ESINK[:, li * 8 + ag * 4:li * 8 + ag * 4 + 4], ALU.add,
                            poT + [self.ESINK_T], [rcT])
                    self.recip(rc[:, 4:8], rc[:, 0:4], [rcT], [rcT])
                    t, tT = self.tmp.get()
                    t3 = t[:, 0:256].rearrange("p (a d) -> p a d", a=4)
                    self.tt("dve", t3, po[:, :, 0:64], rc[:, 4:8].unsqueeze(2).to_broadcast([128, 4, 64]), ALU.mult,
                            poT + [rcT], [tT])
                    self.tt("pool", self.OGT[:, i, :], t[:, 0:256], self.ZS[:, i, :], ALU.mult, [tT, self.ZS_T],
                            [self.OGT_T[i]])
                calls.append(dict(lhs_fn=lhs_fn, rhs=self.QTA[0:_R('swa', 64), :, i * 128:(i + 1) * 128], rhs_reads=self.Qh + self.Qa, kts=kts,
                                  mask_fn=mask_fn, pv_fn=pv_fn, ncols=65, valid=(lambda a, kt: True),
                                  lastk=(lambda a, i=i: i), after=after))
            self.run_calls(calls)
            if l == 0:
                self.dbg("ogt_a%d" % ag, self.OGT[:, :, :], self.OGT_T, BF16)
            self.P.cur_tag = 'L%d.A.fin' % l
            self.finish_pairs(l, [(pr * 128, ("rows", "w_out_even", li, (ag * 2 + pr) * 128)) for pr in range(2)])
            if l == 0 and ag == 0:
                self.dbg("xt_a0", self.XT[:, :, :], [t for row in self.XT_T for t in row])
        for g in range(2):
            self.P.cur_tag = 'L%d.B.proj' % l
            for pr in range(2):
                self.qproj("w_in_even", li, EO["qb"] + (g * 2 + pr) * 128, (pr * 2, pr * 2 + 1))
            kc = EO["kc"] + g * 64 + np.arange(64)
            vc = EO["vc"] + g * 64 + np.arange(64)
            wb, wbT = self.wload(("cols", "w_in_even", li, np.concatenate([kc, vc])))
            ws, wsT = self.wload(("cols", "w_in_even", li, np.concatenate([_sw(kc), _sw(kc)])))
            self.proj_fm(wb, wbT, ws, wsT, [
                (0, (lambda tb: self.KA[0:64, tb * 512:(tb + 1) * 512]), self.KAh, True),
                (64, (lambda tb: self.KB[0:64, tb * 512:(tb + 1) * 512]), self.KBh, False)])
            self.P.cur_tag = 'L%d.B.cmpmlp' % l
            for kv in range(2):
                SRC, SRC_T = (self.KA, self.KAh) if kv == 0 else (self.KB, self.KBh)
                pe = self.MB[0:64, 832 + li * 64 + kv * 32:832 + li * 64 + kv * 32 + 32]
                w2 = self.MB[:, 320 + li * 256 + kv * 128:320 + li * 256 + (kv + 1) * 128]
                ph = [self.pM[0], self.pM[1]]
                pbis = [self.pS[0], self.pS[1]]
                for lc in range(8):
                    w1, w1T = self.wload(("w1", "cmp_w1_k" if kv == 0 else "cmp_w1_v", li, lc * 4), nparts=64)
                    for ll in range(4):
                        lidx = lc * 4 + ll
                        for hc in range(2):
                            lhsT = w1[0:64, ll * 256 + hc * 128:ll * 256 + (hc + 1) * 128]
                            self.mm(ph[hc][0][:, 0:127], lhsT, SRC[0:64, lidx:lidx + 16 * 126 + 1:16], lidx == 0, lidx == 31,
                                    [w1T, SRC_T], [ph[hc][1]])
                            self.mm(pbis[hc][0][:, 0:1], lhsT, pe[:, lidx:lidx + 1], lidx == 0, lidx == 31,
                                    [w1T, self.MB_T], [pbis[hc][1]])
                bi, biT = self.sml.get()
                for hc in range(2):
                    self.cp("dve", bi[:, 2 * hc:2 * hc + 1], pbis[hc][0][:, 0:1], [pbis[hc][1]], [biT])
                for hc in range(2):
                    xx, xT = self.tmp.get()
                    self.act(xx[:, 0:127], ph[hc][0][:, 0:127], AF.Identity, [ph[hc][1], biT], [xT],
                             bias=bi[:, 2 * hc:2 * hc + 1])
                    u, uT = self.tmp.get()
                    self.tt("dve", u[:, 0:127], xx[:, 0:127], xx[:, 0:127], ALU.mult, [xT], [uT])
                    self.ts("dve", u[:, 0:127], u[:, 0:127], 0.044715, 1.0, ALU.mult, ALU.add, [uT], [uT])
                    self.tt("dve", u[:, 0:127], u[:, 0:127], xx[:, 0:127], ALU.mult, [uT, xT], [uT])
                    self.act(u[:, 0:127], u[:, 0:127], AF.Sigmoid, [uT], [uT], scale=1.5957691216057308)
                    self.tt("dve", self.HID[:, kv, hc, 0:127], u[:, 0:127], xx[:, 0:127], ALU.mult, [uT, xT],
                            [self.HID_T[kv]])
                pm, pmT = self.nextM()
                if kv == 0:
                    for hc in range(2):
                        self.mm(pm[0:64, 0:127], w2[:, hc * 64:(hc + 1) * 64], self.HID[:, 0, hc, 0:127], hc == 0, hc == 1,
                                [self.MB_T, self.HID_T[0]], [pmT])
                    self.cp("act", self.KCMP[0:64, 0:127], pm[0:64, 0:127], [pmT], [self.KCMP_T])
                else:
                    for hc in range(2):
                        self.mm(pm[0:127, 0:64], self.HID[:, 1, hc, 0:127], w2[:, hc * 64:(hc + 1) * 64], hc == 0, hc == 1,
                                [self.MB_T, self.HID_T[1]], [pmT])
                    self.cp("act", self.VCX[0:127, 0:64], pm[0:127, 0:64], [pmT], [self.VCX_T])
            self.P.cur_tag = 'L%d.B.proj2' % l
            for pr in range(2):
                wz, wzT = self.wload(("cols", "w_in_even", li, EO["zb"] + (g * 2 + pr) * 128 + np.arange(128)))

                def evz(tg, ps3, pT, pr=pr):
                    self.act(self.ZS[:, tg * 4:(tg + 1) * 4, pr * 128:(pr + 1) * 128], ps3, AF.Silu, [pT], [self.ZS_T])
                self.proj_tm(wz, wzT, evz)
            cols = np.concatenate([EO["vs"] + g * 64 + np.arange(64), EO["vw"] + g * 64 + np.arange(64)])
            wv, wvT = self.wload(("cols", "w_in_even", li, cols))

            def evv2(tg, ps3, pT):
                for h in range(2):
                    self.cp("act", self.VA[:, tg * 4:(tg + 1) * 4, h, 0:64], ps3[:, :, h * 64:(h + 1) * 64], [pT],
                            [self.VA_T[h]])
            self.proj_tm(wv, wvT, evv2)
            self.P.cur_tag = 'L%d.B.cmpattn' % l
            W = self.SELW
            calls = []
            for i in range(16):
                def lhs_fn(kt):
                    return self.KCMP[0:_R('cmp', 64), 0:127], [self.KCMP_T]

                def mask_fn(kt, i=i):
                    return [([[0, 4], [1, 128]], 128 * i - 31, -16)]

                def pv_fn(kt):
                    return self.VCX[0:127, :], [self.VCX_T]

                def after(po, poT, i=i, g=g):
                    rc, rcT = self.sml.get()
                    self.ts("dve", rc[:, 0:4], po[:, :, 64], 1e-30, None, ALU.max, None, poT, [rcT])
                    self.recip(rc[:, 4:8], rc[:, 0:4], [rcT], [rcT])
                    imp = W[:, 0:32]
                    for r in range(4):
                        if r == 0:
                            self.ts("dve", imp, po[:, 0, 65:97], rc[:, 4:5], None, ALU.mult, None, [poT[0], rcT],
                                    [self.SELW_T])
                        else:
                            self.stt(imp, po[:, r, 65:97], rc[:, 4 + r:5 + r], imp, ALU.mult, ALU.add,
                                     [poT[r], rcT, self.SELW_T], [self.SELW_T])
                    gsl = self.GSG[:, i, :].rearrange("p (h b) -> p h b", b=3)[:, g * 4:(g + 1) * 4, 0]
                    self.tt("dve", rc[:, 8:12], rc[:, 4:8], gsl, ALU.mult, [rcT, self.GSG_T], [rcT])
                    self.tt("dve", self.OGT[:, i, :].rearrange("p (a d) -> p a d", a=4), po[:, :, 0:64],
                            rc[:, 8:12].unsqueeze(2).to_broadcast([128, 4, 64]), ALU.mult, poT + [rcT], [self.OGT_T[i]])
                    self.tt("dve", W[:, 32:64], imp, M[:, FT0 + i * 32:FT0 + (i + 1) * 32], ALU.max,
                            [self.SELW_T, self.MISC_T], [self.SELW_T])
                    self.tt("dve", W[:, 64:96], W[:, 32:64], M[:, VT0 + i * 32:VT0 + (i + 1) * 32], ALU.min,
                            [self.SELW_T, self.MISC_T], [self.SELW_T])
                    self.P.add("dve", lambda e: e.max(out=W[:, 96:104], in_=W[:, 64:96]), [self.SELW_T], [self.SELW_T])
                    self.ts("dve", self.SELB[:, i, 0, :], W[:, 64:96], W[:, 103:104], NEG, ALU.is_lt, ALU.mult,
                            [self.SELW_T], [self.SELB_T])
                calls.append(dict(lhs_fn=lhs_fn, rhs=self.QTA[0:_R('cmp', 64), :, i * 128:(i + 1) * 128], rhs_reads=self.Qh + self.Qa, kts=[0],
                                  mask_fn=mask_fn, pv_fn=pv_fn, ncols=97, valid=(lambda a, kt: True),
                                  lastk=(lambda a: 0), kparts=127, after=after))
            self.run_calls(calls)
            for g4 in range(4):
                pm, pmT = self.nextM()
                for a in range(4):
                    tt_ = g4 * 4 + a
                    self.mm(pm[0:32, a * 128:(a + 1) * 128], self.SELB[:, tt_, 0, :], self.IDB, True, True,
                            [self.SELB_T, self.MB_T], [pmT])
                self.cp("act", self.QTA[64:96, :, g4 * 512:(g4 + 1) * 512],
                        pm[0:32, :].unsqueeze(1).to_broadcast([32, 4, 512]), [pmT], self.Qa)
            if l == 0 and g == 0:
                self.dbg("kcmp", self.KCMP[:, :], [self.KCMP_T], BF16)
                self.dbg("vcx", self.VCX[:, :], [self.VCX_T], BF16)
                self.dbg("ogt_cmp", self.OGT[:, :, :], self.OGT_T, BF16)
                self.dbg("selb", self.SELB[:, :, :, :], [self.SELB_T], BF16)
                self.dbg("qaug", self.QTA[:, :, :], self.Qh + self.Qa, BF16)
            self.P.cur_tag = 'L%d.B.kskw' % l
            ks = EO["ks"] + g * 64 + np.arange(64)
            kw = EO["kw"] + g * 64 + np.arange(64)
            cols = np.concatenate([ks, kw])
            wb, wbT = self.wload(("cols", "w_in_even", li, cols))
            ws, wsT = self.wload(("cols", "w_in_even", li, _sw(cols)))
            self.proj_fm(wb, wbT, ws, wsT, [
                (0, (lambda tb: self.KA[0:64, tb * 512:(tb + 1) * 512]), self.KAh, True),
                (64, (lambda tb: self.KB[0:64, tb * 512:(tb + 1) * 512]), self.KBh, True)])
            self.P.cur_tag = 'L%d.B.slcwin' % l
            calls = []
            for i in range(16):
                for br in (1, 2):
                    if br == 1:
                        kts = list(range(i + 1))

                        def lhs_fn(kt):
                            return self.KA[0:_R('slc', 96), kt * 128:(kt + 1) * 128], [self.KAh, self.Kaug]
                        rhs = self.QTA[0:_R('slc', 96), :, i * 128:(i + 1) * 128]
                        rreads = self.Qh + self.Qa

                        def mask_fn(kt, i=i):
                            if kt == i:
                                return [([[0, 4], [1, 128]], 0, -1)]
                            return None

                        def pv_fn(kt):
                            return self.VA[:, kt, 0, :], [self.VA_T[0]]
                    else:
                        kts = list(range(max(0, i - 4), i + 1))

                        def lhs_fn(kt):
                            return self.KB[0:_R('win', 64), kt * 128:(kt + 1) * 128], [self.KBh, self.KBaug]
                        rhs = self.QTA[0:_R('win', 64), :, i * 128:(i + 1) * 128]
                        rreads = self.Qh + self.Qa

                        def mask_fn(kt, i=i):
                            if kt == i:
                                return [([[0, 4], [1, 128]], 0, -1)]
                            if kt == i - 4:
                                return [([[0, 4], [-1, 128]], -1, 1)]
                            return None

                        def pv_fn(kt):
                            return self.VA[:, kt, 1, :], [self.VA_T[1]]

                    def after(po, poT, i=i, br=br, g=g):
                        rc, rcT = self.sml.get()
                        self.recip(rc[:, 0:4], po[:, :, 64], poT, [rcT])
                        gsl = self.GSG[:, i, :].rearrange("p (h b) -> p h b", b=3)[:, g * 4:(g + 1) * 4, br]
                        self.tt("dve", rc[:, 4:8], rc[:, 0:4], gsl, ALU.mult, [rcT, self.GSG_T], [rcT])
                        t, tT = self.tmp.get()
                        self.tt("dve", t[:, 0:256].rearrange("p (a d) -> p a d", a=4), po[:, :, 0:64],
                                rc[:, 4:8].unsqueeze(2).to_broadcast([128, 4, 64]), ALU.mult, poT + [rcT], [tT])
                        self.tt("pool", self.OGT[:, i, :], t[:, 0:256], self.OGT[:, i, :], ALU.add, [tT, self.OGT_T[i]],
                                [self.OGT_T[i]])
                        if br == 2:
                            self.tt("pool", self.OGT[:, i, :], self.OGT[:, i, :], self.ZS[:, i, :], ALU.mult,
                                    [self.OGT_T[i], self.ZS_T], [self.OGT_T[i]])
                    calls.append(dict(lhs_fn=lhs_fn, rhs=rhs, rhs_reads=rreads, kts=kts, mask_fn=mask_fn, pv_fn=pv_fn,
                                      ncols=65, valid=(lambda a, kt: True), lastk=(lambda a, i=i: i), after=after))
            self.run_calls(calls)
            if l == 0:
                self.dbg("ogt_b%d" % g, self.OGT[:, :, :], self.OGT_T, BF16)
            self.P.cur_tag = 'L%d.B.fin' % l
            self.finish_pairs(l, [(pr * 128, ("rows", "w_out_even", li, 512 + (g * 2 + pr) * 128)) for pr in range(2)])

    def build(self):
        self.phase0()
        for l in range(self.depth):
            if l % 2 == 0:
                self.even_layer(l)
            else:
                self.odd_layer(l)
        self.final()
        self.P.finalize()
        return self.nc


def _rope_tables():
    inv = (np.float32(10000.0) ** (-np.arange(0, 64, 2, dtype=np.float32) / np.float32(64))).astype(np.float32)
    ang = (np.arange(S, dtype=np.float32)[:, None] * inv[None, :]).astype(np.float32)
    cos = np.cos(ang).astype(np.float32).T
    sin = np.sin(ang).astype(np.float32).T
    CC = np.concatenate([cos] * 4, axis=0)
    SS = np.concatenate([-sin, sin, -sin, sin], axis=0)
    return np.ascontiguousarray(np.concatenate([CC, SS], axis=1).astype(np.float32))


def _static_misc():
    m = np.zeros((128, MISC_W), np.float32)
    m[:, IDF0:IDF0 + 128] = np.eye(128, dtype=np.float32)
    pb = np.zeros((16, 8), np.float32)
    npb = np.zeros((16, 8), np.float32)
    for i in range(16):
        for j in range(8):
            if j < i // 2:
                npb[i, j] = NEG
            else:
                pb[i, j] = -1e30
    m[:, PBT0:PBT0 + 128] = pb.reshape(1, 128)
    m[:, NPT0:NPT0 + 128] = npb.reshape(1, 128)
    t = np.arange(S)
    tb = t // 64
    jj = np.arange(32)
    valid = jj[None, :] <= tb[:, None]
    forced = (jj[None, :] == 0) | (jj[None, :] == tb[:, None]) | (jj[None, :] == tb[:, None] - 1)
    Ft = np.where(forced, np.float32(1e4), np.float32(0.0)).astype(np.float32)
    Vt = np.where(valid, np.float32(1e30), np.float32(-1e30)).astype(np.float32)
    m[:, FT0:FT0 + 512] = Ft.reshape(16, 128, 32).transpose(1, 0, 2).reshape(128, 512)
    m[:, VT0:VT0 + 512] = Vt.reshape(16, 128, 32).transpose(1, 0, 2).reshape(128, 512)
    ncmp = 127
    cst = np.arange(ncmp) * 16
    ov = ((cst[:, None] < (jj[None, :] + 1) * 64) & (cst[:, None] + 32 > jj[None, :] * 64)).astype(np.float32)
    m2 = np.zeros((128, 1024), np.float32)
    m2[0:127, OV0] = 1.0
    m2[0:127, OV0 + 1:OV0 + 33] = ov
    return m, m2


def _e32():
    e = np.zeros((128, S), np.float32)
    k = np.arange(S)
    for j in range(32):
        e[64 + j, :] = (k // 64 == j)
    return e


def _build_wblk(descs, inputs):
    wb = np.zeros((NB, 128, 1024), np.float32)
    for n, d in enumerate(descs):
        kind = d[0]
        if kind == "ada":
            _, l, nb = d
            W = inputs["w_ada"][l][:, nb * 128:(nb + 1) * 128]
            wb[n] = W.reshape(8, 128, 128).transpose(1, 0, 2).reshape(128, 1024)
        elif kind == "cols":
            _, name, li, cols = d
            W = inputs[name][li][:, cols]
            k = len(cols)
            blk = wb[n].reshape(128, 8, 128)
            blk[:, :, 0:k] = W.reshape(8, 128, k).transpose(1, 0, 2)
        elif kind == "rows":
            _, name, li, r0 = d
            wb[n] = inputs[name][li][r0:r0 + 128, :]
        elif kind == "w1":
            _, name, li, l0 = d
            W = inputs[name][li][l0 * 64:(l0 + 4) * 64, :]
            wb[n, 0:64, :] = W.reshape(4, 64, 256).transpose(1, 0, 2).reshape(64, 1024)
    return wb


_CACHE = {}


def kernel(**inputs):
    inputs = {k: np.asarray(v) for k, v in inputs.items()}
    if "b" not in _CACHE:
        b = Builder(4)
        nc = b.build()
        assert len(b.descs) == NB, (len(b.descs), NB)
        _CACHE["b"] = (b, nc)
    b, nc = _CACHE["b"]
    wblk = _build_wblk(b.descs, inputs)
    cs = _rope_tables()
    e32 = _e32()
    base, m2 = _static_misc()
    base[:, NG0:NG0 + 32] = inputs["norm_g"].reshape(4, 8, 128).transpose(2, 0, 1).reshape(128, 32)
    base[:, FG0:FG0 + 8] = inputs["final_g"].reshape(8, 128).T
    base[:, BADA0:BADA0 + 96] = inputs["b_ada"].reshape(4, 24, 128).transpose(2, 0, 1).reshape(128, 96)
    base[:, SINK0:SINK0 + 16] = inputs["a_sinks"].reshape(1, 16)
    for i in range(2):
        m2[:, W20 + i * 256:W20 + i * 256 + 128] = inputs["cmp_w2_k"][i].reshape(2, 128, 64).transpose(1, 0, 2).reshape(128, 128)
        m2[:, W20 + i * 256 + 128:W20 + (i + 1) * 256] = inputs["cmp_w2_v"][i].reshape(2, 128, 64).transpose(1, 0, 2).reshape(128, 128)
        m2[0:64, PE0 + i * 64:PE0 + i * 64 + 32] = inputs["cmp_pe_k"][i].T
        m2[0:64, PE0 + i * 64 + 32:PE0 + (i + 1) * 64] = inputs["cmp_pe_v"][i].T
    in_maps = []
    for core in range(8):
        m = base.copy()
        m[:, CV0:CV0 + 8] = inputs["c"][core].reshape(8, 128).T
        in_maps.append({"x": np.ascontiguousarray(inputs["x"][core]), "misc": m, "misc2": m2, "cs": cs, "e32": e32, "wblk": wblk})
    res = run_bass_kernel_spmd(nc, in_maps, core_ids=list(range(8)))
    return np.stack([np.asarray(r["out"], dtype=np.float32) for r in res.results], axis=0)
```
